# Optimizing a Trainium2 kernel written in Bass

```python
import jax, jax.numpy as jnp
from jax import lax
import numpy as np

D_MODEL = 2048
BATCH = 4
SEQ = 8192
DEPTH = 4

N_MIXERS = 3
N_POOL = (DEPTH + 2) // 3
N_DSA = (DEPTH + 1) // 3
N_CONV = DEPTH // 3

POOL_WINDOWS = (2, 4, 8, 16)
POOL_GROUPS = len(POOL_WINDOWS)
POOL_CH = D_MODEL // POOL_GROUPS

HEAD_DIM = 128
N_HEADS = D_MODEL // HEAD_DIM
N_KV_HEADS = 4
IDX_HEADS = 16
IDX_DIM = 64
TOP_K_MAX = 256
QBLK = 128
ROPE_THETA = 10000.0

Q_COLS = N_HEADS * HEAD_DIM
KV_COLS = N_KV_HEADS * HEAD_DIM
IQ_COLS = IDX_HEADS * IDX_DIM
DSA_IN_COLS = Q_COLS + 2 * KV_COLS + IQ_COLS + IDX_DIM + IDX_HEADS
DSA_SPLITS = (Q_COLS, Q_COLS + KV_COLS, Q_COLS + 2 * KV_COLS,
              Q_COLS + 2 * KV_COLS + IQ_COLS, Q_COLS + 2 * KV_COLS + IQ_COLS + IDX_DIM)

CONV_WIDTH = 31
D_FF = 4 * D_MODEL
EPS = 1e-6

kernel_name = "hybrid_pool_dsa_conformer_trunk"


def _rmsnorm(x, g):
    xf = x.astype(jnp.float32)
    y = xf * lax.rsqrt(jnp.mean(xf * xf, axis=-1, keepdims=True) + EPS)
    return (y * g.astype(jnp.float32)).astype(x.dtype)


def _layernorm(x, g, b):
    xf = x.astype(jnp.float32)
    mu = jnp.mean(xf, axis=-1, keepdims=True)
    var = jnp.mean(jnp.square(xf - mu), axis=-1, keepdims=True)
    y = (xf - mu) * lax.rsqrt(var + EPS)
    return (y * g.astype(jnp.float32) + b.astype(jnp.float32)).astype(x.dtype)


def _rope(x, pos):
    d = x.shape[-1]
    inv = ROPE_THETA ** (-jnp.arange(0, d, 2, dtype=jnp.float32) / d)
    ang = pos.astype(jnp.float32)[:, None] * inv[None, :]
    cos = jnp.cos(ang)[None, :, None, :]
    sin = jnp.sin(ang)[None, :, None, :]
    xf = x.astype(jnp.float32)
    x1, x2 = xf[..., : d // 2], xf[..., d // 2:]
    return jnp.concatenate([x1 * cos - x2 * sin, x2 * cos + x1 * sin], axis=-1).astype(x.dtype)


def _pool_mixer(h, w_grp, scale):
    B, L, D = h.shape
    hf = h.astype(jnp.float32)
    pos = jnp.arange(L, dtype=jnp.float32)
    outs = []
    for g, w in enumerate(POOL_WINDOWS):
        xg = hf[..., g * POOL_CH:(g + 1) * POOL_CH]
        cs = jnp.cumsum(xg, axis=1)
        lower = jnp.pad(cs[:, : L - w], ((0, 0), (w, 0), (0, 0)))
        count = jnp.minimum(pos + 1.0, float(w))[None, :, None]
        outs.append((cs - lower) / count - xg)
    y = jnp.stack(outs, axis=2).astype(h.dtype)
    y = jnp.einsum('blgc,gce->blge', y, w_grp).reshape(B, L, D)
    return y * scale


def _dsa_mixer(h, w_in, w_out):
    B, L, _ = h.shape
    pos = jnp.arange(L)
    proj = h @ w_in
    q, k, v, iq, ik, iw = jnp.split(proj, DSA_SPLITS, axis=-1)
    q = _rope(q.reshape(B, L, N_HEADS, HEAD_DIM), pos)
    k = _rope(k.reshape(B, L, N_KV_HEADS, HEAD_DIM), pos)
    v = v.reshape(B, L, N_KV_HEADS, HEAD_DIM)
    iq = _rope(iq.reshape(B, L, IDX_HEADS, IDX_DIM), pos)
    ik = _rope(ik.reshape(B, L, 1, IDX_DIM), pos)[:, :, 0]
    iw = iw * (IDX_HEADS ** -0.5) * (IDX_DIM ** -0.5)
    top_k = min(TOP_K_MAX, L // 4)
    n_blk = L // QBLK
    rep = N_HEADS // N_KV_HEADS
    key_pos = jnp.arange(L)

    def block(bi):
        start = bi * QBLK
        qb = lax.dynamic_slice_in_dim(q, start, QBLK, axis=1)
        iqb = lax.dynamic_slice_in_dim(iq, start, QBLK, axis=1)
        iwb = lax.dynamic_slice_in_dim(iw, start, QBLK, axis=1)
        t = start + jnp.arange(QBLK)
        rel = jax.nn.relu(jnp.einsum('bthd,bsd->bths', iqb, ik))
        score = jnp.einsum('bth,bths->bts', iwb, rel).astype(jnp.float32)
        causal = key_pos[None, :] <= t[:, None]
        score = jnp.where(causal[None], score, -jnp.inf)
        _, idx = lax.top_k(score, top_k)
        valid = idx <= t[None, :, None]
        kg = jax.vmap(lambda kb, ib: kb[ib])(k, idx)
        vg = jax.vmap(lambda vb, ib: vb[ib])(v, idx)
        qg = qb.reshape(B, QBLK, N_KV_HEADS, rep, HEAD_DIM)
        s = jnp.einsum('btgrd,btkgd->btgrk', qg, kg).astype(jnp.float32) * (HEAD_DIM ** -0.5)
        s = jnp.where(valid[:, :, None, None, :], s, -jnp.inf)
        p = jax.nn.softmax(s, axis=-1).astype(vg.dtype)
        o = jnp.einsum('btgrk,btkgd->btgrd', p, vg)
        return o.reshape(B, QBLK, N_HEADS * HEAD_DIM)

    out = lax.map(block, jnp.arange(n_blk))
    out = jnp.transpose(out, (1, 0, 2, 3)).reshape(B, L, N_HEADS * HEAD_DIM)
    return out @ w_out


def _conv_mixer(h, w_pw1, b_pw1, w_dw, b_dw, ln_g, ln_b, w_pw2, b_pw2):
    D = h.shape[-1]
    u = h @ w_pw1 + b_pw1
    a, gate = jnp.split(u, 2, axis=-1)
    u = a * jax.nn.sigmoid(gate)
    u = lax.conv_general_dilated(u, w_dw.reshape(CONV_WIDTH, 1, D), window_strides=(1,),
                                 padding=[(CONV_WIDTH - 1, 0)],
                                 dimension_numbers=('NWC', 'WIO', 'NWC'),
                                 feature_group_count=D) + b_dw
    u = jax.nn.silu(_layernorm(u, ln_g, ln_b))
    return u @ w_pw2 + b_pw2


def setup_inputs(seed: int = 0) -> dict:
    key = jax.random.key(seed)
    ks = jax.random.split(key, 24)
    D = D_MODEL
    f32 = jnp.float32
    nrm = lambda k, shape, s: jax.random.normal(k, shape, f32) * s
    return {
        "x": nrm(ks[0], (BATCH, SEQ, D), 1.0),
        "norm_mix": 1.0 + nrm(ks[1], (DEPTH, D), 0.02),
        "norm_mlp": 1.0 + nrm(ks[2], (DEPTH, D), 0.02),
        "mlp_up": nrm(ks[3], (DEPTH, D, D_FF), D ** -0.5),
        "mlp_down": nrm(ks[4], (DEPTH, D_FF, D), 0.5 * D_FF ** -0.5),
        "pool_w": nrm(ks[5], (N_POOL, POOL_GROUPS, POOL_CH, POOL_CH), POOL_CH ** -0.5),
        "pool_scale": 1.0 + nrm(ks[6], (N_POOL, D), 0.02),
        "dsa_w_in": nrm(ks[7], (N_DSA, D, DSA_IN_COLS), D ** -0.5),
        "dsa_w_out": nrm(ks[8], (N_DSA, Q_COLS, D), Q_COLS ** -0.5),
        "conv_w_pw1": nrm(ks[9], (N_CONV, D, 2 * D), D ** -0.5),
        "conv_b_pw1": nrm(ks[10], (N_CONV, 2 * D), 0.01),
        "conv_w_dw": nrm(ks[11], (N_CONV, CONV_WIDTH, D), CONV_WIDTH ** -0.5),
        "conv_b_dw": nrm(ks[12], (N_CONV, D), 0.01),
        "conv_ln_g": 1.0 + nrm(ks[13], (N_CONV, D), 0.02),
        "conv_ln_b": nrm(ks[14], (N_CONV, D), 0.01),
        "conv_w_pw2": nrm(ks[15], (N_CONV, D, D), D ** -0.5),
        "conv_b_pw2": nrm(ks[16], (N_CONV, D), 0.01),
        "norm_final": 1.0 + nrm(ks[17], (D,), 0.02),
    }


def reference(x, norm_mix, norm_mlp, mlp_up, mlp_down, pool_w, pool_scale, dsa_w_in, dsa_w_out,
              conv_w_pw1, conv_b_pw1, conv_w_dw, conv_b_dw, conv_ln_g, conv_ln_b, conv_w_pw2,
              conv_b_pw2, norm_final):
    for i in range(DEPTH):
        kind, j = i % N_MIXERS, i // N_MIXERS
        h = _rmsnorm(x, norm_mix[i])
        if kind == 0:
            y = _pool_mixer(h, pool_w[j], pool_scale[j])
        elif kind == 1:
            y = _dsa_mixer(h, dsa_w_in[j], dsa_w_out[j])
        else:
            y = _conv_mixer(h, conv_w_pw1[j], conv_b_pw1[j], conv_w_dw[j], conv_b_dw[j],
                            conv_ln_g[j], conv_ln_b[j], conv_w_pw2[j], conv_b_pw2[j])
        x = x + y.astype(x.dtype)
        h = _rmsnorm(x, norm_mlp[i])
        x = x + jnp.square(jax.nn.relu(h @ mlp_up[i])) @ mlp_down[i]
    return _rmsnorm(x, norm_final)
```

```python
import numpy as np
import ml_dtypes
import concourse.bass as bass
import concourse.mybir as mybir
from concourse.bass_utils import run_bass_kernel_spmd

F32 = mybir.dt.float32
BF16 = mybir.dt.bfloat16
AF = mybir.ActivationFunctionType
ALU = mybir.AluOpType

D = 2048
DC = 16
DFF = 8192
FC = 64
SEQ = 8192
OWN = 4096
PRE = 128
NT = OWN + PRE
HALO = 32
EPS = 1e-6
NEG = -1.0e30
POOL_W = (2, 4, 8, 16)
CONVW = 31
NCORES = 8


class Tl:
    __slots__ = ("name", "last_w", "readers")

    def __init__(self, name=""):
        self.name = name
        self.last_w = None
        self.readers = {}


class Op:
    __slots__ = ("eng", "fn", "deps", "dma", "idx")


ENG_BLOCK = {"pe": "tensor", "act": "scalar", "dve": "vector", "pool": "gpsimd", "sp": "sync"}


class Prog:
    NDS = 8

    def __init__(self, nc):
        self.nc = nc
        self.ops = []
        self.dma_since_barrier = []

    def op(self, eng, fn, reads=(), writes=(), dma=False, extra_deps=()):
        idx = len(self.ops)
        deps = set(extra_deps)
        for t in reads:
            if t.last_w is not None:
                deps.add(t.last_w)
        for t in writes:
            if t.last_w is not None:
                deps.add(t.last_w)
            deps.update(t.readers.values())
        key = ("d", idx) if dma else eng
        for t in reads:
            t.readers[key] = idx
        for t in writes:
            t.last_w = idx
            t.readers = {}
        deps.discard(idx)
        o = Op()
        o.eng = eng
        o.fn = fn
        o.dma = dma
        o.idx = idx
        ops = self.ops
        if eng == "pe" and not dma:
            o.deps = [d for d in deps if not (ops[d].eng == "pe" and not ops[d].dma)]
        else:
            o.deps = list(deps)
        ops.append(o)
        if dma:
            self.dma_since_barrier.append(idx)
        return idx

    def emit(self):
        nc = self.nc
        ops = self.ops
        need = [False] * len(ops)
        for o in ops:
            for d in o.deps:
                need[d] = True
        engs = []
        for o in ops:
            if o.eng not in engs:
                engs.append(o.eng)
        csem = {}
        ccnt = {}
        dsem = {}
        dcnt = {}
        drr = {}
        for e in engs:
            csem[e] = nc.alloc_semaphore(name=f"c_{e}")
            ccnt[e] = 0
            dsem[e] = [nc.alloc_semaphore(name=f"d_{e}_{i}") for i in range(self.NDS)]
            drr[e] = 0
            for i in range(self.NDS):
                dcnt[(e, i)] = 0
        sig = {}
        pre_wait = {}
        for o in ops:
            if o.dma:
                i = drr[o.eng]
                drr[o.eng] = (i + 1) % self.NDS
                prev = dcnt[(o.eng, i)]
                pre_wait[o.idx] = (dsem[o.eng][i], prev)
                dcnt[(o.eng, i)] = prev + 16
                sig[o.idx] = (dsem[o.eng][i], prev + 16)
            elif need[o.idx]:
                ccnt[o.eng] += 1
                sig[o.idx] = (csem[o.eng], ccnt[o.eng])
        self.max_counts = dict(ccnt)
        final_dma = [(dsem[e][i], dcnt[(e, i)]) for e in engs for i in range(self.NDS) if dcnt[(e, i)] > 0]
        with nc.Block() as block:
            for e in engs:
                my = [o for o in ops if o.eng == e]

                def body(eng, my=my, e=e):
                    waited = {}

                    def wait(sem, val):
                        if val <= 0:
                            return
                        k = sem.num
                        if waited.get(k, 0) >= val:
                            return
                        eng.wait_ge(sem, val)
                        waited[k] = val

                    for o in my:
                        for d in sorted(o.deps):
                            wait(*sig[d])
                        if o.dma:
                            wait(*pre_wait[o.idx])
                        ins = o.fn(eng)
                        if o.idx in sig:
                            ins.then_inc(sig[o.idx][0], 16 if o.dma else 1)
                    if e == "sp":
                        for s, v in final_dma:
                            wait(s, v)

                getattr(block, ENG_BLOCK[e])(body)


def make_tiles(nt):
    tiles = [(0, PRE)]
    t = PRE
    while t < nt:
        tiles.append((t, 512))
        t += 512
    return tiles


class Builder:
    def __init__(self, nt=NT, layers=(0, 1, 2, 3), mode="full", ncores=NCORES):
        self.ncores = ncores
        self.nt = nt
        self.tiles = make_tiles(nt)
        self.own = nt - PRE
        self.tiles0 = [(512 * i, 512) for i in range(2 * self.own // 512)]
        self.layers = layers
        self.mode = mode
        self.nc = bass.Bass("TRN2", target_bir_lowering=False)
        self.P = Prog(self.nc)
        self.dram = {}
        self.dtl = {}

    def din(self, name, shape, dt=F32):
        t = self.nc.dram_tensor(name, list(shape), dt, kind="ExternalInput")
        self.dram[name] = t
        if not hasattr(self, "in_names"):
            self.in_names = []
        self.in_names.append(name)
        return t

    def dout(self, name, shape, dt=F32):
        t = self.nc.dram_tensor(name, list(shape), dt, kind="ExternalOutput")
        self.dram[name] = t
        return t

    def dscr(self, name, shape, dt):
        t = self.nc.dram_tensor(name, list(shape), dt)
        self.dram[name] = t
        return t

    def dT(self, key):
        if key not in self.dtl:
            self.dtl[key] = Tl(str(key))
        return self.dtl[key]

    def arena_init(self):
        nc = self.nc
        self.ARENA_W = 52800
        self.arena = nc.alloc_sbuf_tensor("arena", [128, self.ARENA_W], F32)
        self.arena_off = 0
        self.psum = nc.alloc_psum_tensor("psum", [128, 4096], F32)
        self.pbank = [Tl(f"bank{i}") for i in range(8)]
        self.pb_rr = 0

    def arena_reset(self, keep=0):
        self.arena_off = keep

    def alloc(self, nbytes, dt=F32):
        words = (nbytes + 3) // 4
        words = (words + 7) // 8 * 8
        a = self.arena[:, self.arena_off:self.arena_off + words]
        self.arena_off += words
        assert self.arena_off <= self.ARENA_W, ("SBUF arena overflow", self.arena_off * 4)
        if dt == BF16:
            a = a.bitcast(BF16)
        return a

    def bank(self, i=None):
        if i is None:
            i = self.pb_rr
            self.pb_rr = (self.pb_rr + 1) % 8
        return self.psum[:, i * 512:(i + 1) * 512], self.pbank[i]

    def barrier(self):
        P = self.P
        nc = self.nc
        if not hasattr(self, "bar_sb"):
            self.bar_sb = nc.alloc_sbuf_tensor("bar_sb", [128, 64], F32)
            self.bar_tl = {e: Tl("bar_" + e) for e in ("act", "dve", "pool", "pe")}
        sb = self.bar_sb
        tl = self.bar_tl
        dmas = list(P.dma_since_barrier)
        P.dma_since_barrier = []
        i1 = P.op("act", lambda e: e.activation(out=sb[:, 0:8], in_=sb[:, 32:40], func=AF.Copy),
                  writes=[tl["act"]], extra_deps=dmas)
        i2 = P.op("dve", lambda e: e.memset(sb[:, 8:16], 0.0), writes=[tl["dve"]], extra_deps=dmas)
        i3 = P.op("pool", lambda e: e.memset(sb[:, 16:24], 0.0), writes=[tl["pool"]], extra_deps=dmas)
        pb, pbt = self.bank(7)
        i4 = P.op("pe", lambda e: e.matmul(pb[:, 0:8], lhsT=self.ident[:, 0:128], rhs=self.ident[:, 0:8],
                                           start=True, stop=True),
                  reads=[self.cbf_tl], writes=[pbt, tl["pe"]], extra_deps=dmas)
        allb = [i1, i2, i3, i4]
        P.op("act", lambda e: e.activation(out=sb[:, 0:8], in_=sb[:, 32:40], func=AF.Copy),
             writes=[tl["act"]], extra_deps=allb)
        P.op("dve", lambda e: e.memset(sb[:, 8:16], 0.0), writes=[tl["dve"]], extra_deps=allb)
        P.op("pool", lambda e: e.memset(sb[:, 16:24], 0.0), writes=[tl["pool"]], extra_deps=allb)
        P.op("pe", lambda e: e.matmul(pb[:, 0:8], lhsT=self.ident[:, 0:128], rhs=self.ident[:, 0:8],
                                      start=True, stop=True),
             writes=[pbt, tl["pe"]], extra_deps=allb)
        self.sp_fence = allb

    def dma(self, out, in_, reads=(), writes=(), eng="sp"):
        fence = getattr(self, "sp_fence", ())
        return self.P.op(eng, lambda e: e.dma_start(out=out, in_=in_), reads=reads, writes=writes,
                         dma=True, extra_deps=fence)

    def load_consts(self):
        P = self.P
        nc = self.nc
        vec_in = self.din("vecs", [128, self.NVEC])
        self.vecs = self.alloc(self.NVEC * 4)
        self.vecs_tl = Tl("vecs")
        self.dma(self.vecs, vec_in[:, :], writes=[self.vecs_tl])
        cb_in = self.din("cbf", [128, 384], BF16)
        self.cbf = self.alloc(384 * 2, BF16)
        self.cbf_tl = Tl("cbf")
        self.dma(self.cbf, cb_in[:, :], writes=[self.cbf_tl])
        self.ident = self.cbf[:, 0:128]
        self.onesm = self.cbf[:, 128:256]
        self.ones = self.cbf[:, 256:384]
        cw_in = self.din("convw", [128, 16 * CONVW])
        self.convw = self.alloc(16 * CONVW * 4)
        self.convw_tl = Tl("convw")
        self.dma(self.convw, cw_in[:, :], writes=[self.convw_tl])
        sm_in = self.din("smalls", [128, 160])
        self.smalls = self.alloc(160 * 4)
        self.smalls_tl = Tl("smalls")
        self.dma(self.smalls, sm_in[:, :], writes=[self.smalls_tl])
        self.pm = self.smalls[:, 0:1]
        self.epsc = self.smalls[:, 65:66]
        self.const_end = self.arena_off

    def vec(self, j, c):
        return self.vecs[:, j * 16 + c: j * 16 + c + 1]

    def cast_weights(self, specs):
        P = self.P
        CH = 4096
        NB = 3
        stage = [self.alloc(CH * 4) for _ in range(NB)]
        stage_tl = [Tl(f"stg{i}") for i in range(NB)]
        outb = [self.alloc(CH * 2, BF16) for _ in range(NB)]
        outb_tl = [Tl(f"cst{i}") for i in range(NB)]
        k = 0
        for name, nblk, E in specs:
            src = self.din(name + "_f", [128, nblk * E])
            dst = self.dscr(name, [128, nblk * E], BF16)
            self.wscr[name] = (dst, nblk, E)
            tot = nblk * E
            for o in range(0, tot, CH):
                w = min(CH, tot - o)
                i = k % NB
                self.dma(stage[i][:, :w], src[:, o:o + w], writes=[stage_tl[i]])
                ce = ("dve", "act", "pool")[k % 3]
                so, oo = stage[i][:, :w], outb[i][:, :w]
                if ce == "act":
                    P.op("act", lambda e, so=so, oo=oo: e.activation(out=oo, in_=so, func=AF.Copy),
                         reads=[stage_tl[i]], writes=[outb_tl[i]])
                elif ce == "dve":
                    P.op("dve", lambda e, so=so, oo=oo: e.tensor_copy(out=oo, in_=so),
                         reads=[stage_tl[i]], writes=[outb_tl[i]])
                else:
                    P.op("pool", lambda e, so=so, oo=oo: e.tensor_copy(out=oo, in_=so),
                         reads=[stage_tl[i]], writes=[outb_tl[i]])
                self.dma(dst[:, o:o + w], outb[i][:, :w], reads=[outb_tl[i]],
                         writes=[self.dT((name, "c", o // CH))])
                k += 1

    def wring_init(self, nbuf=3, E=8192):
        self.wr = [self.alloc(E * 2, BF16) for _ in range(nbuf)]
        self.wr_tl = [Tl(f"wr{i}") for i in range(nbuf)]
        self.wr_i = 0

    def wload(self, name, blk):
        dst, nblk, E = self.wscr[name]
        i = self.wr_i
        self.wr_i = (self.wr_i + 1) % len(self.wr)
        buf = self.wr[i][:, :E]
        rd = [self.dT((name, "c", q // 4096)) for q in range(blk * E, (blk + 1) * E, 4096)]
        self.dma(buf, dst[:, blk * E:(blk + 1) * E], reads=rd, writes=[self.wr_tl[i]])
        return buf, self.wr_tl[i]

    def linear(self, wname, KC, nchunks, NBC, ins, ins_tl, TT, epilogue, blk0=0):
        P = self.P
        nb = NBC * 128
        for b in range(nchunks // NBC):
            wt, wtl = self.wload(wname, blk0 + b)
            wv = wt.rearrange("p (k n) -> p k n", k=KC)
            for jj in range(NBC):
                n = b * NBC + jj
                ps, pst = self.bank()
                for kc in range(KC):
                    P.op("pe", lambda e, ps=ps, wv=wv, kc=kc, jj=jj, r=ins[kc]: e.matmul(
                        ps[:, :TT], lhsT=wv[:, kc, jj * 128:(jj + 1) * 128], rhs=r,
                        start=(kc == 0), stop=(kc == KC - 1)),
                        reads=[wtl, ins_tl[kc]], writes=[pst])
                epilogue(n, ps[:, :TT], pst)

    def rmsnorm(self, TT, gj, out_ap_fn, out_tl_fn, sq_view, sq_tl):
        P = self.P
        X, Xtl = self.X, self.X_tl
        sqs = [sq_view(c)[:, :TT] for c in range(DC)]
        sqt = [sq_tl(c) for c in range(DC)]
        oaps = [out_ap_fn(c) for c in range(DC)]
        otls = [out_tl_fn(c) for c in range(DC)]
        for c in range(DC):
            P.op("act", lambda e, c=c: e.activation(out=sqs[c], in_=X[:, c, :TT], func=AF.Square),
                 reads=[Xtl[c]], writes=[sqt[c]])
        ps, pst = self.bank()
        for c in range(DC):
            P.op("pe", lambda e, c=c, ps=ps: e.matmul(ps[:, :TT], lhsT=self.onesm, rhs=sqs[c],
                                                      start=(c == 0), stop=(c == DC - 1)),
                 reads=[sqt[c], self.cbf_tl], writes=[pst])
        rs = self.rstd
        P.op("act", lambda e, ps=ps: e.activation(out=rs[:, :TT], in_=ps[:, :TT], func=AF.Sqrt,
                                                  bias=self.epsc, scale=1.0),
             reads=[pst, self.smalls_tl], writes=[self.rstd_tl])
        P.op("dve", lambda e: e.reciprocal(out=rs[:, :TT], in_=rs[:, :TT]),
             reads=[self.rstd_tl], writes=[self.rstd_tl])
        for c in range(DC):
            P.op("dve", lambda e, c=c: e.scalar_tensor_tensor(
                out=oaps[c], in0=X[:, c, :TT], scalar=self.vec(gj, c), in1=rs[:, :TT],
                op0=ALU.mult, op1=ALU.mult),
                reads=[Xtl[c], self.rstd_tl, self.vecs_tl], writes=[otls[c]])

    def mlp(self, l, TT):
        P = self.P
        X, Xtl = self.X, self.X_tl
        H, Htl = self.H, self.H_tl
        G = self.G
        Gtl = self.G_tl

        def gv(j):
            return G[j // 4][:, (j % 4) * 512:(j % 4) * 512 + 512]

        self.rmsnorm(TT, self.VJ["norm_mlp"] + l,
                     lambda c: H[:, c, HALO:HALO + TT], lambda c: Htl[c],
                     lambda c: gv(c), lambda c: Gtl[c // 4])
        ins = [H[:, c, HALO:HALO + TT] for c in range(DC)]

        def ep_up(n, ps, pst):
            r, rtl = self.R[n % 2], self.R_tl[n % 2]
            P.op("act", lambda e: e.activation(out=r[:, :TT], in_=ps, func=AF.Relu),
                 reads=[pst], writes=[rtl])
            P.op("pool" if (n % 2) else "dve",
                 lambda e: e.tensor_tensor(out=gv(n)[:, :TT], in0=r[:, :TT], in1=r[:, :TT], op=ALU.mult),
                 reads=[rtl], writes=[Gtl[n // 4]])

        self.linear(f"up{l}", DC, FC, 4, ins, Htl, TT, ep_up)
        gins = [gv(j)[:, :TT] for j in range(FC)]
        gtl = [Gtl[j // 4] for j in range(FC)]

        def ep_dn(n, ps, pst):
            P.op("dve", lambda e: e.tensor_tensor(out=X[:, n, :TT], in0=X[:, n, :TT], in1=ps, op=ALU.add),
                 reads=[pst, Xtl[n]], writes=[Xtl[n]])

        self.linear(f"dn{l}", FC, DC, 1, gins, gtl, TT, ep_dn)

    def tile_bufs(self):
        self.arena_reset(self.const_end)
        Xf = self.alloc(DC * 512 * 4)
        self.X = Xf.rearrange("p (c t) -> p c t", c=DC)
        self.X_tl = [Tl(f"X{c}") for c in range(DC)]
        Hf = self.alloc(DC * (HALO + 512) * 2, BF16)
        self.H = Hf.rearrange("p (c t) -> p c t", c=DC)
        self.H_tl = [Tl(f"H{c}") for c in range(DC)]
        RB = 4352
        self.G_off = self.arena_off
        self.G = [self.alloc(RB, BF16) for _ in range(16)]
        self.G_tl = [Tl(f"G{i}") for i in range(16)]
        self.rstd = self.alloc(512 * 4)
        self.rstd_tl = Tl("rstd")
        self.tmpA = self.alloc(512 * 4)
        self.tmpA_tl = Tl("tmpA")
        self.tmpB = self.alloc(512 * 4)
        self.tmpB_tl = Tl("tmpB")
        self.UH = self.alloc(DC * HALO * 4).rearrange("p (c t) -> p c t", c=DC)
        self.UH_tl = Tl("UH")
        self.HH = self.alloc(DC * HALO * 2, BF16).rearrange("p (c t) -> p c t", c=DC)
        self.HH_tl = Tl("HH")
        self.R = [self.alloc(512 * 2, BF16) for _ in range(2)]
        self.R_tl = [Tl("R0"), Tl("R1")]
        self.wring_init(3, 8192)

    def load_x(self, src, t0, TT, off=0):
        self.dma(self.X[:, :, :TT], src[:, :, off + t0:off + t0 + TT], reads=[self.dT((src.name, off + t0))],
                 writes=self.X_tl, eng="pool")

    def store_x(self, dst, t0, TT, c0=0):
        self.dma(dst[:, :, t0 - c0:t0 - c0 + TT], self.X[:, :, :TT], reads=self.X_tl,
                 writes=[self.dT((dst.name, t0))], eng="pool")

    def mask_pre(self, ap3, tls):
        P = self.P
        P.op("dve", lambda e: e.tensor_scalar(out=ap3, in0=ap3, scalar1=self.pm, scalar2=None, op0=ALU.mult),
             reads=list(tls) + [self.smalls_tl], writes=list(tls))

    def pool_mixer(self, l, j, ti, TT, icol=None):
        P = self.P
        X, Xtl, H, Htl = self.X, self.X_tl, self.H, self.H_tl
        G, Gtl = self.G, self.G_tl
        if ti == 0:
            P.op("pool", lambda e: e.memset(H[:, :, 0:HALO], 0.0), writes=Htl)
        else:
            P.op("pool", lambda e: e.tensor_copy(out=H[:, :, 0:HALO], in_=self.HH[:, :, :]),
                 reads=[self.HH_tl], writes=Htl)
        self.rmsnorm(TT, self.VJ["norm_mix"] + l,
                     lambda c: H[:, c, HALO:HALO + TT], lambda c: Htl[c],
                     lambda c: G[c][:, 0:512], lambda c: Gtl[c])
        P.op("pool", lambda e: e.tensor_copy(out=self.HH[:, :, :], in_=H[:, :, TT:TT + HALO]),
             reads=Htl, writes=[self.HH_tl])
        W = HALO + TT
        for g in range(4):
            nsteps = g + 1
            w = POOL_W[g]
            for k in range(4):
                r = 4 * g + k
                reg32 = G[r].bitcast(F32)
                rtl = Gtl[r]
                eng = "pool" if (k % 2) else "dve"
                sh = 1
                for s in range(nsteps):
                    lo = 2 * sh - 1
                    dst = reg32[:, (s % 2) * 544:(s % 2) * 544 + 544]
                    if s == 0:
                        a = H[:, r, lo:W]
                        b = H[:, r, lo - sh:W - sh]
                        rd = [Htl[r]]
                    else:
                        srcb = reg32[:, ((s - 1) % 2) * 544:((s - 1) % 2) * 544 + 544]
                        a = srcb[:, lo:W]
                        b = srcb[:, lo - sh:W - sh]
                        rd = [rtl]
                    P.op(eng, lambda e, dst=dst, a=a, b=b, lo=lo: e.tensor_tensor(out=dst[:, lo:W], in0=a, in1=b, op=ALU.add),
                         reads=rd, writes=[rtl])
                    sh *= 2
                fin = reg32[:, ((nsteps - 1) % 2) * 544:((nsteps - 1) % 2) * 544 + 544]
                y = G[r][:, (nsteps % 2) * 1088:(nsteps % 2) * 1088 + 512]
                P.op("dve", lambda e, y=y, fin=fin, r=r, w=w: e.scalar_tensor_tensor(
                    out=y[:, :TT], in0=fin[:, HALO:HALO + TT], scalar=1.0 / w, in1=H[:, r, HALO:HALO + TT],
                    op0=ALU.mult, op1=ALU.subtract),
                    reads=[rtl, Htl[r]], writes=[rtl])
                if icol is not None:
                    ic = self.smalls[:, icol + 16 * g: icol + 16 * g + 16]
                    tmp = self.tmpA[:, 0:16]
                    P.op("dve", lambda e, fin=fin, ic=ic, tmp=tmp: e.tensor_tensor(
                        out=tmp, in0=fin[:, HALO:HALO + 16], in1=ic, op=ALU.mult),
                        reads=[rtl, self.smalls_tl], writes=[self.tmpA_tl])
                    P.op("dve", lambda e, y=y, tmp=tmp, r=r: e.tensor_tensor(
                        out=y[:, 0:16], in0=tmp, in1=H[:, r, HALO:HALO + 16], op=ALU.subtract),
                        reads=[self.tmpA_tl, Htl[r]], writes=[rtl])
        wt, wtl = self.wload(f"pool{j}", 0)
        wv = wt.rearrange("p (g k n) -> p g k n", g=4, k=4)
        for g in range(4):
            nsteps = g + 1
            for jj in range(4):
                n = 4 * g + jj
                ps, pst = self.bank()
                for kc in range(4):
                    y = G[4 * g + kc][:, (nsteps % 2) * 1088:(nsteps % 2) * 1088 + 512]
                    P.op("pe", lambda e, ps=ps, g=g, kc=kc, jj=jj, y=y: e.matmul(
                        ps[:, :TT], lhsT=wv[:, g, kc, jj * 128:(jj + 1) * 128], rhs=y[:, :TT],
                        start=(kc == 0), stop=(kc == 3)),
                        reads=[wtl, Gtl[4 * g + kc]], writes=[pst])
                P.op("dve", lambda e, ps=ps, n=n: e.scalar_tensor_tensor(
                    out=X[:, n, :TT], in0=ps[:, :TT], scalar=self.vec(self.VJ["pool_scale"] + j, n), in1=X[:, n, :TT],
                    op0=ALU.mult, op1=ALU.add),
                    reads=[pst, Xtl[n], self.vecs_tl], writes=[Xtl[n]])

    def conv_mixer(self, l, ti, TT):
        P = self.P
        X, Xtl, H, Htl = self.X, self.X_tl, self.H, self.H_tl
        G, Gtl = self.G, self.G_tl
        VJ = self.VJ
        UW = HALO + 512

        def U(c):
            return G[c].bitcast(F32)[:, 0:UW]

        def ACC(c):
            return G[c].bitcast(F32)[:, UW:UW + 512]

        self.rmsnorm(TT, VJ["norm_mix"] + l,
                     lambda c: H[:, c, HALO:HALO + TT], lambda c: Htl[c],
                     lambda c: G[c][:, 0:512], lambda c: Gtl[c])
        for c in range(DC):
            if ti == 0:
                P.op("pool", lambda e, c=c: e.memset(U(c)[:, 0:HALO], 0.0), writes=[Gtl[c]])
            else:
                P.op("pool", lambda e, c=c: e.tensor_copy(out=U(c)[:, 0:HALO], in_=self.UH[:, c, :]),
                     reads=[self.UH_tl], writes=[Gtl[c]])
        ins = [H[:, c, HALO:HALO + TT] for c in range(DC)]
        state = {}

        def ep_pw1(n, ps, pst):
            c = n // 2
            if n % 2 == 0:
                state["a"] = (ps, pst)
                return
            aps, apst = state["a"]
            sig = self.tmpA
            P.op("act", lambda e: e.activation(out=sig[:, :TT], in_=ps, func=AF.Sigmoid,
                                               bias=self.vec(VJ["b_pw1g"], c), scale=1.0),
                 reads=[pst, self.vecs_tl], writes=[self.tmpA_tl])
            P.op("dve", lambda e: e.scalar_tensor_tensor(
                out=U(c)[:, HALO:HALO + TT], in0=aps, scalar=self.vec(VJ["b_pw1a"], c), in1=sig[:, :TT],
                op0=ALU.add, op1=ALU.mult),
                reads=[apst, self.tmpA_tl, self.vecs_tl], writes=[Gtl[c]])

        self.linear("pw1", DC, 32, 2, ins, Htl, TT, ep_pw1)
        if ti == 0:
            for c in range(DC):
                self.mask_pre(U(c)[:, HALO:HALO + TT], [Gtl[c]])
        P.op("pool", lambda e: e.tensor_copy(
            out=self.UH[:, 0, :], in_=U(0)[:, TT:TT + HALO]), reads=[Gtl[0]], writes=[self.UH_tl])
        for c in range(1, DC):
            P.op("pool", lambda e, c=c: e.tensor_copy(out=self.UH[:, c, :], in_=U(c)[:, TT:TT + HALO]),
                 reads=[Gtl[c], self.UH_tl], writes=[self.UH_tl])
        for c in range(DC):
            u = U(c)
            acc = ACC(c)
            for k in range(CONVW):
                off = HALO - (CONVW - 1) + k
                wcol = self.convw[:, c * CONVW + k: c * CONVW + k + 1]
                if k == 0:
                    P.op("dve", lambda e, u=u, acc=acc, off=off, wcol=wcol, c=c: e.tensor_scalar(
                        out=acc[:, :TT], in0=u[:, off:off + TT], scalar1=wcol, scalar2=self.vec(VJ["b_dw"], c),
                        op0=ALU.mult, op1=ALU.add),
                        reads=[Gtl[c], self.convw_tl, self.vecs_tl], writes=[Gtl[c]])
                else:
                    P.op("dve", lambda e, u=u, acc=acc, off=off, wcol=wcol: e.scalar_tensor_tensor(
                        out=acc[:, :TT], in0=u[:, off:off + TT], scalar=wcol, in1=acc[:, :TT],
                        op0=ALU.mult, op1=ALU.add),
                        reads=[Gtl[c], self.convw_tl], writes=[Gtl[c]])
        for c in range(DC):
            P.op("act", lambda e, c=c: e.activation(out=H[:, c, HALO:HALO + TT], in_=ACC(c)[:, :TT], func=AF.Copy),
                 reads=[Gtl[c]], writes=[Htl[c]])
            P.op("act", lambda e, c=c: e.activation(out=G[c][:, 0:TT], in_=ACC(c)[:, :TT], func=AF.Square),
                 reads=[Gtl[c]], writes=[Gtl[c]])
        psm, psmt = self.bank()
        pss, psst = self.bank()
        for c in range(DC):
            P.op("pe", lambda e, c=c: e.matmul(psm[:, :TT], lhsT=self.onesm, rhs=H[:, c, HALO:HALO + TT],
                                               start=(c == 0), stop=(c == DC - 1)),
                 reads=[Htl[c], self.cbf_tl], writes=[psmt])
        for c in range(DC):
            P.op("pe", lambda e, c=c: e.matmul(pss[:, :TT], lhsT=self.onesm, rhs=G[c][:, 0:TT],
                                               start=(c == 0), stop=(c == DC - 1)),
                 reads=[Gtl[c], self.cbf_tl], writes=[psst])
        mean = self.tmpA
        var = self.tmpB
        rs = self.rstd
        P.op("act", lambda e: e.activation(out=mean[:, :TT], in_=psm[:, :TT], func=AF.Copy),
             reads=[psmt], writes=[self.tmpA_tl])
        P.op("dve", lambda e: e.tensor_tensor(out=var[:, :TT], in0=mean[:, :TT], in1=mean[:, :TT], op=ALU.mult),
             reads=[self.tmpA_tl], writes=[self.tmpB_tl])
        P.op("dve", lambda e: e.tensor_tensor(out=var[:, :TT], in0=pss[:, :TT], in1=var[:, :TT], op=ALU.subtract),
             reads=[psst, self.tmpB_tl], writes=[self.tmpB_tl])
        P.op("act", lambda e: e.activation(out=rs[:, :TT], in_=var[:, :TT], func=AF.Sqrt,
                                           bias=self.epsc, scale=1.0),
             reads=[self.tmpB_tl, self.smalls_tl], writes=[self.rstd_tl])
        P.op("dve", lambda e: e.reciprocal(out=rs[:, :TT], in_=rs[:, :TT]),
             reads=[self.rstd_tl], writes=[self.rstd_tl])
        for c in range(DC):
            acc = ACC(c)
            P.op("dve", lambda e, acc=acc: e.tensor_tensor(out=acc[:, :TT], in0=acc[:, :TT], in1=mean[:, :TT],
                                                           op=ALU.subtract),
                 reads=[Gtl[c], self.tmpA_tl], writes=[Gtl[c]])
            P.op("dve", lambda e, acc=acc, c=c: e.scalar_tensor_tensor(
                out=acc[:, :TT], in0=acc[:, :TT], scalar=self.vec(VJ["ln_g"], c), in1=rs[:, :TT],
                op0=ALU.mult, op1=ALU.mult),
                reads=[Gtl[c], self.rstd_tl, self.vecs_tl], writes=[Gtl[c]])
            P.op("act", lambda e, acc=acc, c=c: e.activation(
                out=H[:, c, HALO:HALO + TT], in_=acc[:, :TT], func=AF.Silu, bias=self.vec(VJ["ln_b"], c), scale=1.0),
                reads=[Gtl[c], self.vecs_tl], writes=[Htl[c]])

        def ep_pw2(n, ps, pst):
            P.op("dve", lambda e: e.scalar_tensor_tensor(
                out=X[:, n, :TT], in0=ps, scalar=self.vec(VJ["b_pw2"], n), in1=X[:, n, :TT],
                op0=ALU.add, op1=ALU.add),
                reads=[pst, Xtl[n], self.vecs_tl], writes=[Xtl[n]])

        self.linear("pw2", DC, DC, 4, ins, Htl, TT, ep_pw2)

    def final_norm(self, TT):
        X, Xtl = self.X, self.X_tl
        G, Gtl = self.G, self.G_tl
        self.rmsnorm(TT, self.VJ["norm_final"],
                     lambda c: X[:, c, :TT], lambda c: Xtl[c],
                     lambda c: G[c][:, 0:512], lambda c: Gtl[c])


    def dsa_proj(self, src, l=1):
        P = self.P
        X, Xtl, H, Htl = self.X, self.X_tl, self.H, self.H_tl
        G, Gtl = self.G, self.G_tl
        D_ = self.dsa
        tabs = [self.alloc(512 * 4) for _ in range(4)]
        tabs_tl = [Tl(f"tab{i}") for i in range(4)]
        wvv = self.arena[:, self.G_off:self.G_off + 16 * 1088].bitcast(BF16).rearrange(
            "p (k n) -> p k n", k=16)[:, :, 1024:1536]
        wiw = self.alloc(256 * 2, BF16)
        wiw_tl = Tl("wiw")
        dst, _, _ = self.wscr["wv"]
        self.dma(wvv, dst[:, :].rearrange("p (k n) -> p k n", k=16),
                 reads=[self.dT(("wv", "c", 0)), self.dT(("wv", "c", 1))], writes=Gtl)
        dst, _, _ = self.wscr["wiw"]
        self.dma(wiw, dst[:, :], reads=[self.dT(("wiw", "c", 0))], writes=[wiw_tl])
        wiwv = wiw.rearrange("p (k n) -> p k n", k=16)
        stg = [self.alloc(512 * 2, BF16) for _ in range(4)]
        stg_tl = [Tl(f"stg{i}") for i in range(4)]
        iwst = self.alloc(16 * 4)
        iwst_tl = Tl("iwst")
        sk = [0]
        own = self.own
        QOFF = own - 512
        for ti, (t0, TT) in enumerate(self.tiles0):
            self.load_x(src, t0, TT)
            full = (t0 >= QOFF)
            EXd = D_ if t0 >= own else D_["lowv"]
            kcol = t0 - own if t0 >= own else t0
            for i, nm in enumerate(("cos128", "sin128", "cos64", "sin64")):
                self.dma(tabs[i][:, :TT], self.dram[nm][:, t0:t0 + TT], writes=[tabs_tl[i]], eng="pool")
            self.rmsnorm(TT, self.VJ["norm_mix"] + l,
                         lambda c: H[:, c, HALO:HALO + TT], lambda c: Htl[c],
                         lambda c: G[c][:, 0:512], lambda c: Gtl[c])
            ins = [H[:, c, HALO:HALO + TT] for c in range(DC)]
            state = {}

            def ep(n, ps, pst, ti=ti, t0=t0, TT=TT, EXd=EXd, kcol=kcol):
                p = n // 2
                if n % 2 == 0:
                    state["a"] = (ps, pst)
                    return
                aps, apst = state["a"]
                big = p < 20
                ct, st_ = (tabs[0], tabs[1]) if big else (tabs[2], tabs[3])
                ctl, stl = (tabs_tl[0], tabs_tl[1]) if big else (tabs_tl[2], tabs_tl[3])
                P.op("dve", lambda e: e.tensor_tensor(out=self.tmpA[:, :TT], in0=aps, in1=ct[:, :TT], op=ALU.mult),
                     reads=[apst, ctl], writes=[self.tmpA_tl])
                P.op("dve", lambda e: e.tensor_tensor(out=self.tmpB[:, :TT], in0=ps, in1=st_[:, :TT], op=ALU.mult),
                     reads=[pst, stl], writes=[self.tmpB_tl])
                i = sk[0] % 4
                sk[0] += 1
                P.op("pool", lambda e: e.tensor_tensor(out=stg[i][:, :TT], in0=self.tmpA[:, :TT],
                                                       in1=self.tmpB[:, :TT], op=ALU.add),
                     reads=[self.tmpA_tl, self.tmpB_tl], writes=[stg_tl[i]])
                if p < 16:
                    self.dma(D_["qT"][:, p, t0 - QOFF:t0 - QOFF + TT], stg[i][:, :TT], reads=[stg_tl[i]],
                             writes=[self.dT(("qT", p, t0))], eng="pool")
                elif p < 20:
                    self.dma(EXd["kT"][:, p - 16, kcol:kcol + TT], stg[i][:, :TT], reads=[stg_tl[i]],
                             writes=[self.dT(("kT", p - 16, t0))], eng="pool")
                elif p < 28:
                    self.dma(D_["iqT"][:, p - 20, t0 - QOFF:t0 - QOFF + TT], stg[i][:, :TT], reads=[stg_tl[i]],
                             writes=[self.dT(("iqT", p - 20, t0))], eng="pool")
                else:
                    self.dma(EXd["ikT"][:, kcol:kcol + TT], stg[i][:, :TT], reads=[stg_tl[i]],
                             writes=[self.dT(("ikT", t0))], eng="pool")

            if full:
                self.linear("win", DC, 58, 2, ins, Htl, TT, ep)
            else:
                self.linear("win", DC, 8, 2, ins, Htl, TT, lambda n, ps, pst, ep=ep: ep(n + 32, ps, pst), blk0=16)
                self.linear("win", DC, 2, 2, ins, Htl, TT, lambda n, ps, pst, ep=ep: ep(n + 56, ps, pst), blk0=28)
            for tb in range(TT // 128):
                blk = (kcol + tb * 128) // 128
                hb = lambda kc, tb=tb: H[:, kc, HALO + tb * 128:HALO + tb * 128 + 128]
                if True:
                    ps, pst = self.bank()
                    for kc in range(DC):
                        P.op("pe", lambda e, ps=ps, kc=kc, hb=hb: e.matmul(ps[:, :512], lhsT=hb(kc), rhs=wvv[:, kc, :],
                                                                          start=(kc == 0), stop=(kc == DC - 1)),
                             reads=[Htl[kc], Gtl[kc]], writes=[pst])
                    i = sk[0] % 4
                    sk[0] += 1
                    P.op("act", lambda e, ps=ps, i=i: e.activation(out=stg[i][:, :512], in_=ps[:, :512], func=AF.Copy),
                         reads=[pst], writes=[stg_tl[i]])
                    self.dma(EXd["v"][:, blk, :], stg[i][:, :512], reads=[stg_tl[i]],
                             writes=[self.dT(("v", t0, tb))], eng="pool")
                if not full:
                    continue
                qblk = (t0 - QOFF) // 128 + tb
                ps, pst = self.bank()
                for kc in range(DC):
                    P.op("pe", lambda e, ps=ps, kc=kc, hb=hb: e.matmul(ps[:, :16], lhsT=hb(kc), rhs=wiwv[:, kc, :],
                                                                      start=(kc == 0), stop=(kc == DC - 1)),
                         reads=[Htl[kc], wiw_tl], writes=[pst])
                P.op("act", lambda e, ps=ps: e.activation(out=iwst[:, :16], in_=ps[:, :16], func=AF.Copy,
                                                          scale=1.0 / 32.0),
                     reads=[pst], writes=[iwst_tl])
                self.dma(D_["iw"][:, qblk, :], iwst[:, :16], reads=[iwst_tl], writes=[self.dT(("iw", qblk))], eng="pool")

    def exchange(self):
        own = self.nt - PRE
        D_ = self.dsa
        EX = D_["EX"]
        gath = self.dscr("gath", [256, 9 * own], BF16)
        D_["low"] = gath[0:128, :]
        rd = [self.dT(("kT", g, t0)) for g in range(4) for (t0, TT) in self.tiles[1:]]
        rd += [self.dT(("v", b)) for b in range(1, own // 128 + 1)]
        rd += [self.dT(("ikT", t0)) for (t0, TT) in self.tiles[1:]]
        groups = [[2 * i, 2 * i + 1] for i in range(self.ncores // 2)]
        self.P.op("pool", lambda e: e.collective_compute("AllGather", op=ALU.bypass, replica_groups=groups,
                                                         ins=[EX[:, :]], outs=[gath[:, :]]),
                  reads=rd, writes=[self.dT(("low",))], dma=True)

    def dsa_attn(self, nsteps=22):
        P = self.P
        D_ = self.dsa
        nqb = self.nt // 128
        own = self.nt - PRE
        nob = own // 128
        LOWK = D_["low"][:, 0:4 * own].rearrange("p (g t) -> p g t", g=4)
        LOWV = D_["low"][:, 4 * own:8 * own].rearrange("p (b c) -> p b c", b=nob)
        OWNK = D_["kT"]
        OWNV = D_["v"]
        self.arena_reset(self.const_end)
        ikT = self.alloc(8192 * 2, BF16)
        ikT_tl = Tl("ikT")
        self.dma(ikT[:, 0:own], D_["low"][:, 8 * own:9 * own], writes=[ikT_tl])
        self.dma(ikT[:, own:2 * own], D_["ikT"][:, :], writes=[ikT_tl])
        kval = self.alloc(8192 * 2, BF16)
        kval_tl = Tl("kval")
        self.dma(kval[:, :2 * own], self.dram["kvalid"][:, :], writes=[kval_tl])
        tri = self.alloc(128 * 2, BF16)
        tri_tl = Tl("tri")
        self.dma(tri, self.dram["tri"][:, :], writes=[tri_tl])
        score = self.alloc(8192 * 4)
        score_tl = Tl("score")
        M = self.alloc(8192 * 2, BF16)
        M_tl = Tl("M")
        MT = self.alloc(8192 * 2, BF16).rearrange("p (b q) -> p b q", b=64)
        MT_tl = Tl("MT")
        NR = 4
        Rr = [self.alloc(512 * 2, BF16) for _ in range(NR)]
        Rr_tl = [Tl(f"R{i}") for i in range(NR)]
        ETr = [self.alloc(512 * 2, BF16) for _ in range(3)]
        ETr_tl = [Tl(f"ET{i}") for i in range(3)]
        PTr = [self.alloc(512 * 2, BF16) for _ in range(3)]
        PTr_tl = [Tl(f"PT{i}") for i in range(3)]
        Kr = [self.alloc(512 * 2, BF16) for _ in range(4)]
        Kr_tl = [Tl(f"K{i}") for i in range(4)]
        Vr = [self.alloc(512 * 2, BF16).rearrange("p (b d) -> p b d", b=4) for _ in range(4)]
        Vr_tl = [Tl(f"V{i}") for i in range(4)]
        QT = self.alloc(16 * 128 * 2, BF16).rearrange("p (h q) -> p h q", h=16)
        QT_tl = Tl("QT")
        IQ = self.alloc(8 * 128 * 2, BF16).rearrange("p (h q) -> p h q", h=8)
        IQ_tl = Tl("IQ")
        IW = self.alloc(16 * 4)
        IW_tl = Tl("IW")
        diag = self.alloc(16 * 128 * 2, BF16).rearrange("p (h q) -> p h q", h=16)
        diag_tl = [Tl(f"diag{h}") for h in range(16)]
        rden = self.alloc(512 * 4)
        rden_tl = Tl("rden")
        OTs = [self.alloc(512 * 2, BF16) for _ in range(2)]
        OTs_tl = [Tl("OT0"), Tl("OT1")]
        sm = self.alloc(16 * 4)
        sm_tl = {k: Tl("sm_" + k) for k in ("lo", "hi", "mid", "cnt", "ge", "d1", "d2")}
        lo, hi, mid, cnt, ge, d1, d2 = (sm[:, i:i + 1] for i in range(7))
        B_IDX = (0, 1)
        B_SC = 2
        B_ST = (3, 4)
        B_O = 5
        B_DEN = 6
        B_TR = 7
        rk = [0, 0, 0, 0, 0]
        for qb in range(nqb):
            c0 = qb * 128
            nkb = nob - 1 + qb + 1
            nk = nkb * 128
            qc = 512 - PRE + c0
            self.dma(QT[:, :, :], D_["qT"][:, :, qc:qc + 128], writes=[QT_tl])
            self.dma(IQ[:, :, :], D_["iqT"][:, :, qc:qc + 128], writes=[IQ_tl])
            self.dma(IW[:, :], D_["iw"][:, qc // 128, :], writes=[IW_tl])
            for h in range(16):
                P.op("pool", lambda e, h=h: e.tensor_scalar(out=diag[:, h, :], in0=self.ident, scalar1=IW[:, h:h + 1],
                                                            scalar2=None, op0=ALU.mult),
                     reads=[IW_tl, self.cbf_tl], writes=[diag_tl[h]])
            for k0 in range(0, nk, 512):
                w = min(512, nk - k0)
                scp, scpt = self.bank(B_SC)
                for h in range(16):
                    ps, pst = self.bank(B_IDX[h % 2])
                    hp = 64 * (h % 2)
                    P.op("pe", lambda e, ps=ps, h=h, hp=hp, k0=k0, w=w: e.matmul(
                        ps[:, :w], lhsT=IQ[hp:hp + 64, h // 2, :], rhs=ikT[hp:hp + 64, k0:k0 + w], start=True, stop=True),
                        reads=[IQ_tl, ikT_tl], writes=[pst])
                    ri = rk[0] % NR
                    rk[0] += 1
                    if h % 2 == 0:
                        P.op("act", lambda e, ps=ps, ri=ri, w=w: e.activation(out=Rr[ri][:, :w], in_=ps[:, :w], func=AF.Relu),
                             reads=[pst], writes=[Rr_tl[ri]])
                    else:
                        P.op("dve", lambda e, ps=ps, ri=ri, w=w: e.tensor_scalar(out=Rr[ri][:, :w], in0=ps[:, :w], scalar1=0.0,
                                                                                 scalar2=None, op0=ALU.max),
                             reads=[pst], writes=[Rr_tl[ri]])
                    P.op("pe", lambda e, scp=scp, h=h, ri=ri, w=w: e.matmul(
                        scp[:, :w], lhsT=diag[:, h, :], rhs=Rr[ri][:, :w], start=(h == 0), stop=(h == 15)),
                        reads=[diag_tl[h], Rr_tl[ri]], writes=[scpt])
                P.op("act", lambda e, scp=scp, k0=k0, w=w: e.activation(out=score[:, k0:k0 + w], in_=scp[:, :w], func=AF.Copy),
                     reads=[scpt], writes=[score_tl])
            P.op("dve", lambda e, nk=nk: e.tensor_reduce(out=lo, in_=score[:, :nk], axis=mybir.AxisListType.X, op=ALU.min),
                 reads=[score_tl], writes=[sm_tl["lo"]])
            P.op("dve", lambda e, nk=nk: e.tensor_reduce(out=hi, in_=score[:, :nk], axis=mybir.AxisListType.X, op=ALU.max),
                 reads=[score_tl], writes=[sm_tl["hi"]])
            P.op("dve", lambda e, nk=nk: e.tensor_tensor(out=score[:, :nk], in0=score[:, :nk], in1=kval[:, :nk], op=ALU.add),
                 reads=[score_tl, kval_tl], writes=[score_tl])
            P.op("dve", lambda e, nk=nk: e.tensor_tensor(out=score[:, nk - 128:nk], in0=score[:, nk - 128:nk], in1=tri[:, :],
                                                         op=ALU.add),
                 reads=[score_tl, tri_tl], writes=[score_tl])
            for s in range(nsteps):
                P.op("dve", lambda e: e.tensor_tensor(out=mid, in0=lo, in1=hi, op=ALU.add),
                     reads=[sm_tl["lo"], sm_tl["hi"]], writes=[sm_tl["mid"]])
                P.op("dve", lambda e: e.tensor_scalar(out=mid, in0=mid, scalar1=0.5, scalar2=None, op0=ALU.mult),
                     reads=[sm_tl["mid"]], writes=[sm_tl["mid"]])
                P.op("dve", lambda e, nk=nk: e.tensor_scalar(
                    out=M[:, :nk], in0=score[:, :nk], scalar1=mid, scalar2=0.0, op0=ALU.is_ge, op1=ALU.add,
                    accum_out=cnt),
                    reads=[score_tl, sm_tl["mid"]], writes=[M_tl, sm_tl["cnt"]])
                P.op("dve", lambda e: e.tensor_scalar(out=ge, in0=cnt, scalar1=255.5, scalar2=None, op0=ALU.is_ge),
                     reads=[sm_tl["cnt"]], writes=[sm_tl["ge"]])
                P.op("dve", lambda e: e.tensor_tensor(out=d1, in0=mid, in1=lo, op=ALU.subtract),
                     reads=[sm_tl["mid"], sm_tl["lo"]], writes=[sm_tl["d1"]])
                P.op("dve", lambda e: e.tensor_tensor(out=d2, in0=hi, in1=mid, op=ALU.subtract),
                     reads=[sm_tl["mid"], sm_tl["hi"]], writes=[sm_tl["d2"]])
                P.op("dve", lambda e: e.scalar_tensor_tensor(out=lo, in0=d1, scalar=ge, in1=lo, op0=ALU.mult, op1=ALU.add),
                     reads=[sm_tl["d1"], sm_tl["ge"], sm_tl["lo"]], writes=[sm_tl["lo"]])
                P.op("dve", lambda e: e.scalar_tensor_tensor(out=hi, in0=d2, scalar=ge, in1=mid, op0=ALU.mult, op1=ALU.add),
                     reads=[sm_tl["d2"], sm_tl["ge"], sm_tl["mid"]], writes=[sm_tl["hi"]])
            P.op("dve", lambda e, nk=nk: e.tensor_scalar(
                out=M[:, :nk], in0=score[:, :nk], scalar1=lo, scalar2=None, op0=ALU.is_ge),
                reads=[score_tl, sm_tl["lo"]], writes=[M_tl])
            for kb0 in range(0, nkb, 4):
                nb = min(4, nkb - kb0)
                trp, trpt = self.bank(B_TR)
                trb = trp.bitcast(BF16)
                for j in range(nb):
                    kb = kb0 + j
                    P.op("pe", lambda e, trb=trb, j=j, kb=kb: e.transpose(
                        out=trb[:, j * 128:(j + 1) * 128], in_=M[:, kb * 128:(kb + 1) * 128], identity=self.ident),
                        reads=[M_tl, self.cbf_tl], writes=[trpt])
                P.op("act", lambda e, trb=trb, kb0=kb0, nb=nb: e.activation(
                    out=MT[:, kb0:kb0 + nb, :], in_=trb[:, :nb * 128].rearrange("p (b q) -> p b q", b=nb), func=AF.Copy),
                    reads=[trpt], writes=[MT_tl])
            for g in range(4):
                ops_, opst = self.bank(B_O)
                dps, dpst = self.bank(B_DEN)
                for k0 in range(0, nk, 512):
                    w = min(512, nk - k0)
                    nb = w // 128
                    ki = rk[1] % 4
                    rk[1] += 1
                    if k0 < own:
                        ksrc = LOWK[:, g, k0:k0 + w]
                        vsrc = LOWV[:, k0 // 128:k0 // 128 + nb, g * 128:(g + 1) * 128]
                        rdk = []
                        rdv = []
                    else:
                        ksrc = OWNK[:, g, k0 - own:k0 - own + w]
                        vsrc = OWNV[:, (k0 - own) // 128:(k0 - own) // 128 + nb, g * 128:(g + 1) * 128]
                        rdk = []
                        rdv = []
                    self.dma(Kr[ki][:, :w], ksrc, reads=rdk, writes=[Kr_tl[ki]])
                    self.dma(Vr[ki][:, :nb, :], vsrc, reads=rdv, writes=[Vr_tl[ki]])
                    for j in range(nb):
                        kb = k0 // 128 + j
                        stp, stpt = self.bank(B_ST[rk[2] % 2])
                        rk[2] += 1
                        for r in range(4):
                            P.op("pe", lambda e, stp=stp, ki=ki, j=j, g=g, r=r: e.matmul(
                                stp[:, r * 128:(r + 1) * 128], lhsT=Kr[ki][:, j * 128:(j + 1) * 128], rhs=QT[:, 4 * g + r, :],
                                start=True, stop=True),
                                reads=[Kr_tl[ki], QT_tl], writes=[stpt])
                        ei = rk[3] % 3
                        rk[3] += 1
                        P.op("act", lambda e, stp=stp, ei=ei: e.activation(out=ETr[ei][:, :], in_=stp[:, :], func=AF.Exp,
                                                                            scale=float(128 ** -0.5)),
                             reads=[stpt], writes=[ETr_tl[ei]])
                        P.op("pool" if (kb % 2) else "dve", lambda e, ei=ei, kb=kb: e.tensor_tensor(
                            out=PTr[ei][:, :].rearrange("p (r q) -> p r q", r=4),
                            in0=ETr[ei][:, :].rearrange("p (r q) -> p r q", r=4),
                            in1=MT[:, kb, :].unsqueeze(1).to_broadcast([128, 4, 128]), op=ALU.mult),
                            reads=[ETr_tl[ei], MT_tl], writes=[PTr_tl[ei]])
                        P.op("pe", lambda e, ops_=ops_, ki=ki, j=j, ei=ei, kb=kb, nkb=nkb: e.matmul(
                            ops_[:, :], lhsT=Vr[ki][:, j, :], rhs=PTr[ei][:, :], start=(kb == 0), stop=(kb == nkb - 1)),
                            reads=[Vr_tl[ki], PTr_tl[ei]], writes=[opst])
                        P.op("pe", lambda e, dps=dps, ei=ei, kb=kb, nkb=nkb: e.matmul(
                            dps[:, :], lhsT=self.ones, rhs=PTr[ei][:, :], start=(kb == 0), stop=(kb == nkb - 1)),
                            reads=[self.cbf_tl, PTr_tl[ei]], writes=[dpst])
                P.op("dve", lambda e, dps=dps: e.tensor_scalar(out=rden[:, :], in0=dps[:, :], scalar1=1e-30, scalar2=None,
                                                               op0=ALU.add),
                     reads=[dpst], writes=[rden_tl])
                P.op("dve", lambda e: e.reciprocal(out=rden[:, :], in_=rden[:, :]),
                     reads=[rden_tl], writes=[rden_tl])
                oi = rk[4] % 2
                rk[4] += 1
                P.op("dve", lambda e, ops_=ops_, oi=oi: e.tensor_tensor(out=OTs[oi][:, :], in0=ops_[:, :], in1=rden[:, :],
                                                                       op=ALU.mult),
                     reads=[opst, rden_tl], writes=[OTs_tl[oi]])
                self.dma(D_["oT"][:, 4 * g:4 * g + 4, c0:c0 + 128], OTs[oi][:, :].rearrange("p (r q) -> p r q", r=4),
                         reads=[OTs_tl[oi]], writes=[self.dT(("oT", g, qb))], eng="pool")

    def dsa_out(self, t0, TT):
        P = self.P
        X, Xtl, H, Htl = self.X, self.X_tl, self.H, self.H_tl
        rd = [self.dT(("oT", g, qb)) for g in range(4) for qb in range(t0 // 128, (t0 + TT) // 128)]
        self.dma(H[:, :, HALO:HALO + TT], self.dsa["oT"][:, :, t0:t0 + TT], reads=rd, writes=Htl, eng="pool")
        ins = [H[:, c, HALO:HALO + TT] for c in range(DC)]

        def ep(n, ps, pst):
            P.op("dve", lambda e: e.tensor_tensor(out=X[:, n, :TT], in0=X[:, n, :TT], in1=ps, op=ALU.add),
                 reads=[pst, Xtl[n]], writes=[Xtl[n]])

        self.linear("wout", DC, DC, 4, ins, Htl, TT, ep)


def vec_layout():
    VJ = {}
    j = 0
    for name, n in (("norm_mix", 4), ("norm_mlp", 4), ("pool_scale", 2), ("b_pw1a", 1), ("b_pw1g", 1),
                    ("b_dw", 1), ("ln_g", 1), ("ln_b", 1), ("b_pw2", 1), ("norm_final", 1)):
        VJ[name] = j
        j += n
    return VJ, j


def fm(v):
    return np.ascontiguousarray(np.asarray(v, np.float32).reshape(16, 128).T)


def weight_specs(layers):
    specs = []
    for l in layers:
        k, j = l % 3, l // 3
        if k == 0:
            specs.append((f"pool{j}", 1, 8192))
        elif k == 2:
            specs.append(("pw1", 16, 4096))
            specs.append(("pw2", 4, 8192))
        specs.append((f"up{l}", 16, 8192))
        specs.append((f"dn{l}", 16, 8192))
    return specs


def dsa_tensors(B, mode):
    nt = B.nt
    own = nt - PRE
    nq = own + 512

    def exv(EX):
        return {"kT": EX[:, 0:4 * own].rearrange("p (g t) -> p g t", g=4),
                "v": EX[:, 4 * own:8 * own].rearrange("p (b c) -> p b c", b=own // 128),
                "ikT": EX[:, 8 * own:9 * own]}
    EX = B.dscr("EX", [128, 9 * own], BF16)
    EXL = B.dscr("EXL", [128, 9 * own], BF16)
    D_ = {"EX": EX, "low": EXL, "lowv": exv(EXL)}
    D_.update(exv(EX))
    D_["qT"] = B.dscr("qT", [128, 16, nq], BF16)
    D_["iqT"] = B.dscr("iqT", [128, 8, nq], BF16)
    D_["iw"] = B.dscr("iw", [128, nq // 128, 16], F32)
    D_["oT"] = B.dscr("oT", [128, 16, nt], BF16)
    B.din("kvalid", [128, 2 * own], BF16)
    B.din("tri", [128, 128], BF16)
    for nm in ("cos128", "sin128", "cos64", "sin64"):
        B.din(nm, [128, 2 * own])
    return D_


def mode_layers(mode):
    return {"A": (0, 1), "B": (1, 2, 3), "fused": (0, 1, 2, 3)}[mode]


def mode_specs(mode):
    specs = []
    if mode in ("A", "fused"):
        specs += [("pool0", 1, 8192), ("up0", 16, 8192), ("dn0", 16, 8192),
                  ("win", 29, 4096), ("wv", 1, 8192), ("wiw", 1, 256)]
    if mode in ("B", "fused"):
        specs += [("wout", 4, 8192), ("up1", 16, 8192), ("dn1", 16, 8192),
                  ("pw1", 16, 4096), ("pw2", 4, 8192), ("up2", 16, 8192), ("dn2", 16, 8192),
                  ("pool1", 1, 8192), ("up3", 16, 8192), ("dn3", 16, 8192)]
    return specs


def build_program(mode="fused", nt=NT, upto=3, final=True, ncores=NCORES):
    B = Builder(nt=nt, mode=mode, ncores=ncores)
    own = B.own
    B.arena_init()
    B.VJ, nv = vec_layout()
    B.NVEC = nv * 16
    B.wscr = {}
    B.load_consts()
    B.dsa = dsa_tensors(B, mode)
    xin = B.din("xin", [128, DC, 2 * own])
    x1 = B.dscr("x1", [128, DC, 2 * own], F32)
    xs = [B.dscr("xsA", [128, DC, nt], F32), B.dscr("xsB", [128, DC, nt], F32)]
    out = B.dout("out", [128, DC, own])
    B.cast_weights(mode_specs("fused"))
    B.barrier()
    B.tile_bufs()
    for ti, (t0, TT) in enumerate(B.tiles0):
        B.load_x(xin, t0, TT)
        icol = 66 if ti == 0 else (1 if t0 == own else None)
        B.pool_mixer(0, 0, ti, TT, icol=icol)
        B.mlp(0, TT)
        B.store_x(x1, t0, TT)
    B.dsa_proj(x1)
    B.barrier()
    B.dsa_attn()
    B.barrier()
    B.tile_bufs()

    def layer_loop(l, src, dst, last, off=0):
        kind = l % 3
        for ti, (t0, TT) in enumerate(B.tiles):
            B.load_x(src, t0, TT, off=off)
            if ti == 0:
                B.mask_pre(B.X[:, :, :TT], B.X_tl)
            if kind == 0:
                B.pool_mixer(l, l // 3, ti, TT, icol=(1 if ti == 1 else None))
            elif kind == 2:
                B.conv_mixer(l, ti, TT)
            else:
                B.dsa_out(t0, TT)
            B.mlp(l, TT)
            if last:
                if ti == 0:
                    continue
                if final:
                    B.final_norm(TT)
                B.store_x(dst, t0, TT, c0=PRE)
            else:
                B.store_x(dst, t0, TT)

    seq = [(1, x1, xs[0], own - PRE), (2, xs[0], xs[1], 0), (3, xs[1], out, 0)]
    seq = [s for s in seq if s[0] <= upto]
    for i, (l, s, d, off) in enumerate(seq):
        lastl = (i == len(seq) - 1)
        layer_loop(l, s, out if lastl else d, lastl, off=off)
    B.P.emit()
    return B


def prep_common(inp, layers):
    VJ, nv = vec_layout()
    vecs = np.zeros((128, nv * 16), np.float32)

    def put(name, idx, v):
        j = VJ[name] + idx
        vecs[:, j * 16:(j + 1) * 16] = fm(v)

    for i in range(4):
        put("norm_mix", i, inp["norm_mix"][i])
        put("norm_mlp", i, inp["norm_mlp"][i])
    for i in range(2):
        put("pool_scale", i, inp["pool_scale"][i])
    put("b_pw1a", 0, inp["conv_b_pw1"][0][:D])
    put("b_pw1g", 0, inp["conv_b_pw1"][0][D:])
    put("b_dw", 0, inp["conv_b_dw"][0])
    put("ln_g", 0, inp["conv_ln_g"][0])
    put("ln_b", 0, inp["conv_ln_b"][0])
    put("b_pw2", 0, inp["conv_b_pw2"][0])
    put("norm_final", 0, inp["norm_final"])
    cbf = np.zeros((128, 384), np.float32)
    cbf[:, 0:128] = np.eye(128, dtype=np.float32)
    cbf[:, 128:256] = 1.0 / 2048.0
    cbf[:, 256:384] = 1.0
    cbf = cbf.astype(ml_dtypes.bfloat16)
    wdw = np.asarray(inp["conv_w_dw"][0], np.float32)
    convw = np.ascontiguousarray(wdw.reshape(CONVW, 16, 128).transpose(2, 1, 0)).reshape(128, 16 * CONVW)
    com = {"vecs": vecs, "cbf": cbf, "convw": convw}
    for l in layers:
        k, j = l % 3, l // 3
        if k == 0:
            w = np.asarray(inp["pool_w"][j], np.float32)
            com[f"pool{j}_f"] = np.ascontiguousarray(
                w.reshape(4, 4, 128, 512).transpose(2, 0, 1, 3)).reshape(128, 8192)
        elif k == 2:
            w = np.asarray(inp["conv_w_pw1"][0], np.float32)
            w4 = w.reshape(16, 128, 2, 16, 128)
            com["pw1_f"] = np.ascontiguousarray(w4.transpose(1, 3, 0, 2, 4)).reshape(128, 16 * 4096)
            w = np.asarray(inp["conv_w_pw2"][0], np.float32)
            com["pw2_f"] = np.ascontiguousarray(
                w.reshape(16, 128, 4, 512).transpose(1, 2, 0, 3)).reshape(128, 4 * 8192)
        w = np.asarray(inp["mlp_up"][l], np.float32)
        com[f"up{l}_f"] = np.ascontiguousarray(
            w.reshape(16, 128, 16, 512).transpose(1, 2, 0, 3)).reshape(128, 16 * 8192)
        w = np.asarray(inp["mlp_down"][l], np.float32)
        com[f"dn{l}_f"] = np.ascontiguousarray(
            w.reshape(64, 128, 16, 128).transpose(1, 2, 0, 3)).reshape(128, 16 * 8192)
    return com


def core_tokens(x, k, nt=NT):
    b, half = k // 2, k % 2
    own = np.asarray(x[b, half * OWN: half * OWN + (nt - PRE)], np.float32)
    if half == 1:
        pre = np.asarray(x[b, half * OWN - PRE: half * OWN], np.float32)
    else:
        pre = np.zeros((PRE, D), np.float32)
    return np.concatenate([pre, own], axis=0)


def to_fm(a):
    n = a.shape[0]
    return np.ascontiguousarray(a.T.reshape(16, 128, n).transpose(1, 0, 2))


def from_fm(a):
    n = a.shape[2]
    return np.ascontiguousarray(a.transpose(1, 0, 2).reshape(2048, n).T)


def smalls_for(k):
    half = k % 2
    s = np.zeros((128, 160), np.float32)
    s[:, 0] = float(half)
    s[:, 65] = EPS
    for g, w in enumerate(POOL_W):
        for t in range(16):
            start = 1.0 / min(t + 1, w)
            s[:, 1 + 16 * g + t] = start if half == 0 else 1.0 / w
            s[:, 66 + 16 * g + t] = start
    return s


def rope_tables(k, nt=NT):
    half = k % 2
    own = nt - PRE
    pos = (np.arange(2 * own) - (1 - half) * own).astype(np.float32)
    out = {}
    for nm, d in (("128", 128), ("64", 64)):
        inv = (np.float32(10000.0) ** (-np.arange(0, d, 2, dtype=np.float32) / np.float32(d))).astype(np.float32)
        ang = (pos[:, None] * inv[None, :]).astype(np.float32)
        cos = np.cos(ang).astype(np.float32)
        sin = np.sin(ang).astype(np.float32)
        dd = np.arange(128) % d
        idx = dd % (d // 2)
        sign = np.where(dd < d // 2, -1.0, 1.0).astype(np.float32)
        out["cos" + nm] = np.ascontiguousarray(cos[:, idx].T)
        out["sin" + nm] = np.ascontiguousarray((sin[:, idx] * sign[None, :]).T)
    return out


def dsa_weights(inp):
    w = np.asarray(inp["dsa_w_in"][0], np.float32)
    cols = []
    d = np.arange(128)
    for h in range(16):
        cols.append(h * 128 + d)
        cols.append(h * 128 + (d + 64) % 128)
    for g in range(4):
        cols.append(2048 + g * 128 + d)
        cols.append(2048 + g * 128 + (d + 64) % 128)
    for m in range(8):
        head = 2 * m + d // 64
        dd = d % 64
        cols.append(3072 + head * 64 + dd)
        cols.append(3072 + head * 64 + (dd + 32) % 64)
    dd = d % 64
    cols.append(4096 + dd)
    cols.append(4096 + (dd + 32) % 64)
    cols = np.concatenate(cols)
    wp = w[:, cols]
    win = np.ascontiguousarray(wp.reshape(16, 128, 29, 256).transpose(1, 2, 0, 3)).reshape(128, 29 * 4096)
    wv = np.ascontiguousarray(w[:, 2560:3072].reshape(16, 128, 512).transpose(1, 0, 2)).reshape(128, 8192)
    wiw = np.ascontiguousarray(w[:, 4160:4176].reshape(16, 128, 16).transpose(1, 0, 2)).reshape(128, 256)
    wo = np.asarray(inp["dsa_w_out"][0], np.float32)
    wout = np.ascontiguousarray(wo.reshape(16, 128, 4, 512).transpose(1, 2, 0, 3)).reshape(128, 4 * 8192)
    return {"win_f": win, "wv_f": wv, "wiw_f": wiw, "wout_f": wout}


def attn_masks(k, nt=NT):
    half = k % 2
    own = nt - PRE
    kv = np.zeros((128, 2 * own), np.float32)
    if half == 0:
        kv[:, :own] = NEG
    q = np.arange(128)[:, None]
    s = np.arange(128)[None, :]
    tri = np.where(s <= q, 0.0, NEG).astype(np.float32)
    return {"kvalid": kv.astype(ml_dtypes.bfloat16), "tri": tri.astype(ml_dtypes.bfloat16)}


def kernel(**inp):
    inp = {k: np.asarray(v) for k, v in inp.items()}
    com = prep_common(inp, (0, 1, 2, 3))
    com.update(dsa_weights(inp))
    x = inp["x"]
    maps = []
    for k in range(NCORES):
        b, half = k // 2, k % 2
        m = dict(com)
        seq = np.asarray(x[b], np.float32)
        if half == 0:
            seq = np.concatenate([np.zeros((OWN, D), np.float32), seq[:OWN]], axis=0)
        m["xin"] = to_fm(seq)
        m["smalls"] = smalls_for(k)
        m.update(rope_tables(k))
        m.update(attn_masks(k))
        maps.append(m)
    cores = list(range(NCORES))
    BF = build_program("fused")
    res = run_bass_kernel_spmd(BF.nc, [{n: m[n] for n in BF.in_names} for m in maps], core_ids=cores)
    out = np.empty((4, SEQ, D), np.float32)
    for k in cores:
        b, half = k // 2, k % 2
        out[b, half * OWN:(half + 1) * OWN] = from_fm(np.asarray(res.results[k]["out"]))
    return out
```

```python
import numpy as np
import ml_dtypes
import concourse.bass as bass
import concourse.mybir as mybir
from concourse.bass_utils import run_bass_kernel_spmd

F32 = mybir.dt.float32
BF16 = mybir.dt.bfloat16
AF = mybir.ActivationFunctionType
ALU = mybir.AluOpType

D = 2048
DC = 16
DFF = 8192
FC = 64
SEQ = 8192
OWN = 4096
PRE = 128
NT = OWN + PRE
HALO = 32
EPS = 1e-6
NEG = -1.0e30
POOL_W = (2, 4, 8, 16)
CONVW = 31
NCORES = 8


class Tl:
    __slots__ = ("name", "last_w", "readers")

    def __init__(self, name=""):
        self.name = name
        self.last_w = None
        self.readers = {}


class Op:
    __slots__ = ("eng", "fn", "deps", "dma", "idx")


ENG_BLOCK = {"pe": "tensor", "act": "scalar", "dve": "vector", "pool": "gpsimd", "sp": "sync"}


class Prog:
    NDS = 8

    def __init__(self, nc):
        self.nc = nc
        self.ops = []
        self.dma_since_barrier = []

    def op(self, eng, fn, reads=(), writes=(), dma=False, extra_deps=()):
        idx = len(self.ops)
        deps = set(extra_deps)
        for t in reads:
            if t.last_w is not None:
                deps.add(t.last_w)
        for t in writes:
            if t.last_w is not None:
                deps.add(t.last_w)
            deps.update(t.readers.values())
        key = ("d", idx) if dma else eng
        for t in reads:
            t.readers[key] = idx
        for t in writes:
            t.last_w = idx
            t.readers = {}
        deps.discard(idx)
        o = Op()
        o.eng = eng
        o.fn = fn
        o.dma = dma
        o.idx = idx
        ops = self.ops
        if eng == "pe" and not dma:
            o.deps = [d for d in deps if not (ops[d].eng == "pe" and not ops[d].dma)]
        else:
            o.deps = list(deps)
        ops.append(o)
        if dma:
            self.dma_since_barrier.append(idx)
        return idx

    def emit(self):
        nc = self.nc
        ops = self.ops
        need = [False] * len(ops)
        for o in ops:
            for d in o.deps:
                need[d] = True
        engs = []
        for o in ops:
            if o.eng not in engs:
                engs.append(o.eng)
        csem = {}
        ccnt = {}
        dsem = {}
        dcnt = {}
        drr = {}
        for e in engs:
            csem[e] = nc.alloc_semaphore(name=f"c_{e}")
            ccnt[e] = 0
            dsem[e] = [nc.alloc_semaphore(name=f"d_{e}_{i}") for i in range(self.NDS)]
            drr[e] = 0
            for i in range(self.NDS):
                dcnt[(e, i)] = 0
        sig = {}
        pre_wait = {}
        for o in ops:
            if o.dma:
                i = drr[o.eng]
                drr[o.eng] = (i + 1) % self.NDS
                prev = dcnt[(o.eng, i)]
                pre_wait[o.idx] = (dsem[o.eng][i], prev)
                dcnt[(o.eng, i)] = prev + 16
                sig[o.idx] = (dsem[o.eng][i], prev + 16)
            elif need[o.idx]:
                ccnt[o.eng] += 1
                sig[o.idx] = (csem[o.eng], ccnt[o.eng])
        self.max_counts = dict(ccnt)
        final_dma = [(dsem[e][i], dcnt[(e, i)]) for e in engs for i in range(self.NDS) if dcnt[(e, i)] > 0]
        with nc.Block() as block:
            for e in engs:
                my = [o for o in ops if o.eng == e]

                def body(eng, my=my, e=e):
                    waited = {}

                    def wait(sem, val):
                        if val <= 0:
                            return
                        k = sem.num
                        if waited.get(k, 0) >= val:
                            return
                        eng.wait_ge(sem, val)
                        waited[k] = val

                    for o in my:
                        for d in sorted(o.deps):
                            wait(*sig[d])
                        if o.dma:
                            wait(*pre_wait[o.idx])
                        ins = o.fn(eng)
                        if o.idx in sig:
                            ins.then_inc(sig[o.idx][0], 16 if o.dma else 1)
                    if e == "sp":
                        for s, v in final_dma:
                            wait(s, v)

                getattr(block, ENG_BLOCK[e])(body)


def make_tiles(nt):
    tiles = [(0, PRE)]
    t = PRE
    while t < nt:
        tiles.append((t, 512))
        t += 512
    return tiles


class Builder:
    def __init__(self, nt=NT, layers=(0, 1, 2, 3), mode="full", ncores=NCORES):
        self.ncores = ncores
        self.nt = nt
        self.tiles = make_tiles(nt)
        self.own = nt - PRE
        self.tiles0 = [(512 * i, 512) for i in range(2 * self.own // 512)]
        self.layers = layers
        self.mode = mode
        self.nc = bass.Bass("TRN2", target_bir_lowering=False)
        self.P = Prog(self.nc)
        self.dram = {}
        self.dtl = {}

    def din(self, name, shape, dt=F32):
        t = self.nc.dram_tensor(name, list(shape), dt, kind="ExternalInput")
        self.dram[name] = t
        if not hasattr(self, "in_names"):
            self.in_names = []
        self.in_names.append(name)
        return t

    def dout(self, name, shape, dt=F32):
        t = self.nc.dram_tensor(name, list(shape), dt, kind="ExternalOutput")
        self.dram[name] = t
        return t

    def dscr(self, name, shape, dt):
        t = self.nc.dram_tensor(name, list(shape), dt)
        self.dram[name] = t
        return t

    def dT(self, key):
        if key not in self.dtl:
            self.dtl[key] = Tl(str(key))
        return self.dtl[key]

    def arena_init(self):
        nc = self.nc
        self.ARENA_W = 52800
        self.arena = nc.alloc_sbuf_tensor("arena", [128, self.ARENA_W], F32)
        self.arena_off = 0
        self.psum = nc.alloc_psum_tensor("psum", [128, 4096], F32)
        self.pbank = [Tl(f"bank{i}") for i in range(8)]
        self.pb_rr = 0

    def arena_reset(self, keep=0):
        self.arena_off = keep

    def alloc(self, nbytes, dt=F32):
        words = (nbytes + 3) // 4
        words = (words + 7) // 8 * 8
        a = self.arena[:, self.arena_off:self.arena_off + words]
        self.arena_off += words
        assert self.arena_off <= self.ARENA_W, ("SBUF arena overflow", self.arena_off * 4)
        if dt == BF16:
            a = a.bitcast(BF16)
        return a

    def bank(self, i=None):
        if i is None:
            i = self.pb_rr
            self.pb_rr = (self.pb_rr + 1) % 8
        return self.psum[:, i * 512:(i + 1) * 512], self.pbank[i]

    def barrier(self):
        P = self.P
        nc = self.nc
        if not hasattr(self, "bar_sb"):
            self.bar_sb = nc.alloc_sbuf_tensor("bar_sb", [128, 64], F32)
            self.bar_tl = {e: Tl("bar_" + e) for e in ("act", "dve", "pool", "pe")}
        sb = self.bar_sb
        tl = self.bar_tl
        dmas = list(P.dma_since_barrier)
        P.dma_since_barrier = []
        i1 = P.op("act", lambda e: e.activation(out=sb[:, 0:8], in_=sb[:, 32:40], func=AF.Copy),
                  writes=[tl["act"]], extra_deps=dmas)
        i2 = P.op("dve", lambda e: e.memset(sb[:, 8:16], 0.0), writes=[tl["dve"]], extra_deps=dmas)
        i3 = P.op("pool", lambda e: e.memset(sb[:, 16:24], 0.0), writes=[tl["pool"]], extra_deps=dmas)
        pb, pbt = self.bank(7)
        i4 = P.op("pe", lambda e: e.matmul(pb[:, 0:8], lhsT=self.ident[:, 0:128], rhs=self.ident[:, 0:8],
                                           start=True, stop=True),
                  reads=[self.cbf_tl], writes=[pbt, tl["pe"]], extra_deps=dmas)
        allb = [i1, i2, i3, i4]
        P.op("act", lambda e: e.activation(out=sb[:, 0:8], in_=sb[:, 32:40], func=AF.Copy),
             writes=[tl["act"]], extra_deps=allb)
        P.op("dve", lambda e: e.memset(sb[:, 8:16], 0.0), writes=[tl["dve"]], extra_deps=allb)
        P.op("pool", lambda e: e.memset(sb[:, 16:24], 0.0), writes=[tl["pool"]], extra_deps=allb)
        P.op("pe", lambda e: e.matmul(pb[:, 0:8], lhsT=self.ident[:, 0:128], rhs=self.ident[:, 0:8],
                                      start=True, stop=True),
             writes=[pbt, tl["pe"]], extra_deps=allb)
        self.sp_fence = allb

    def dma(self, out, in_, reads=(), writes=(), eng="sp"):
        fence = getattr(self, "sp_fence", ())
        return self.P.op(eng, lambda e: e.dma_start(out=out, in_=in_), reads=reads, writes=writes,
                         dma=True, extra_deps=fence)

    def load_consts(self):
        P = self.P
        nc = self.nc
        vec_in = self.din("vecs", [128, self.NVEC])
        self.vecs = self.alloc(self.NVEC * 4)
        self.vecs_tl = Tl("vecs")
        self.dma(self.vecs, vec_in[:, :], writes=[self.vecs_tl])
        cb_in = self.din("cbf", [128, 384], BF16)
        self.cbf = self.alloc(384 * 2, BF16)
        self.cbf_tl = Tl("cbf")
        self.dma(self.cbf, cb_in[:, :], writes=[self.cbf_tl])
        self.ident = self.cbf[:, 0:128]
        self.onesm = self.cbf[:, 128:256]
        self.ones = self.cbf[:, 256:384]
        cw_in = self.din("convw", [128, 16 * CONVW])
        self.convw = self.alloc(16 * CONVW * 4)
        self.convw_tl = Tl("convw")
        self.dma(self.convw, cw_in[:, :], writes=[self.convw_tl])
        sm_in = self.din("smalls", [128, 192])
        self.smalls = self.alloc(192 * 4)
        self.smalls_tl = Tl("smalls")
        self.dma(self.smalls, sm_in[:, :], writes=[self.smalls_tl])
        self.pm = self.smalls[:, 0:1]
        self.epsc = self.smalls[:, 65:66]
        self.tinyc = self.smalls[:, 130:131]
        self.const_end = self.arena_off

    def vec(self, j, c):
        return self.vecs[:, j * 16 + c: j * 16 + c + 1]

    def cast_weights(self, specs):
        P = self.P
        CH = 4096
        NB = 3
        stage = [self.alloc(CH * 4) for _ in range(NB)]
        stage_tl = [Tl(f"stg{i}") for i in range(NB)]
        outb = [self.alloc(CH * 2, BF16) for _ in range(NB)]
        outb_tl = [Tl(f"cst{i}") for i in range(NB)]
        k = 0
        for name, nblk, E in specs:
            src = self.din(name + "_f", [128, nblk * E])
            dst = self.dscr(name, [128, nblk * E], BF16)
            self.wscr[name] = (dst, nblk, E)
            tot = nblk * E
            for o in range(0, tot, CH):
                w = min(CH, tot - o)
                i = k % NB
                self.dma(stage[i][:, :w], src[:, o:o + w], writes=[stage_tl[i]])
                ce = ("dve", "act", "pool")[k % 3]
                so, oo = stage[i][:, :w], outb[i][:, :w]
                if ce == "act":
                    P.op("act", lambda e, so=so, oo=oo: e.activation(out=oo, in_=so, func=AF.Copy),
                         reads=[stage_tl[i]], writes=[outb_tl[i]])
                elif ce == "dve":
                    P.op("dve", lambda e, so=so, oo=oo: e.tensor_copy(out=oo, in_=so),
                         reads=[stage_tl[i]], writes=[outb_tl[i]])
                else:
                    P.op("pool", lambda e, so=so, oo=oo: e.tensor_copy(out=oo, in_=so),
                         reads=[stage_tl[i]], writes=[outb_tl[i]])
                self.dma(dst[:, o:o + w], outb[i][:, :w], reads=[outb_tl[i]],
                         writes=[self.dT((name, "c", o // CH))])
                k += 1

    def wring_init(self, nbuf=3, E=8192):
        self.wr = [self.alloc(E * 2, BF16) for _ in range(nbuf)]
        self.wr_tl = [Tl(f"wr{i}") for i in range(nbuf)]
        self.wr_i = 0

    def wload(self, name, blk):
        dst, nblk, E = self.wscr[name]
        i = self.wr_i
        self.wr_i = (self.wr_i + 1) % len(self.wr)
        buf = self.wr[i][:, :E]
        rd = [self.dT((name, "c", q // 4096)) for q in range(blk * E, (blk + 1) * E, 4096)]
        self.dma(buf, dst[:, blk * E:(blk + 1) * E], reads=rd, writes=[self.wr_tl[i]])
        return buf, self.wr_tl[i]

    def linear(self, wname, KC, nchunks, NBC, ins, ins_tl, TT, epilogue, blk0=0):
        P = self.P
        nb = NBC * 128
        for b in range(nchunks // NBC):
            wt, wtl = self.wload(wname, blk0 + b)
            wv = wt.rearrange("p (k n) -> p k n", k=KC)
            for jj in range(NBC):
                n = b * NBC + jj
                ps, pst = self.bank()
                for kc in range(KC):
                    P.op("pe", lambda e, ps=ps, wv=wv, kc=kc, jj=jj, r=ins[kc]: e.matmul(
                        ps[:, :TT], lhsT=wv[:, kc, jj * 128:(jj + 1) * 128], rhs=r,
                        start=(kc == 0), stop=(kc == KC - 1)),
                        reads=[wtl, ins_tl[kc]], writes=[pst])
                epilogue(n, ps[:, :TT], pst)

    def rmsnorm(self, TT, gj, out_ap_fn, out_tl_fn, sq_view, sq_tl):
        P = self.P
        X, Xtl = self.X, self.X_tl
        sqs = [sq_view(c)[:, :TT] for c in range(DC)]
        sqt = [sq_tl(c) for c in range(DC)]
        oaps = [out_ap_fn(c) for c in range(DC)]
        otls = [out_tl_fn(c) for c in range(DC)]
        for c in range(DC):
            P.op("act", lambda e, c=c: e.activation(out=sqs[c], in_=X[:, c, :TT], func=AF.Square),
                 reads=[Xtl[c]], writes=[sqt[c]])
        ps, pst = self.bank()
        for c in range(DC):
            P.op("pe", lambda e, c=c, ps=ps: e.matmul(ps[:, :TT], lhsT=self.onesm, rhs=sqs[c],
                                                      start=(c == 0), stop=(c == DC - 1)),
                 reads=[sqt[c], self.cbf_tl], writes=[pst])
        rs = self.rstd
        P.op("act", lambda e, ps=ps: e.activation(out=rs[:, :TT], in_=ps[:, :TT], func=AF.Sqrt,
                                                  bias=self.epsc, scale=1.0),
             reads=[pst, self.smalls_tl], writes=[self.rstd_tl])
        P.op("dve", lambda e: e.reciprocal(out=rs[:, :TT], in_=rs[:, :TT]),
             reads=[self.rstd_tl], writes=[self.rstd_tl])
        for c in range(DC):
            P.op("dve", lambda e, c=c: e.scalar_tensor_tensor(
                out=oaps[c], in0=X[:, c, :TT], scalar=self.vec(gj, c), in1=rs[:, :TT],
                op0=ALU.mult, op1=ALU.mult),
                reads=[Xtl[c], self.rstd_tl, self.vecs_tl], writes=[otls[c]])

    def mlp(self, l, TT):
        P = self.P
        X, Xtl = self.X, self.X_tl
        H, Htl = self.H, self.H_tl
        G = self.G
        Gtl = self.G_tl

        def gv(j):
            return G[j // 4][:, (j % 4) * 512:(j % 4) * 512 + 512]

        self.rmsnorm(TT, self.VJ["norm_mlp"] + l,
                     lambda c: H[:, c, HALO:HALO + TT], lambda c: Htl[c],
                     lambda c: gv(c), lambda c: Gtl[c // 4])
        ins = [H[:, c, HALO:HALO + TT] for c in range(DC)]

        def ep_up(n, ps, pst):
            r, rtl = self.R[n % 2], self.R_tl[n % 2]
            P.op("act", lambda e: e.activation(out=r[:, :TT], in_=ps, func=AF.Relu),
                 reads=[pst], writes=[rtl])
            P.op("pool" if (n % 2) else "dve",
                 lambda e: e.tensor_tensor(out=gv(n)[:, :TT], in0=r[:, :TT], in1=r[:, :TT], op=ALU.mult),
                 reads=[rtl], writes=[Gtl[n // 4]])

        self.linear(f"up{l}", DC, FC, 4, ins, Htl, TT, ep_up)
        gins = [gv(j)[:, :TT] for j in range(FC)]
        gtl = [Gtl[j // 4] for j in range(FC)]

        def ep_dn(n, ps, pst):
            P.op("dve", lambda e: e.tensor_tensor(out=X[:, n, :TT], in0=X[:, n, :TT], in1=ps, op=ALU.add),
                 reads=[pst, Xtl[n]], writes=[Xtl[n]])

        self.linear(f"dn{l}", FC, DC, 1, gins, gtl, TT, ep_dn)

    def tile_bufs(self):
        self.arena_reset(self.const_end)
        Xf = self.alloc(DC * 512 * 4)
        self.X = Xf.rearrange("p (c t) -> p c t", c=DC)
        self.X_tl = [Tl(f"X{c}") for c in range(DC)]
        Hf = self.alloc(DC * (HALO + 512) * 2, BF16)
        self.H = Hf.rearrange("p (c t) -> p c t", c=DC)
        self.H_tl = [Tl(f"H{c}") for c in range(DC)]
        RB = 4352
        self.G_off = self.arena_off
        self.G = [self.alloc(RB, BF16) for _ in range(16)]
        self.G_tl = [Tl(f"G{i}") for i in range(16)]
        self.rstd = self.alloc(512 * 4)
        self.rstd_tl = Tl("rstd")
        self.tmpA = self.alloc(512 * 4)
        self.tmpA_tl = Tl("tmpA")
        self.tmpB = self.alloc(512 * 4)
        self.tmpB_tl = Tl("tmpB")
        self.UH = self.alloc(DC * HALO * 4).rearrange("p (c t) -> p c t", c=DC)
        self.UH_tl = Tl("UH")
        self.HH = self.alloc(DC * HALO * 2, BF16).rearrange("p (c t) -> p c t", c=DC)
        self.HH_tl = Tl("HH")
        self.R = [self.alloc(512 * 2, BF16) for _ in range(2)]
        self.R_tl = [Tl("R0"), Tl("R1")]
        self.wring_init(3, 8192)

    def load_x(self, src, t0, TT, off=0):
        self.dma(self.X[:, :, :TT], src[:, :, off + t0:off + t0 + TT], reads=[self.dT((src.name, off + t0))],
                 writes=self.X_tl, eng="pool")

    def store_x(self, dst, t0, TT, c0=0):
        self.dma(dst[:, :, t0 - c0:t0 - c0 + TT], self.X[:, :, :TT], reads=self.X_tl,
                 writes=[self.dT((dst.name, t0))], eng="pool")

    def mask_pre(self, ap3, tls):
        P = self.P
        P.op("dve", lambda e: e.tensor_scalar(out=ap3, in0=ap3, scalar1=self.pm, scalar2=None, op0=ALU.mult),
             reads=list(tls) + [self.smalls_tl], writes=list(tls))

    def pool_mixer(self, l, j, ti, TT, icol=None):
        P = self.P
        X, Xtl, H, Htl = self.X, self.X_tl, self.H, self.H_tl
        G, Gtl = self.G, self.G_tl
        if ti == 0:
            P.op("pool", lambda e: e.memset(H[:, :, 0:HALO], 0.0), writes=Htl)
        else:
            P.op("pool", lambda e: e.tensor_copy(out=H[:, :, 0:HALO], in_=self.HH[:, :, :]),
                 reads=[self.HH_tl], writes=Htl)
        self.rmsnorm(TT, self.VJ["norm_mix"] + l,
                     lambda c: H[:, c, HALO:HALO + TT], lambda c: Htl[c],
                     lambda c: G[c][:, 0:512], lambda c: Gtl[c])
        P.op("pool", lambda e: e.tensor_copy(out=self.HH[:, :, :], in_=H[:, :, TT:TT + HALO]),
             reads=Htl, writes=[self.HH_tl])
        W = HALO + TT
        for g in range(4):
            nsteps = g + 1
            w = POOL_W[g]
            for k in range(4):
                r = 4 * g + k
                reg32 = G[r].bitcast(F32)
                rtl = Gtl[r]
                eng = "pool" if (k % 2) else "dve"
                sh = 1
                for s in range(nsteps):
                    lo = 2 * sh - 1
                    dst = reg32[:, (s % 2) * 544:(s % 2) * 544 + 544]
                    if s == 0:
                        a = H[:, r, lo:W]
                        b = H[:, r, lo - sh:W - sh]
                        rd = [Htl[r]]
                    else:
                        srcb = reg32[:, ((s - 1) % 2) * 544:((s - 1) % 2) * 544 + 544]
                        a = srcb[:, lo:W]
                        b = srcb[:, lo - sh:W - sh]
                        rd = [rtl]
                    P.op(eng, lambda e, dst=dst, a=a, b=b, lo=lo: e.tensor_tensor(out=dst[:, lo:W], in0=a, in1=b, op=ALU.add),
                         reads=rd, writes=[rtl])
                    sh *= 2
                fin = reg32[:, ((nsteps - 1) % 2) * 544:((nsteps - 1) % 2) * 544 + 544]
                y = G[r][:, (nsteps % 2) * 1088:(nsteps % 2) * 1088 + 512]
                P.op("dve", lambda e, y=y, fin=fin, r=r, w=w: e.scalar_tensor_tensor(
                    out=y[:, :TT], in0=fin[:, HALO:HALO + TT], scalar=1.0 / w, in1=H[:, r, HALO:HALO + TT],
                    op0=ALU.mult, op1=ALU.subtract),
                    reads=[rtl, Htl[r]], writes=[rtl])
                if icol is not None:
                    ic = self.smalls[:, icol + 16 * g: icol + 16 * g + 16]
                    tmp = self.tmpA[:, 0:16]
                    P.op("dve", lambda e, fin=fin, ic=ic, tmp=tmp: e.tensor_tensor(
                        out=tmp, in0=fin[:, HALO:HALO + 16], in1=ic, op=ALU.mult),
                        reads=[rtl, self.smalls_tl], writes=[self.tmpA_tl])
                    P.op("dve", lambda e, y=y, tmp=tmp, r=r: e.tensor_tensor(
                        out=y[:, 0:16], in0=tmp, in1=H[:, r, HALO:HALO + 16], op=ALU.subtract),
                        reads=[self.tmpA_tl, Htl[r]], writes=[rtl])
        wt, wtl = self.wload(f"pool{j}", 0)
        wv = wt.rearrange("p (g k n) -> p g k n", g=4, k=4)
        for g in range(4):
            nsteps = g + 1
            for jj in range(4):
                n = 4 * g + jj
                ps, pst = self.bank()
                for kc in range(4):
                    y = G[4 * g + kc][:, (nsteps % 2) * 1088:(nsteps % 2) * 1088 + 512]
                    P.op("pe", lambda e, ps=ps, g=g, kc=kc, jj=jj, y=y: e.matmul(
                        ps[:, :TT], lhsT=wv[:, g, kc, jj * 128:(jj + 1) * 128], rhs=y[:, :TT],
                        start=(kc == 0), stop=(kc == 3)),
                        reads=[wtl, Gtl[4 * g + kc]], writes=[pst])
                P.op("dve", lambda e, ps=ps, n=n: e.scalar_tensor_tensor(
                    out=X[:, n, :TT], in0=ps[:, :TT], scalar=self.vec(self.VJ["pool_scale"] + j, n), in1=X[:, n, :TT],
                    op0=ALU.mult, op1=ALU.add),
                    reads=[pst, Xtl[n], self.vecs_tl], writes=[Xtl[n]])

    def conv_mixer(self, l, ti, TT):
        P = self.P
        X, Xtl, H, Htl = self.X, self.X_tl, self.H, self.H_tl
        G, Gtl = self.G, self.G_tl
        VJ = self.VJ
        UW = HALO + 512

        def U(c):
            return G[c].bitcast(F32)[:, 0:UW]

        def ACC(c):
            return G[c].bitcast(F32)[:, UW:UW + 512]

        self.rmsnorm(TT, VJ["norm_mix"] + l,
                     lambda c: H[:, c, HALO:HALO + TT], lambda c: Htl[c],
                     lambda c: G[c][:, 0:512], lambda c: Gtl[c])
        for c in range(DC):
            if ti == 0:
                P.op("pool", lambda e, c=c: e.memset(U(c)[:, 0:HALO], 0.0), writes=[Gtl[c]])
            else:
                P.op("pool", lambda e, c=c: e.tensor_copy(out=U(c)[:, 0:HALO], in_=self.UH[:, c, :]),
                     reads=[self.UH_tl], writes=[Gtl[c]])
        ins = [H[:, c, HALO:HALO + TT] for c in range(DC)]
        state = {}

        def ep_pw1(n, ps, pst):
            c = n // 2
            if n % 2 == 0:
                state["a"] = (ps, pst)
                return
            aps, apst = state["a"]
            sig = self.tmpA
            P.op("act", lambda e: e.activation(out=sig[:, :TT], in_=ps, func=AF.Sigmoid,
                                               bias=self.vec(VJ["b_pw1g"], c), scale=1.0),
                 reads=[pst, self.vecs_tl], writes=[self.tmpA_tl])
            P.op("dve", lambda e: e.scalar_tensor_tensor(
                out=U(c)[:, HALO:HALO + TT], in0=aps, scalar=self.vec(VJ["b_pw1a"], c), in1=sig[:, :TT],
                op0=ALU.add, op1=ALU.mult),
                reads=[apst, self.tmpA_tl, self.vecs_tl], writes=[Gtl[c]])

        self.linear("pw1", DC, 32, 2, ins, Htl, TT, ep_pw1)
        if ti == 0:
            for c in range(DC):
                self.mask_pre(U(c)[:, HALO:HALO + TT], [Gtl[c]])
        P.op("pool", lambda e: e.tensor_copy(
            out=self.UH[:, 0, :], in_=U(0)[:, TT:TT + HALO]), reads=[Gtl[0]], writes=[self.UH_tl])
        for c in range(1, DC):
            P.op("pool", lambda e, c=c: e.tensor_copy(out=self.UH[:, c, :], in_=U(c)[:, TT:TT + HALO]),
                 reads=[Gtl[c], self.UH_tl], writes=[self.UH_tl])
        for c in range(DC):
            u = U(c)
            acc = ACC(c)
            for k in range(CONVW):
                off = HALO - (CONVW - 1) + k
                wcol = self.convw[:, c * CONVW + k: c * CONVW + k + 1]
                if k == 0:
                    P.op("dve", lambda e, u=u, acc=acc, off=off, wcol=wcol, c=c: e.tensor_scalar(
                        out=acc[:, :TT], in0=u[:, off:off + TT], scalar1=wcol, scalar2=self.vec(VJ["b_dw"], c),
                        op0=ALU.mult, op1=ALU.add),
                        reads=[Gtl[c], self.convw_tl, self.vecs_tl], writes=[Gtl[c]])
                else:
                    P.op("dve", lambda e, u=u, acc=acc, off=off, wcol=wcol: e.scalar_tensor_tensor(
                        out=acc[:, :TT], in0=u[:, off:off + TT], scalar=wcol, in1=acc[:, :TT],
                        op0=ALU.mult, op1=ALU.add),
                        reads=[Gtl[c], self.convw_tl], writes=[Gtl[c]])
        for c in range(DC):
            P.op("act", lambda e, c=c: e.activation(out=H[:, c, HALO:HALO + TT], in_=ACC(c)[:, :TT], func=AF.Copy),
                 reads=[Gtl[c]], writes=[Htl[c]])
            P.op("act", lambda e, c=c: e.activation(out=G[c][:, 0:TT], in_=ACC(c)[:, :TT], func=AF.Square),
                 reads=[Gtl[c]], writes=[Gtl[c]])
        psm, psmt = self.bank()
        pss, psst = self.bank()
        for c in range(DC):
            P.op("pe", lambda e, c=c: e.matmul(psm[:, :TT], lhsT=self.onesm, rhs=H[:, c, HALO:HALO + TT],
                                               start=(c == 0), stop=(c == DC - 1)),
                 reads=[Htl[c], self.cbf_tl], writes=[psmt])
        for c in range(DC):
            P.op("pe", lambda e, c=c: e.matmul(pss[:, :TT], lhsT=self.onesm, rhs=G[c][:, 0:TT],
                                               start=(c == 0), stop=(c == DC - 1)),
                 reads=[Gtl[c], self.cbf_tl], writes=[psst])
        mean = self.tmpA
        var = self.tmpB
        rs = self.rstd
        P.op("act", lambda e: e.activation(out=mean[:, :TT], in_=psm[:, :TT], func=AF.Copy),
             reads=[psmt], writes=[self.tmpA_tl])
        P.op("dve", lambda e: e.tensor_tensor(out=var[:, :TT], in0=mean[:, :TT], in1=mean[:, :TT], op=ALU.mult),
             reads=[self.tmpA_tl], writes=[self.tmpB_tl])
        P.op("dve", lambda e: e.tensor_tensor(out=var[:, :TT], in0=pss[:, :TT], in1=var[:, :TT], op=ALU.subtract),
             reads=[psst, self.tmpB_tl], writes=[self.tmpB_tl])
        P.op("act", lambda e: e.activation(out=rs[:, :TT], in_=var[:, :TT], func=AF.Sqrt,
                                           bias=self.epsc, scale=1.0),
             reads=[self.tmpB_tl, self.smalls_tl], writes=[self.rstd_tl])
        P.op("dve", lambda e: e.reciprocal(out=rs[:, :TT], in_=rs[:, :TT]),
             reads=[self.rstd_tl], writes=[self.rstd_tl])
        for c in range(DC):
            acc = ACC(c)
            P.op("dve", lambda e, acc=acc: e.tensor_tensor(out=acc[:, :TT], in0=acc[:, :TT], in1=mean[:, :TT],
                                                           op=ALU.subtract),
                 reads=[Gtl[c], self.tmpA_tl], writes=[Gtl[c]])
            P.op("dve", lambda e, acc=acc, c=c: e.scalar_tensor_tensor(
                out=acc[:, :TT], in0=acc[:, :TT], scalar=self.vec(VJ["ln_g"], c), in1=rs[:, :TT],
                op0=ALU.mult, op1=ALU.mult),
                reads=[Gtl[c], self.rstd_tl, self.vecs_tl], writes=[Gtl[c]])
            P.op("act", lambda e, acc=acc, c=c: e.activation(
                out=H[:, c, HALO:HALO + TT], in_=acc[:, :TT], func=AF.Silu, bias=self.vec(VJ["ln_b"], c), scale=1.0),
                reads=[Gtl[c], self.vecs_tl], writes=[Htl[c]])

        def ep_pw2(n, ps, pst):
            P.op("dve", lambda e: e.scalar_tensor_tensor(
                out=X[:, n, :TT], in0=ps, scalar=self.vec(VJ["b_pw2"], n), in1=X[:, n, :TT],
                op0=ALU.add, op1=ALU.add),
                reads=[pst, Xtl[n], self.vecs_tl], writes=[Xtl[n]])

        self.linear("pw2", DC, DC, 4, ins, Htl, TT, ep_pw2)

    def final_norm(self, TT):
        X, Xtl = self.X, self.X_tl
        G, Gtl = self.G, self.G_tl
        self.rmsnorm(TT, self.VJ["norm_final"],
                     lambda c: X[:, c, :TT], lambda c: Xtl[c],
                     lambda c: G[c][:, 0:512], lambda c: Gtl[c])


    def dsa_proj(self, src, l=1):
        P = self.P
        X, Xtl, H, Htl = self.X, self.X_tl, self.H, self.H_tl
        G, Gtl = self.G, self.G_tl
        D_ = self.dsa
        tabs = [self.alloc(512 * 4) for _ in range(4)]
        tabs_tl = [Tl(f"tab{i}") for i in range(4)]
        wvv = self.arena[:, self.G_off:self.G_off + 16 * 1088].bitcast(BF16).rearrange(
            "p (k n) -> p k n", k=16)[:, :, 1024:1536]
        wiw = self.alloc(256 * 2, BF16)
        wiw_tl = Tl("wiw")
        dst, _, _ = self.wscr["wv"]
        self.dma(wvv, dst[:, :].rearrange("p (k n) -> p k n", k=16),
                 reads=[self.dT(("wv", "c", 0)), self.dT(("wv", "c", 1))], writes=Gtl)
        dst, _, _ = self.wscr["wiw"]
        self.dma(wiw, dst[:, :], reads=[self.dT(("wiw", "c", 0))], writes=[wiw_tl])
        wiwv = wiw.rearrange("p (k n) -> p k n", k=16)
        stg = [self.alloc(512 * 2, BF16) for _ in range(4)]
        stg_tl = [Tl(f"stg{i}") for i in range(4)]
        iwst = self.alloc(16 * 4)
        iwst_tl = Tl("iwst")
        sk = [0]
        own = self.own
        QOFF = own - 512
        for ti, (t0, TT) in enumerate(self.tiles0):
            self.load_x(src, t0, TT)
            full = (t0 >= QOFF)
            EXd = D_ if t0 >= own else D_["lowv"]
            kcol = t0 - own if t0 >= own else t0
            for i, nm in enumerate(("cos128", "sin128", "cos64", "sin64")):
                self.dma(tabs[i][:, :TT], self.dram[nm][:, t0:t0 + TT], writes=[tabs_tl[i]], eng="pool")
            self.rmsnorm(TT, self.VJ["norm_mix"] + l,
                         lambda c: H[:, c, HALO:HALO + TT], lambda c: Htl[c],
                         lambda c: G[c][:, 0:512], lambda c: Gtl[c])
            ins = [H[:, c, HALO:HALO + TT] for c in range(DC)]
            state = {}

            def ep(n, ps, pst, ti=ti, t0=t0, TT=TT, EXd=EXd, kcol=kcol):
                p = n // 2
                if n % 2 == 0:
                    state["a"] = (ps, pst)
                    return
                aps, apst = state["a"]
                big = p < 20
                ct, st_ = (tabs[0], tabs[1]) if big else (tabs[2], tabs[3])
                ctl, stl = (tabs_tl[0], tabs_tl[1]) if big else (tabs_tl[2], tabs_tl[3])
                P.op("dve", lambda e: e.tensor_tensor(out=self.tmpA[:, :TT], in0=aps, in1=ct[:, :TT], op=ALU.mult),
                     reads=[apst, ctl], writes=[self.tmpA_tl])
                P.op("dve", lambda e: e.tensor_tensor(out=self.tmpB[:, :TT], in0=ps, in1=st_[:, :TT], op=ALU.mult),
                     reads=[pst, stl], writes=[self.tmpB_tl])
                i = sk[0] % 4
                sk[0] += 1
                P.op("pool", lambda e: e.tensor_tensor(out=stg[i][:, :TT], in0=self.tmpA[:, :TT],
                                                       in1=self.tmpB[:, :TT], op=ALU.add),
                     reads=[self.tmpA_tl, self.tmpB_tl], writes=[stg_tl[i]])
                if p < 16:
                    self.dma(D_["qT"][:, p, t0 - QOFF:t0 - QOFF + TT], stg[i][:, :TT], reads=[stg_tl[i]],
                             writes=[self.dT(("qT", p, t0))], eng="pool")
                elif p < 20:
                    self.dma(EXd["kT"][:, p - 16, kcol:kcol + TT], stg[i][:, :TT], reads=[stg_tl[i]],
                             writes=[self.dT(("kT", p - 16, t0))], eng="pool")
                elif p < 28:
                    self.dma(D_["iqT"][:, p - 20, t0 - QOFF:t0 - QOFF + TT], stg[i][:, :TT], reads=[stg_tl[i]],
                             writes=[self.dT(("iqT", p - 20, t0))], eng="pool")
                else:
                    self.dma(EXd["ikT"][:, kcol:kcol + TT], stg[i][:, :TT], reads=[stg_tl[i]],
                             writes=[self.dT(("ikT", t0))], eng="pool")

            if full:
                self.linear("win", DC, 58, 2, ins, Htl, TT, ep)
            else:
                self.linear("win", DC, 8, 2, ins, Htl, TT, lambda n, ps, pst, ep=ep: ep(n + 32, ps, pst), blk0=16)
                self.linear("win", DC, 2, 2, ins, Htl, TT, lambda n, ps, pst, ep=ep: ep(n + 56, ps, pst), blk0=28)
            for tb in range(TT // 128):
                blk = (kcol + tb * 128) // 128
                hb = lambda kc, tb=tb: H[:, kc, HALO + tb * 128:HALO + tb * 128 + 128]
                if True:
                    ps, pst = self.bank()
                    for kc in range(DC):
                        P.op("pe", lambda e, ps=ps, kc=kc, hb=hb: e.matmul(ps[:, :512], lhsT=hb(kc), rhs=wvv[:, kc, :],
                                                                          start=(kc == 0), stop=(kc == DC - 1)),
                             reads=[Htl[kc], Gtl[kc]], writes=[pst])
                    i = sk[0] % 4
                    sk[0] += 1
                    P.op("act", lambda e, ps=ps, i=i: e.activation(out=stg[i][:, :512], in_=ps[:, :512], func=AF.Copy),
                         reads=[pst], writes=[stg_tl[i]])
                    self.dma(EXd["v"][:, blk, :], stg[i][:, :512], reads=[stg_tl[i]],
                             writes=[self.dT(("v", t0, tb))], eng="pool")
                if not full:
                    continue
                qblk = (t0 - QOFF) // 128 + tb
                ps, pst = self.bank()
                for kc in range(DC):
                    P.op("pe", lambda e, ps=ps, kc=kc, hb=hb: e.matmul(ps[:, :16], lhsT=hb(kc), rhs=wiwv[:, kc, :],
                                                                      start=(kc == 0), stop=(kc == DC - 1)),
                         reads=[Htl[kc], wiw_tl], writes=[pst])
                P.op("act", lambda e, ps=ps: e.activation(out=iwst[:, :16], in_=ps[:, :16], func=AF.Copy,
                                                          scale=1.0 / 32.0),
                     reads=[pst], writes=[iwst_tl])
                self.dma(D_["iw"][:, qblk, :], iwst[:, :16], reads=[iwst_tl], writes=[self.dT(("iw", qblk))], eng="pool")

    def exchange(self):
        own = self.nt - PRE
        D_ = self.dsa
        EX = D_["EX"]
        gath = self.dscr("gath", [256, 9 * own], BF16)
        D_["low"] = gath[0:128, :]
        rd = [self.dT(("kT", g, t0)) for g in range(4) for (t0, TT) in self.tiles[1:]]
        rd += [self.dT(("v", b)) for b in range(1, own // 128 + 1)]
        rd += [self.dT(("ikT", t0)) for (t0, TT) in self.tiles[1:]]
        groups = [[2 * i, 2 * i + 1] for i in range(self.ncores // 2)]
        self.P.op("pool", lambda e: e.collective_compute("AllGather", op=ALU.bypass, replica_groups=groups,
                                                         ins=[EX[:, :]], outs=[gath[:, :]]),
                  reads=rd, writes=[self.dT(("low",))], dma=True)

    def dsa_attn(self, nsteps=20):
        P = self.P
        D_ = self.dsa
        nqb = self.nt // 128
        own = self.nt - PRE
        nob = own // 128
        LOWK = D_["low"][:, 0:4 * own].rearrange("p (g t) -> p g t", g=4)
        LOWV = D_["low"][:, 4 * own:8 * own].rearrange("p (b c) -> p b c", b=nob)
        OWNK = D_["kT"]
        OWNV = D_["v"]
        self.arena_reset(self.const_end)
        ikT = self.alloc(8192 * 2, BF16)
        ikT_tl = Tl("ikT")
        self.dma(ikT[:, 0:own], D_["low"][:, 8 * own:9 * own], writes=[ikT_tl])
        self.dma(ikT[:, own:2 * own], D_["ikT"][:, :], writes=[ikT_tl])
        kval = self.alloc(8192 * 2, BF16)
        kval_tl = Tl("kval")
        self.dma(kval[:, :2 * own], self.dram["kvalid"][:, :], writes=[kval_tl])
        tri = self.alloc(128 * 2, BF16)
        tri_tl = Tl("tri")
        self.dma(tri, self.dram["tri"][:, :], writes=[tri_tl])
        score = self.alloc(8192 * 4)
        score_tl = Tl("score")
        M = self.alloc(8192 * 2, BF16)
        M_tl = Tl("M")
        MT2 = [self.alloc(8192 * 2, BF16).rearrange("p (b q) -> p b q", b=64) for _ in range(2)]
        MT2_tl = [Tl("MT0"), Tl("MT1")]
        QT2 = [self.alloc(16 * 128 * 2, BF16).rearrange("p (h q) -> p h q", h=16) for _ in range(2)]
        QT2_tl = [Tl("QT0"), Tl("QT1")]
        NR = 4
        Rr = [self.alloc(512 * 2, BF16) for _ in range(NR)]
        Rr_tl = [Tl(f"R{i}") for i in range(NR)]
        NE = 4
        ETr = [self.alloc(512 * 2, BF16) for _ in range(NE)]
        ETr_tl = [Tl(f"ET{i}") for i in range(NE)]
        PTr = [self.alloc(512 * 2, BF16) for _ in range(NE)]
        PTr_tl = [Tl(f"PT{i}") for i in range(NE)]
        Kr = [self.alloc(512 * 2, BF16) for _ in range(4)]
        Kr_tl = [Tl(f"K{i}") for i in range(4)]
        Vr = [self.alloc(512 * 2, BF16).rearrange("p (b d) -> p b d", b=4) for _ in range(4)]
        Vr_tl = [Tl(f"V{i}") for i in range(4)]
        IQ = self.alloc(8 * 128 * 2, BF16).rearrange("p (h q) -> p h q", h=8)
        IQ_tl = Tl("IQ")
        IW = self.alloc(16 * 4)
        IW_tl = Tl("IW")
        diag = self.alloc(16 * 128 * 2, BF16).rearrange("p (h q) -> p h q", h=16)
        diag_tl = [Tl(f"diag{h}") for h in range(16)]
        osb = [self.alloc(512 * 4) for _ in range(2)]
        osb_tl = [Tl("osb0"), Tl("osb1")]
        lnd = [self.alloc(512 * 4) for _ in range(2)]
        lnd_tl = [Tl("lnd0"), Tl("lnd1")]
        OTs = [self.alloc(512 * 2, BF16) for _ in range(2)]
        OTs_tl = [Tl("OT0"), Tl("OT1")]
        sm = self.alloc(64 * 4)
        lo, hi, mid, cnt, u, rng = (sm[:, i:i + 1] for i in range(6))
        hsx = sm[:, 8:8 + nsteps + 2]
        sm_tl = {k: Tl("sm_" + k) for k in ("lo", "hi", "mid", "cnt", "u", "rng", "hsx")}
        pw = self.smalls[:, 132:132 + nsteps + 2]
        B_IDX = (0, 1, 2)
        B_SC = 3
        B_ST = (4, 5)
        B_O = 6
        B_DEN = 7
        rk = [0, 0, 0, 0, 0, 0]

        def s1a(qb):
            c0 = qb * 128
            nkb = nob + qb
            nk = nkb * 128
            qc = 512 - PRE + c0
            QT, QT_tl = QT2[qb % 2], QT2_tl[qb % 2]
            self.dma(QT[:, :, :], D_["qT"][:, :, qc:qc + 128], writes=[QT_tl])
            self.dma(IQ[:, :, :], D_["iqT"][:, :, qc:qc + 128], writes=[IQ_tl])
            self.dma(IW[:, :], D_["iw"][:, qc // 128, :], writes=[IW_tl])
            for h in range(16):
                P.op("pool", lambda e, h=h: e.tensor_scalar(out=diag[:, h, :], in0=self.ident, scalar1=IW[:, h:h + 1],
                                                            scalar2=None, op0=ALU.mult),
                     reads=[IW_tl, self.cbf_tl], writes=[diag_tl[h]])
            for k0 in range(0, nk, 512):
                w = min(512, nk - k0)
                scp, scpt = self.bank(B_SC)
                for h in range(16):
                    ps, pst = self.bank(B_IDX[rk[5] % 3])
                    rk[5] += 1
                    hp = 64 * (h % 2)
                    P.op("pe", lambda e, ps=ps, h=h, hp=hp, k0=k0, w=w: e.matmul(
                        ps[:, :w], lhsT=IQ[hp:hp + 64, h // 2, :], rhs=ikT[hp:hp + 64, k0:k0 + w], start=True, stop=True),
                        reads=[IQ_tl, ikT_tl], writes=[pst])
                    ri = rk[0] % NR
                    rk[0] += 1
                    if h % 2 == 0:
                        P.op("act", lambda e, ps=ps, ri=ri, w=w: e.activation(out=Rr[ri][:, :w], in_=ps[:, :w], func=AF.Relu),
                             reads=[pst], writes=[Rr_tl[ri]])
                    else:
                        P.op("dve", lambda e, ps=ps, ri=ri, w=w: e.tensor_scalar(out=Rr[ri][:, :w], in0=ps[:, :w], scalar1=0.0,
                                                                                 scalar2=None, op0=ALU.max),
                             reads=[pst], writes=[Rr_tl[ri]])
                    P.op("pe", lambda e, scp=scp, h=h, ri=ri, w=w: e.matmul(
                        scp[:, :w], lhsT=diag[:, h, :], rhs=Rr[ri][:, :w], start=(h == 0), stop=(h == 15)),
                        reads=[diag_tl[h], Rr_tl[ri]], writes=[scpt])
                P.op("act", lambda e, scp=scp, k0=k0, w=w: e.activation(out=score[:, k0:k0 + w], in_=scp[:, :w], func=AF.Copy),
                     reads=[scpt], writes=[score_tl])

        def s1b(qb):
            nkb = nob + qb
            nk = nkb * 128
            P.op("dve", lambda e: e.tensor_reduce(out=lo, in_=score[:, :nk], axis=mybir.AxisListType.X, op=ALU.min),
                 reads=[score_tl], writes=[sm_tl["lo"]])
            P.op("dve", lambda e: e.tensor_reduce(out=hi, in_=score[:, :nk], axis=mybir.AxisListType.X, op=ALU.max),
                 reads=[score_tl], writes=[sm_tl["hi"]])
            P.op("dve", lambda e: e.tensor_tensor(out=score[:, :nk], in0=score[:, :nk], in1=kval[:, :nk], op=ALU.add),
                 reads=[score_tl, kval_tl], writes=[score_tl])
            P.op("dve", lambda e: e.tensor_tensor(out=score[:, nk - 128:nk], in0=score[:, nk - 128:nk], in1=tri[:, :],
                                                  op=ALU.add),
                 reads=[score_tl, tri_tl], writes=[score_tl])
            P.op("dve", lambda e: e.tensor_tensor(out=rng, in0=hi, in1=lo, op=ALU.subtract),
                 reads=[sm_tl["lo"], sm_tl["hi"]], writes=[sm_tl["rng"]])
            P.op("dve", lambda e: e.tensor_scalar(out=hsx, in0=pw, scalar1=rng, scalar2=None, op0=ALU.mult),
                 reads=[sm_tl["rng"], self.smalls_tl], writes=[sm_tl["hsx"]])
            P.op("dve", lambda e: e.tensor_tensor(out=mid, in0=lo, in1=hsx[:, 1:2], op=ALU.add),
                 reads=[sm_tl["lo"], sm_tl["hsx"]], writes=[sm_tl["mid"]])
            for s in range(nsteps):
                P.op("dve", lambda e: e.tensor_scalar(
                    out=M[:, :nk], in0=score[:, :nk], scalar1=mid, scalar2=0.0, op0=ALU.is_ge, op1=ALU.add,
                    accum_out=cnt),
                    reads=[score_tl, sm_tl["mid"]], writes=[M_tl, sm_tl["cnt"]])
                last = (s == nsteps - 1)
                ha = hsx[:, s + 1:s + 2] if not last else hsx[:, s + 1:s + 2]
                hb = hsx[:, s + 2:s + 3] if not last else hsx[:, s + 1:s + 2]
                P.op("dve", lambda e, ha=ha: e.scalar_tensor_tensor(out=u, in0=cnt, scalar=255.5, in1=ha,
                                                                    op0=ALU.is_ge, op1=ALU.mult),
                     reads=[sm_tl["cnt"], sm_tl["hsx"]], writes=[sm_tl["u"]])
                P.op("dve", lambda e, hb=hb: e.scalar_tensor_tensor(out=mid, in0=mid, scalar=hb, in1=u,
                                                                    op0=ALU.subtract, op1=ALU.add),
                     reads=[sm_tl["mid"], sm_tl["u"], sm_tl["hsx"]], writes=[sm_tl["mid"]])
            P.op("dve", lambda e: e.tensor_scalar(out=M[:, :nk], in0=score[:, :nk], scalar1=mid, scalar2=None, op0=ALU.is_ge),
                 reads=[score_tl, sm_tl["mid"]], writes=[M_tl])

        def s1c(qb):
            nkb = nob + qb
            MT, MT_tl = MT2[qb % 2], MT2_tl[qb % 2]
            for kb0 in range(0, nkb, 4):
                nb = min(4, nkb - kb0)
                trp, trpt = self.bank(B_SC)
                trb = trp.bitcast(BF16)
                for j in range(nb):
                    kb = kb0 + j
                    P.op("pe", lambda e, trb=trb, j=j, kb=kb: e.transpose(
                        out=trb[:, j * 128:(j + 1) * 128], in_=M[:, kb * 128:(kb + 1) * 128], identity=self.ident),
                        reads=[M_tl, self.cbf_tl], writes=[trpt])
                P.op("act", lambda e, trb=trb, kb0=kb0, nb=nb: e.activation(
                    out=MT[:, kb0:kb0 + nb, :], in_=trb[:, :nb * 128].rearrange("p (b q) -> p b q", b=nb), func=AF.Copy),
                    reads=[trpt], writes=[MT_tl])

        def s2(qb):
            c0 = qb * 128
            nkb = nob + qb
            nk = nkb * 128
            QT, QT_tl = QT2[qb % 2], QT2_tl[qb % 2]
            MT, MT_tl = MT2[qb % 2], MT2_tl[qb % 2]
            for g in range(4):
                ops_, opst = self.bank(B_O)
                dps, dpst = self.bank(B_DEN)
                for k0 in range(0, nk, 512):
                    w = min(512, nk - k0)
                    nb = w // 128
                    ki = rk[1] % 4
                    rk[1] += 1
                    if k0 < own:
                        ksrc = LOWK[:, g, k0:k0 + w]
                        vsrc = LOWV[:, k0 // 128:k0 // 128 + nb, g * 128:(g + 1) * 128]
                    else:
                        ksrc = OWNK[:, g, k0 - own:k0 - own + w]
                        vsrc = OWNV[:, (k0 - own) // 128:(k0 - own) // 128 + nb, g * 128:(g + 1) * 128]
                    self.dma(Kr[ki][:, :w], ksrc, writes=[Kr_tl[ki]])
                    self.dma(Vr[ki][:, :nb, :], vsrc, writes=[Vr_tl[ki]])
                    for j in range(nb):
                        kb = k0 // 128 + j
                        stp, stpt = self.bank(B_ST[rk[2] % 2])
                        rk[2] += 1
                        P.op("pe", lambda e, stp=stp, ki=ki, j=j, g=g: e.matmul(
                            stp[:, :], lhsT=Kr[ki][:, j * 128:(j + 1) * 128],
                            rhs=QT[:, 4 * g:4 * g + 4, :], start=True, stop=True),
                            reads=[Kr_tl[ki], QT_tl], writes=[stpt])
                        ei = rk[3] % NE
                        rk[3] += 1
                        P.op("act", lambda e, stp=stp, ei=ei: e.activation(out=ETr[ei][:, :], in_=stp[:, :], func=AF.Exp,
                                                                            scale=float(128 ** -0.5)),
                             reads=[stpt], writes=[ETr_tl[ei]])
                        P.op("pool", lambda e, ei=ei, kb=kb: e.tensor_tensor(
                            out=PTr[ei][:, :].rearrange("p (r q) -> p r q", r=4),
                            in0=ETr[ei][:, :].rearrange("p (r q) -> p r q", r=4),
                            in1=MT[:, kb, :].unsqueeze(1).to_broadcast([128, 4, 128]), op=ALU.mult),
                            reads=[ETr_tl[ei], MT_tl], writes=[PTr_tl[ei]])
                        P.op("pe", lambda e, ops_=ops_, ki=ki, j=j, ei=ei, kb=kb: e.matmul(
                            ops_[:, :], lhsT=Vr[ki][:, j, :], rhs=PTr[ei][:, :], start=(kb == 0), stop=(kb == nkb - 1)),
                            reads=[Vr_tl[ki], PTr_tl[ei]], writes=[opst])
                        P.op("pe", lambda e, dps=dps, ei=ei, kb=kb: e.matmul(
                            dps[:, :], lhsT=self.ones, rhs=PTr[ei][:, :], start=(kb == 0), stop=(kb == nkb - 1)),
                            reads=[self.cbf_tl, PTr_tl[ei]], writes=[dpst])
                oi = rk[4] % 2
                rk[4] += 1
                P.op("act", lambda e, dps=dps, oi=oi: e.activation(out=lnd[oi][:, :], in_=dps[:, :], func=AF.Ln,
                                                                   bias=self.tinyc, scale=1.0),
                     reads=[dpst, self.smalls_tl], writes=[lnd_tl[oi]])
                P.op("act", lambda e, oi=oi: e.activation(out=lnd[oi][:, :], in_=lnd[oi][:, :], func=AF.Exp, scale=-1.0),
                     reads=[lnd_tl[oi]], writes=[lnd_tl[oi]])
                P.op("act", lambda e, ops_=ops_, oi=oi: e.activation(out=osb[oi][:, :], in_=ops_[:, :], func=AF.Copy),
                     reads=[opst], writes=[osb_tl[oi]])
                P.op("pool", lambda e, oi=oi: e.tensor_tensor(out=OTs[oi][:, :], in0=osb[oi][:, :], in1=lnd[oi][:, :],
                                                              op=ALU.mult),
                     reads=[osb_tl[oi], lnd_tl[oi]], writes=[OTs_tl[oi]])
                self.dma(D_["oT"][:, 4 * g:4 * g + 4, c0:c0 + 128], OTs[oi][:, :].rearrange("p (r q) -> p r q", r=4),
                         reads=[OTs_tl[oi]], writes=[self.dT(("oT", g, qb))], eng="pool")

        s1a(0)
        s1b(0)
        s1c(0)
        for qb in range(nqb):
            if qb + 1 < nqb:
                s1a(qb + 1)
                s1b(qb + 1)
            s2(qb)
            if qb + 1 < nqb:
                s1c(qb + 1)

    def dsa_out(self, t0, TT):
        P = self.P
        X, Xtl, H, Htl = self.X, self.X_tl, self.H, self.H_tl
        rd = [self.dT(("oT", g, qb)) for g in range(4) for qb in range(t0 // 128, (t0 + TT) // 128)]
        self.dma(H[:, :, HALO:HALO + TT], self.dsa["oT"][:, :, t0:t0 + TT], reads=rd, writes=Htl, eng="pool")
        ins = [H[:, c, HALO:HALO + TT] for c in range(DC)]

        def ep(n, ps, pst):
            P.op("dve", lambda e: e.tensor_tensor(out=X[:, n, :TT], in0=X[:, n, :TT], in1=ps, op=ALU.add),
                 reads=[pst, Xtl[n]], writes=[Xtl[n]])

        self.linear("wout", DC, DC, 4, ins, Htl, TT, ep)


def vec_layout():
    VJ = {}
    j = 0
    for name, n in (("norm_mix", 4), ("norm_mlp", 4), ("pool_scale", 2), ("b_pw1a", 1), ("b_pw1g", 1),
                    ("b_dw", 1), ("ln_g", 1), ("ln_b", 1), ("b_pw2", 1), ("norm_final", 1)):
        VJ[name] = j
        j += n
    return VJ, j


def fm(v):
    return np.ascontiguousarray(np.asarray(v, np.float32).reshape(16, 128).T)


def weight_specs(layers):
    specs = []
    for l in layers:
        k, j = l % 3, l // 3
        if k == 0:
            specs.append((f"pool{j}", 1, 8192))
        elif k == 2:
            specs.append(("pw1", 16, 4096))
            specs.append(("pw2", 4, 8192))
        specs.append((f"up{l}", 16, 8192))
        specs.append((f"dn{l}", 16, 8192))
    return specs


def dsa_tensors(B, mode):
    nt = B.nt
    own = nt - PRE
    nq = own + 512

    def exv(EX):
        return {"kT": EX[:, 0:4 * own].rearrange("p (g t) -> p g t", g=4),
                "v": EX[:, 4 * own:8 * own].rearrange("p (b c) -> p b c", b=own // 128),
                "ikT": EX[:, 8 * own:9 * own]}
    EX = B.dscr("EX", [128, 9 * own], BF16)
    EXL = B.dscr("EXL", [128, 9 * own], BF16)
    D_ = {"EX": EX, "low": EXL, "lowv": exv(EXL)}
    D_.update(exv(EX))
    D_["qT"] = B.dscr("qT", [128, 16, nq], BF16)
    D_["iqT"] = B.dscr("iqT", [128, 8, nq], BF16)
    D_["iw"] = B.dscr("iw", [128, nq // 128, 16], F32)
    D_["oT"] = B.dscr("oT", [128, 16, nt], BF16)
    B.din("kvalid", [128, 2 * own], BF16)
    B.din("tri", [128, 128], BF16)
    for nm in ("cos128", "sin128", "cos64", "sin64"):
        B.din(nm, [128, 2 * own])
    return D_


def mode_layers(mode):
    return {"A": (0, 1), "B": (1, 2, 3), "fused": (0, 1, 2, 3)}[mode]


def mode_specs(mode):
    specs = []
    if mode in ("A", "fused"):
        specs += [("pool0", 1, 8192), ("up0", 16, 8192), ("dn0", 16, 8192),
                  ("win", 29, 4096), ("wv", 1, 8192), ("wiw", 1, 256)]
    if mode in ("B", "fused"):
        specs += [("wout", 4, 8192), ("up1", 16, 8192), ("dn1", 16, 8192),
                  ("pw1", 16, 4096), ("pw2", 4, 8192), ("up2", 16, 8192), ("dn2", 16, 8192),
                  ("pool1", 1, 8192), ("up3", 16, 8192), ("dn3", 16, 8192)]
    return specs


def build_program(mode="fused", nt=NT, upto=3, final=True, ncores=NCORES):
    B = Builder(nt=nt, mode=mode, ncores=ncores)
    own = B.own
    B.arena_init()
    B.VJ, nv = vec_layout()
    B.NVEC = nv * 16
    B.wscr = {}
    B.load_consts()
    B.dsa = dsa_tensors(B, mode)
    xin = B.din("xin", [128, DC, 2 * own])
    x1 = B.dscr("x1", [128, DC, 2 * own], F32)
    xs = [B.dscr("xsA", [128, DC, nt], F32), B.dscr("xsB", [128, DC, nt], F32)]
    out = B.dout("out", [128, DC, own])
    B.cast_weights(mode_specs("fused"))
    B.barrier()
    B.tile_bufs()
    for ti, (t0, TT) in enumerate(B.tiles0):
        B.load_x(xin, t0, TT)
        icol = 66 if ti == 0 else (1 if t0 == own else None)
        B.pool_mixer(0, 0, ti, TT, icol=icol)
        B.mlp(0, TT)
        B.store_x(x1, t0, TT)
    B.dsa_proj(x1)
    B.barrier()
    B.dsa_attn()
    B.barrier()
    B.tile_bufs()

    def layer_loop(l, src, dst, last, off=0):
        kind = l % 3
        for ti, (t0, TT) in enumerate(B.tiles):
            B.load_x(src, t0, TT, off=off)
            if ti == 0:
                B.mask_pre(B.X[:, :, :TT], B.X_tl)
            if kind == 0:
                B.pool_mixer(l, l // 3, ti, TT, icol=(1 if ti == 1 else None))
            elif kind == 2:
                B.conv_mixer(l, ti, TT)
            else:
                B.dsa_out(t0, TT)
            B.mlp(l, TT)
            if last:
                if ti == 0:
                    continue
                if final:
                    B.final_norm(TT)
                B.store_x(dst, t0, TT, c0=PRE)
            else:
                B.store_x(dst, t0, TT)

    seq = [(1, x1, xs[0], own - PRE), (2, xs[0], xs[1], 0), (3, xs[1], out, 0)]
    seq = [s for s in seq if s[0] <= upto]
    for i, (l, s, d, off) in enumerate(seq):
        lastl = (i == len(seq) - 1)
        layer_loop(l, s, out if lastl else d, lastl, off=off)
    B.P.emit()
    return B


def prep_common(inp, layers):
    VJ, nv = vec_layout()
    vecs = np.zeros((128, nv * 16), np.float32)

    def put(name, idx, v):
        j = VJ[name] + idx
        vecs[:, j * 16:(j + 1) * 16] = fm(v)

    for i in range(4):
        put("norm_mix", i, inp["norm_mix"][i])
        put("norm_mlp", i, inp["norm_mlp"][i])
    for i in range(2):
        put("pool_scale", i, inp["pool_scale"][i])
    put("b_pw1a", 0, inp["conv_b_pw1"][0][:D])
    put("b_pw1g", 0, inp["conv_b_pw1"][0][D:])
    put("b_dw", 0, inp["conv_b_dw"][0])
    put("ln_g", 0, inp["conv_ln_g"][0])
    put("ln_b", 0, inp["conv_ln_b"][0])
    put("b_pw2", 0, inp["conv_b_pw2"][0])
    put("norm_final", 0, inp["norm_final"])
    cbf = np.zeros((128, 384), np.float32)
    cbf[:, 0:128] = np.eye(128, dtype=np.float32)
    cbf[:, 128:256] = 1.0 / 2048.0
    cbf[:, 256:384] = 1.0
    cbf = cbf.astype(ml_dtypes.bfloat16)
    wdw = np.asarray(inp["conv_w_dw"][0], np.float32)
    convw = np.ascontiguousarray(wdw.reshape(CONVW, 16, 128).transpose(2, 1, 0)).reshape(128, 16 * CONVW)
    com = {"vecs": vecs, "cbf": cbf, "convw": convw}
    for l in layers:
        k, j = l % 3, l // 3
        if k == 0:
            w = np.asarray(inp["pool_w"][j], np.float32)
            com[f"pool{j}_f"] = np.ascontiguousarray(
                w.reshape(4, 4, 128, 512).transpose(2, 0, 1, 3)).reshape(128, 8192)
        elif k == 2:
            w = np.asarray(inp["conv_w_pw1"][0], np.float32)
            w4 = w.reshape(16, 128, 2, 16, 128)
            com["pw1_f"] = np.ascontiguousarray(w4.transpose(1, 3, 0, 2, 4)).reshape(128, 16 * 4096)
            w = np.asarray(inp["conv_w_pw2"][0], np.float32)
            com["pw2_f"] = np.ascontiguousarray(
                w.reshape(16, 128, 4, 512).transpose(1, 2, 0, 3)).reshape(128, 4 * 8192)
        w = np.asarray(inp["mlp_up"][l], np.float32)
        com[f"up{l}_f"] = np.ascontiguousarray(
            w.reshape(16, 128, 16, 512).transpose(1, 2, 0, 3)).reshape(128, 16 * 8192)
        w = np.asarray(inp["mlp_down"][l], np.float32)
        com[f"dn{l}_f"] = np.ascontiguousarray(
            w.reshape(64, 128, 16, 128).transpose(1, 2, 0, 3)).reshape(128, 16 * 8192)
    return com


def core_tokens(x, k, nt=NT):
    b, half = k // 2, k % 2
    own = np.asarray(x[b, half * OWN: half * OWN + (nt - PRE)], np.float32)
    if half == 1:
        pre = np.asarray(x[b, half * OWN - PRE: half * OWN], np.float32)
    else:
        pre = np.zeros((PRE, D), np.float32)
    return np.concatenate([pre, own], axis=0)


def to_fm(a):
    n = a.shape[0]
    return np.ascontiguousarray(a.T.reshape(16, 128, n).transpose(1, 0, 2))


def from_fm(a):
    n = a.shape[2]
    return np.ascontiguousarray(a.transpose(1, 0, 2).reshape(2048, n).T)


def smalls_for(k):
    half = k % 2
    s = np.zeros((128, 192), np.float32)
    s[:, 0] = float(half)
    s[:, 65] = EPS
    s[:, 130] = 1e-18
    for j in range(40):
        s[:, 132 + j] = 2.0 ** (-j)
    for g, w in enumerate(POOL_W):
        for t in range(16):
            start = 1.0 / min(t + 1, w)
            s[:, 1 + 16 * g + t] = start if half == 0 else 1.0 / w
            s[:, 66 + 16 * g + t] = start
    return s


def rope_tables(k, nt=NT):
    half = k % 2
    own = nt - PRE
    pos = (np.arange(2 * own) - (1 - half) * own).astype(np.float32)
    out = {}
    for nm, d in (("128", 128), ("64", 64)):
        inv = (np.float32(10000.0) ** (-np.arange(0, d, 2, dtype=np.float32) / np.float32(d))).astype(np.float32)
        ang = (pos[:, None] * inv[None, :]).astype(np.float32)
        cos = np.cos(ang).astype(np.float32)
        sin = np.sin(ang).astype(np.float32)
        dd = np.arange(128) % d
        idx = dd % (d // 2)
        sign = np.where(dd < d // 2, -1.0, 1.0).astype(np.float32)
        out["cos" + nm] = np.ascontiguousarray(cos[:, idx].T)
        out["sin" + nm] = np.ascontiguousarray((sin[:, idx] * sign[None, :]).T)
    return out


def dsa_weights(inp):
    w = np.asarray(inp["dsa_w_in"][0], np.float32)
    cols = []
    d = np.arange(128)
    for h in range(16):
        cols.append(h * 128 + d)
        cols.append(h * 128 + (d + 64) % 128)
    for g in range(4):
        cols.append(2048 + g * 128 + d)
        cols.append(2048 + g * 128 + (d + 64) % 128)
    for m in range(8):
        head = 2 * m + d // 64
        dd = d % 64
        cols.append(3072 + head * 64 + dd)
        cols.append(3072 + head * 64 + (dd + 32) % 64)
    dd = d % 64
    cols.append(4096 + dd)
    cols.append(4096 + (dd + 32) % 64)
    cols = np.concatenate(cols)
    wp = w[:, cols]
    win = np.ascontiguousarray(wp.reshape(16, 128, 29, 256).transpose(1, 2, 0, 3)).reshape(128, 29 * 4096)
    wv = np.ascontiguousarray(w[:, 2560:3072].reshape(16, 128, 512).transpose(1, 0, 2)).reshape(128, 8192)
    wiw = np.ascontiguousarray(w[:, 4160:4176].reshape(16, 128, 16).transpose(1, 0, 2)).reshape(128, 256)
    wo = np.asarray(inp["dsa_w_out"][0], np.float32)
    wout = np.ascontiguousarray(wo.reshape(16, 128, 4, 512).transpose(1, 2, 0, 3)).reshape(128, 4 * 8192)
    return {"win_f": win, "wv_f": wv, "wiw_f": wiw, "wout_f": wout}


def attn_masks(k, nt=NT):
    half = k % 2
    own = nt - PRE
    kv = np.zeros((128, 2 * own), np.float32)
    if half == 0:
        kv[:, :own] = NEG
    q = np.arange(128)[:, None]
    s = np.arange(128)[None, :]
    tri = np.where(s <= q, 0.0, NEG).astype(np.float32)
    return {"kvalid": kv.astype(ml_dtypes.bfloat16), "tri": tri.astype(ml_dtypes.bfloat16)}


def kernel(**inp):
    inp = {k: np.asarray(v) for k, v in inp.items()}
    com = prep_common(inp, (0, 1, 2, 3))
    com.update(dsa_weights(inp))
    x = inp["x"]
    maps = []
    for k in range(NCORES):
        b, half = k // 2, k % 2
        m = dict(com)
        seq = np.asarray(x[b], np.float32)
        if half == 0:
            seq = np.concatenate([np.zeros((OWN, D), np.float32), seq[:OWN]], axis=0)
        m["xin"] = to_fm(seq)
        m["smalls"] = smalls_for(k)
        m.update(rope_tables(k))
        m.update(attn_masks(k))
        maps.append(m)
    cores = list(range(NCORES))
    BF = build_program("fused")
    res = run_bass_kernel_spmd(BF.nc, [{n: m[n] for n in BF.in_names} for m in maps], core_ids=cores)
    out = np.empty((4, SEQ, D), np.float32)
    for k in cores:
        b, half = k // 2, k % 2
        out[b, half * OWN:(half + 1) * OWN] = from_fm(np.asarray(res.results[k]["out"]))
    return out
```

```python
import numpy as np
import ml_dtypes
import concourse.bass as bass
import concourse.mybir as mybir
from concourse.bass_utils import run_bass_kernel_spmd

F32 = mybir.dt.float32
BF16 = mybir.dt.bfloat16
AF = mybir.ActivationFunctionType
ALU = mybir.AluOpType

D = 2048
DC = 16
DFF = 8192
FC = 64
SEQ = 8192
OWN = 4096
PRE = 128
NT = OWN + PRE
HALO = 32
EPS = 1e-6
NEG = -1.0e30
POOL_W = (2, 4, 8, 16)
CONVW = 31
NCORES = 8


class Tl:
    __slots__ = ("name", "last_w", "readers")

    def __init__(self, name=""):
        self.name = name
        self.last_w = None
        self.readers = {}


class Op:
    __slots__ = ("eng", "fn", "deps", "dma", "idx")


ENG_BLOCK = {"pe": "tensor", "act": "scalar", "dve": "vector", "pool": "gpsimd", "sp": "sync"}


class Prog:
    NDS = 8

    def __init__(self, nc):
        self.nc = nc
        self.ops = []
        self.dma_since_barrier = []

    def op(self, eng, fn, reads=(), writes=(), dma=False, extra_deps=()):
        idx = len(self.ops)
        deps = set(extra_deps)
        for t in reads:
            if t.last_w is not None:
                deps.add(t.last_w)
        for t in writes:
            if t.last_w is not None:
                deps.add(t.last_w)
            deps.update(t.readers.values())
        key = ("d", idx) if dma else eng
        for t in reads:
            t.readers[key] = idx
        for t in writes:
            t.last_w = idx
            t.readers = {}
        deps.discard(idx)
        o = Op()
        o.eng = eng
        o.fn = fn
        o.dma = dma
        o.idx = idx
        ops = self.ops
        if eng == "pe" and not dma:
            o.deps = [d for d in deps if not (ops[d].eng == "pe" and not ops[d].dma)]
        else:
            o.deps = list(deps)
        ops.append(o)
        if dma:
            self.dma_since_barrier.append(idx)
        return idx

    def emit(self):
        nc = self.nc
        ops = self.ops
        need = [False] * len(ops)
        for o in ops:
            for d in o.deps:
                need[d] = True
        engs = []
        for o in ops:
            if o.eng not in engs:
                engs.append(o.eng)
        csem = {}
        ccnt = {}
        dsem = {}
        dcnt = {}
        drr = {}
        for e in engs:
            csem[e] = nc.alloc_semaphore(name=f"c_{e}")
            ccnt[e] = 0
            dsem[e] = [nc.alloc_semaphore(name=f"d_{e}_{i}") for i in range(self.NDS)]
            drr[e] = 0
            for i in range(self.NDS):
                dcnt[(e, i)] = 0
        sig = {}
        pre_wait = {}
        for o in ops:
            if o.dma:
                i = drr[o.eng]
                drr[o.eng] = (i + 1) % self.NDS
                prev = dcnt[(o.eng, i)]
                pre_wait[o.idx] = (dsem[o.eng][i], prev)
                dcnt[(o.eng, i)] = prev + 16
                sig[o.idx] = (dsem[o.eng][i], prev + 16)
            elif need[o.idx]:
                ccnt[o.eng] += 1
                sig[o.idx] = (csem[o.eng], ccnt[o.eng])
        self.max_counts = dict(ccnt)
        final_dma = [(dsem[e][i], dcnt[(e, i)]) for e in engs for i in range(self.NDS) if dcnt[(e, i)] > 0]
        with nc.Block() as block:
            for e in engs:
                my = [o for o in ops if o.eng == e]

                def body(eng, my=my, e=e):
                    waited = {}

                    def wait(sem, val):
                        if val <= 0:
                            return
                        k = sem.num
                        if waited.get(k, 0) >= val:
                            return
                        eng.wait_ge(sem, val)
                        waited[k] = val

                    for o in my:
                        for d in sorted(o.deps):
                            wait(*sig[d])
                        if o.dma:
                            wait(*pre_wait[o.idx])
                        ins = o.fn(eng)
                        if o.idx in sig:
                            ins.then_inc(sig[o.idx][0], 16 if o.dma else 1)
                    if e == "sp":
                        for s, v in final_dma:
                            wait(s, v)

                getattr(block, ENG_BLOCK[e])(body)


def make_tiles(nt):
    tiles = [(0, PRE)]
    t = PRE
    while t < nt:
        tiles.append((t, 512))
        t += 512
    return tiles


class Builder:
    def __init__(self, nt=NT, layers=(0, 1, 2, 3), mode="full", ncores=NCORES):
        self.ncores = ncores
        self.nt = nt
        self.tiles = make_tiles(nt)
        self.own = nt - PRE
        self.tiles0 = [(512 * i, 512) for i in range(2 * self.own // 512)]
        self.layers = layers
        self.mode = mode
        self.nc = bass.Bass("TRN2", target_bir_lowering=False)
        self.P = Prog(self.nc)
        self.dram = {}
        self.dtl = {}

    def din(self, name, shape, dt=F32):
        t = self.nc.dram_tensor(name, list(shape), dt, kind="ExternalInput")
        self.dram[name] = t
        if not hasattr(self, "in_names"):
            self.in_names = []
        self.in_names.append(name)
        return t

    def dout(self, name, shape, dt=F32):
        t = self.nc.dram_tensor(name, list(shape), dt, kind="ExternalOutput")
        self.dram[name] = t
        return t

    def dscr(self, name, shape, dt):
        t = self.nc.dram_tensor(name, list(shape), dt)
        self.dram[name] = t
        return t

    def dT(self, key):
        if key not in self.dtl:
            self.dtl[key] = Tl(str(key))
        return self.dtl[key]

    def arena_init(self):
        nc = self.nc
        self.ARENA_W = 52800
        self.arena = nc.alloc_sbuf_tensor("arena", [128, self.ARENA_W], F32)
        self.arena_off = 0
        self.psum = nc.alloc_psum_tensor("psum", [128, 4096], F32)
        self.pbank = [Tl(f"bank{i}") for i in range(8)]
        self.pb_rr = 0

    def arena_reset(self, keep=0):
        self.arena_off = keep

    def alloc(self, nbytes, dt=F32):
        words = (nbytes + 3) // 4
        words = (words + 7) // 8 * 8
        a = self.arena[:, self.arena_off:self.arena_off + words]
        self.arena_off += words
        assert self.arena_off <= self.ARENA_W, ("SBUF arena overflow", self.arena_off * 4)
        if dt == BF16:
            a = a.bitcast(BF16)
        return a

    def bank(self, i=None):
        if i is None:
            i = self.pb_rr
            self.pb_rr = (self.pb_rr + 1) % 8
        return self.psum[:, i * 512:(i + 1) * 512], self.pbank[i]

    def barrier(self):
        P = self.P
        nc = self.nc
        if not hasattr(self, "bar_sb"):
            self.bar_sb = nc.alloc_sbuf_tensor("bar_sb", [128, 64], F32)
            self.bar_tl = {e: Tl("bar_" + e) for e in ("act", "dve", "pool", "pe")}
        sb = self.bar_sb
        tl = self.bar_tl
        dmas = list(P.dma_since_barrier)
        P.dma_since_barrier = []
        i1 = P.op("act", lambda e: e.activation(out=sb[:, 0:8], in_=sb[:, 32:40], func=AF.Copy),
                  writes=[tl["act"]], extra_deps=dmas)
        i2 = P.op("dve", lambda e: e.memset(sb[:, 8:16], 0.0), writes=[tl["dve"]], extra_deps=dmas)
        i3 = P.op("pool", lambda e: e.memset(sb[:, 16:24], 0.0), writes=[tl["pool"]], extra_deps=dmas)
        pb, pbt = self.bank(7)
        i4 = P.op("pe", lambda e: e.matmul(pb[:, 0:8], lhsT=self.ident[:, 0:128], rhs=self.ident[:, 0:8],
                                           start=True, stop=True),
                  reads=[self.cbf_tl], writes=[pbt, tl["pe"]], extra_deps=dmas)
        allb = [i1, i2, i3, i4]
        P.op("act", lambda e: e.activation(out=sb[:, 0:8], in_=sb[:, 32:40], func=AF.Copy),
             writes=[tl["act"]], extra_deps=allb)
        P.op("dve", lambda e: e.memset(sb[:, 8:16], 0.0), writes=[tl["dve"]], extra_deps=allb)
        P.op("pool", lambda e: e.memset(sb[:, 16:24], 0.0), writes=[tl["pool"]], extra_deps=allb)
        P.op("pe", lambda e: e.matmul(pb[:, 0:8], lhsT=self.ident[:, 0:128], rhs=self.ident[:, 0:8],
                                      start=True, stop=True),
             writes=[pbt, tl["pe"]], extra_deps=allb)
        self.sp_fence = allb

    def dma(self, out, in_, reads=(), writes=(), eng="sp"):
        fence = getattr(self, "sp_fence", ())
        return self.P.op(eng, lambda e: e.dma_start(out=out, in_=in_), reads=reads, writes=writes,
                         dma=True, extra_deps=fence)

    def load_consts(self):
        P = self.P
        nc = self.nc
        vec_in = self.din("vecs", [128, self.NVEC])
        self.vecs = self.alloc(self.NVEC * 4)
        self.vecs_tl = Tl("vecs")
        self.dma(self.vecs, vec_in[:, :], writes=[self.vecs_tl])
        cb_in = self.din("cbf", [128, 384], BF16)
        self.cbf = self.alloc(384 * 2, BF16)
        self.cbf_tl = Tl("cbf")
        self.dma(self.cbf, cb_in[:, :], writes=[self.cbf_tl])
        self.ident = self.cbf[:, 0:128]
        self.onesm = self.cbf[:, 128:256]
        self.ones = self.cbf[:, 256:384]
        cw_in = self.din("convw", [128, 16 * CONVW])
        self.convw = self.alloc(16 * CONVW * 4)
        self.convw_tl = Tl("convw")
        self.dma(self.convw, cw_in[:, :], writes=[self.convw_tl])
        sm_in = self.din("smalls", [128, 192])
        self.smalls = self.alloc(192 * 4)
        self.smalls_tl = Tl("smalls")
        self.dma(self.smalls, sm_in[:, :], writes=[self.smalls_tl])
        self.pm = self.smalls[:, 0:1]
        self.epsc = self.smalls[:, 65:66]
        self.tinyc = self.smalls[:, 130:131]
        self.const_end = self.arena_off

    def vec(self, j, c):
        return self.vecs[:, j * 16 + c: j * 16 + c + 1]

    def cast_weights(self, specs):
        P = self.P
        CH = 4096
        NB = 3
        stage = [self.alloc(CH * 4) for _ in range(NB)]
        stage_tl = [Tl(f"stg{i}") for i in range(NB)]
        outb = [self.alloc(CH * 2, BF16) for _ in range(NB)]
        outb_tl = [Tl(f"cst{i}") for i in range(NB)]
        k = 0
        for name, nblk, E in specs:
            src = self.din(name + "_f", [128, nblk * E])
            dst = self.dscr(name, [128, nblk * E], BF16)
            self.wscr[name] = (dst, nblk, E)
            tot = nblk * E
            for o in range(0, tot, CH):
                w = min(CH, tot - o)
                i = k % NB
                self.dma(stage[i][:, :w], src[:, o:o + w], writes=[stage_tl[i]])
                ce = ("dve", "act", "pool")[k % 3]
                so, oo = stage[i][:, :w], outb[i][:, :w]
                if ce == "act":
                    P.op("act", lambda e, so=so, oo=oo: e.activation(out=oo, in_=so, func=AF.Copy),
                         reads=[stage_tl[i]], writes=[outb_tl[i]])
                elif ce == "dve":
                    P.op("dve", lambda e, so=so, oo=oo: e.tensor_copy(out=oo, in_=so),
                         reads=[stage_tl[i]], writes=[outb_tl[i]])
                else:
                    P.op("pool", lambda e, so=so, oo=oo: e.tensor_copy(out=oo, in_=so),
                         reads=[stage_tl[i]], writes=[outb_tl[i]])
                self.dma(dst[:, o:o + w], outb[i][:, :w], reads=[outb_tl[i]],
                         writes=[self.dT((name, "c", o // CH))])
                k += 1

    def wring_init(self, nbuf=3, E=8192):
        self.wr = [self.alloc(E * 2, BF16) for _ in range(nbuf)]
        self.wr_tl = [Tl(f"wr{i}") for i in range(nbuf)]
        self.wr_i = 0

    def wload(self, name, blk):
        dst, nblk, E = self.wscr[name]
        i = self.wr_i
        self.wr_i = (self.wr_i + 1) % len(self.wr)
        buf = self.wr[i][:, :E]
        rd = [self.dT((name, "c", q // 4096)) for q in range(blk * E, (blk + 1) * E, 4096)]
        self.dma(buf, dst[:, blk * E:(blk + 1) * E], reads=rd, writes=[self.wr_tl[i]])
        return buf, self.wr_tl[i]

    def linear(self, wname, KC, nchunks, NBC, ins, ins_tl, TT, epilogue, blk0=0):
        P = self.P
        nb = NBC * 128
        for b in range(nchunks // NBC):
            wt, wtl = self.wload(wname, blk0 + b)
            wv = wt.rearrange("p (k n) -> p k n", k=KC)
            for jj in range(NBC):
                n = b * NBC + jj
                ps, pst = self.bank()
                for kc in range(KC):
                    P.op("pe", lambda e, ps=ps, wv=wv, kc=kc, jj=jj, r=ins[kc]: e.matmul(
                        ps[:, :TT], lhsT=wv[:, kc, jj * 128:(jj + 1) * 128], rhs=r,
                        start=(kc == 0), stop=(kc == KC - 1)),
                        reads=[wtl, ins_tl[kc]], writes=[pst])
                epilogue(n, ps[:, :TT], pst)

    def rmsnorm(self, TT, gj, out_ap_fn, out_tl_fn, sq_view, sq_tl):
        P = self.P
        X, Xtl = self.X, self.X_tl
        sqs = [sq_view(c)[:, :TT] for c in range(DC)]
        sqt = [sq_tl(c) for c in range(DC)]
        oaps = [out_ap_fn(c) for c in range(DC)]
        otls = [out_tl_fn(c) for c in range(DC)]
        for c in range(DC):
            P.op("act", lambda e, c=c: e.activation(out=sqs[c], in_=X[:, c, :TT], func=AF.Square),
                 reads=[Xtl[c]], writes=[sqt[c]])
        ps, pst = self.bank()
        for c in range(DC):
            P.op("pe", lambda e, c=c, ps=ps: e.matmul(ps[:, :TT], lhsT=self.onesm, rhs=sqs[c],
                                                      start=(c == 0), stop=(c == DC - 1)),
                 reads=[sqt[c], self.cbf_tl], writes=[pst])
        rs = self.rstd
        P.op("act", lambda e, ps=ps: e.activation(out=rs[:, :TT], in_=ps[:, :TT], func=AF.Sqrt,
                                                  bias=self.epsc, scale=1.0),
             reads=[pst, self.smalls_tl], writes=[self.rstd_tl])
        P.op("dve", lambda e: e.reciprocal(out=rs[:, :TT], in_=rs[:, :TT]),
             reads=[self.rstd_tl], writes=[self.rstd_tl])
        for c in range(DC):
            P.op("dve", lambda e, c=c: e.scalar_tensor_tensor(
                out=oaps[c], in0=X[:, c, :TT], scalar=self.vec(gj, c), in1=rs[:, :TT],
                op0=ALU.mult, op1=ALU.mult),
                reads=[Xtl[c], self.rstd_tl, self.vecs_tl], writes=[otls[c]])

    def mlp(self, l, TT):
        P = self.P
        X, Xtl = self.X, self.X_tl
        H, Htl = self.H, self.H_tl
        G = self.G
        Gtl = self.G_tl

        def gv(j):
            return G[j // 4][:, (j % 4) * 512:(j % 4) * 512 + 512]

        self.rmsnorm(TT, self.VJ["norm_mlp"] + l,
                     lambda c: H[:, c, HALO:HALO + TT], lambda c: Htl[c],
                     lambda c: gv(c), lambda c: Gtl[c // 4])
        ins = [H[:, c, HALO:HALO + TT] for c in range(DC)]

        def ep_up(n, ps, pst):
            r, rtl = self.R[n % 2], self.R_tl[n % 2]
            P.op("act", lambda e: e.activation(out=r[:, :TT], in_=ps, func=AF.Relu),
                 reads=[pst], writes=[rtl])
            P.op("pool" if (n % 2) else "dve",
                 lambda e: e.tensor_tensor(out=gv(n)[:, :TT], in0=r[:, :TT], in1=r[:, :TT], op=ALU.mult),
                 reads=[rtl], writes=[Gtl[n // 4]])

        self.linear(f"up{l}", DC, FC, 4, ins, Htl, TT, ep_up)
        gins = [gv(j)[:, :TT] for j in range(FC)]
        gtl = [Gtl[j // 4] for j in range(FC)]

        def ep_dn(n, ps, pst):
            P.op("dve", lambda e: e.tensor_tensor(out=X[:, n, :TT], in0=X[:, n, :TT], in1=ps, op=ALU.add),
                 reads=[pst, Xtl[n]], writes=[Xtl[n]])

        self.linear(f"dn{l}", FC, DC, 1, gins, gtl, TT, ep_dn)

    def tile_bufs(self):
        self.arena_reset(self.const_end)
        Xf = self.alloc(DC * 512 * 4)
        self.X = Xf.rearrange("p (c t) -> p c t", c=DC)
        self.X_tl = [Tl(f"X{c}") for c in range(DC)]
        Hf = self.alloc(DC * (HALO + 512) * 2, BF16)
        self.H = Hf.rearrange("p (c t) -> p c t", c=DC)
        self.H_tl = [Tl(f"H{c}") for c in range(DC)]
        RB = 4352
        self.G_off = self.arena_off
        self.G = [self.alloc(RB, BF16) for _ in range(16)]
        self.G_tl = [Tl(f"G{i}") for i in range(16)]
        self.rstd = self.alloc(512 * 4)
        self.rstd_tl = Tl("rstd")
        self.tmpA = self.alloc(512 * 4)
        self.tmpA_tl = Tl("tmpA")
        self.tmpB = self.alloc(512 * 4)
        self.tmpB_tl = Tl("tmpB")
        self.UH = self.alloc(DC * HALO * 4).rearrange("p (c t) -> p c t", c=DC)
        self.UH_tl = Tl("UH")
        self.HH = self.alloc(DC * HALO * 2, BF16).rearrange("p (c t) -> p c t", c=DC)
        self.HH_tl = Tl("HH")
        self.R = [self.alloc(512 * 2, BF16) for _ in range(2)]
        self.R_tl = [Tl("R0"), Tl("R1")]
        self.wring_init(3, 8192)

    def load_x(self, src, t0, TT, off=0):
        self.dma(self.X[:, :, :TT], src[:, :, off + t0:off + t0 + TT], reads=[self.dT((src.name, off + t0))],
                 writes=self.X_tl, eng="pool")

    def store_x(self, dst, t0, TT, c0=0):
        self.dma(dst[:, :, t0 - c0:t0 - c0 + TT], self.X[:, :, :TT], reads=self.X_tl,
                 writes=[self.dT((dst.name, t0))], eng="pool")

    def mask_pre(self, ap3, tls):
        P = self.P
        P.op("dve", lambda e: e.tensor_scalar(out=ap3, in0=ap3, scalar1=self.pm, scalar2=None, op0=ALU.mult),
             reads=list(tls) + [self.smalls_tl], writes=list(tls))

    def pool_mixer(self, l, j, ti, TT, icol=None):
        P = self.P
        X, Xtl, H, Htl = self.X, self.X_tl, self.H, self.H_tl
        G, Gtl = self.G, self.G_tl
        if ti == 0:
            P.op("pool", lambda e: e.memset(H[:, :, 0:HALO], 0.0), writes=Htl)
        else:
            P.op("pool", lambda e: e.tensor_copy(out=H[:, :, 0:HALO], in_=self.HH[:, :, :]),
                 reads=[self.HH_tl], writes=Htl)
        self.rmsnorm(TT, self.VJ["norm_mix"] + l,
                     lambda c: H[:, c, HALO:HALO + TT], lambda c: Htl[c],
                     lambda c: G[c][:, 0:512], lambda c: Gtl[c])
        P.op("pool", lambda e: e.tensor_copy(out=self.HH[:, :, :], in_=H[:, :, TT:TT + HALO]),
             reads=Htl, writes=[self.HH_tl])
        W = HALO + TT
        for g in range(4):
            nsteps = g + 1
            w = POOL_W[g]
            for k in range(4):
                r = 4 * g + k
                reg32 = G[r].bitcast(F32)
                rtl = Gtl[r]
                eng = "pool" if (k % 2) else "dve"
                sh = 1
                for s in range(nsteps):
                    lo = 2 * sh - 1
                    dst = reg32[:, (s % 2) * 544:(s % 2) * 544 + 544]
                    if s == 0:
                        a = H[:, r, lo:W]
                        b = H[:, r, lo - sh:W - sh]
                        rd = [Htl[r]]
                    else:
                        srcb = reg32[:, ((s - 1) % 2) * 544:((s - 1) % 2) * 544 + 544]
                        a = srcb[:, lo:W]
                        b = srcb[:, lo - sh:W - sh]
                        rd = [rtl]
                    P.op(eng, lambda e, dst=dst, a=a, b=b, lo=lo: e.tensor_tensor(out=dst[:, lo:W], in0=a, in1=b, op=ALU.add),
                         reads=rd, writes=[rtl])
                    sh *= 2
                fin = reg32[:, ((nsteps - 1) % 2) * 544:((nsteps - 1) % 2) * 544 + 544]
                y = G[r][:, (nsteps % 2) * 1088:(nsteps % 2) * 1088 + 512]
                P.op("dve", lambda e, y=y, fin=fin, r=r, w=w: e.scalar_tensor_tensor(
                    out=y[:, :TT], in0=fin[:, HALO:HALO + TT], scalar=1.0 / w, in1=H[:, r, HALO:HALO + TT],
                    op0=ALU.mult, op1=ALU.subtract),
                    reads=[rtl, Htl[r]], writes=[rtl])
                if icol is not None:
                    ic = self.smalls[:, icol + 16 * g: icol + 16 * g + 16]
                    tmp = self.tmpA[:, 0:16]
                    P.op("dve", lambda e, fin=fin, ic=ic, tmp=tmp: e.tensor_tensor(
                        out=tmp, in0=fin[:, HALO:HALO + 16], in1=ic, op=ALU.mult),
                        reads=[rtl, self.smalls_tl], writes=[self.tmpA_tl])
                    P.op("dve", lambda e, y=y, tmp=tmp, r=r: e.tensor_tensor(
                        out=y[:, 0:16], in0=tmp, in1=H[:, r, HALO:HALO + 16], op=ALU.subtract),
                        reads=[self.tmpA_tl, Htl[r]], writes=[rtl])
        wt, wtl = self.wload(f"pool{j}", 0)
        wv = wt.rearrange("p (g k n) -> p g k n", g=4, k=4)
        for g in range(4):
            nsteps = g + 1
            for jj in range(4):
                n = 4 * g + jj
                ps, pst = self.bank()
                for kc in range(4):
                    y = G[4 * g + kc][:, (nsteps % 2) * 1088:(nsteps % 2) * 1088 + 512]
                    P.op("pe", lambda e, ps=ps, g=g, kc=kc, jj=jj, y=y: e.matmul(
                        ps[:, :TT], lhsT=wv[:, g, kc, jj * 128:(jj + 1) * 128], rhs=y[:, :TT],
                        start=(kc == 0), stop=(kc == 3)),
                        reads=[wtl, Gtl[4 * g + kc]], writes=[pst])
                P.op("dve", lambda e, ps=ps, n=n: e.scalar_tensor_tensor(
                    out=X[:, n, :TT], in0=ps[:, :TT], scalar=self.vec(self.VJ["pool_scale"] + j, n), in1=X[:, n, :TT],
                    op0=ALU.mult, op1=ALU.add),
                    reads=[pst, Xtl[n], self.vecs_tl], writes=[Xtl[n]])

    def conv_mixer(self, l, ti, TT):
        P = self.P
        X, Xtl, H, Htl = self.X, self.X_tl, self.H, self.H_tl
        G, Gtl = self.G, self.G_tl
        VJ = self.VJ
        UW = HALO + 512

        def U(c):
            return G[c].bitcast(F32)[:, 0:UW]

        def ACC(c):
            return G[c].bitcast(F32)[:, UW:UW + 512]

        self.rmsnorm(TT, VJ["norm_mix"] + l,
                     lambda c: H[:, c, HALO:HALO + TT], lambda c: Htl[c],
                     lambda c: G[c][:, 0:512], lambda c: Gtl[c])
        for c in range(DC):
            if ti == 0:
                P.op("pool", lambda e, c=c: e.memset(U(c)[:, 0:HALO], 0.0), writes=[Gtl[c]])
            else:
                P.op("pool", lambda e, c=c: e.tensor_copy(out=U(c)[:, 0:HALO], in_=self.UH[:, c, :]),
                     reads=[self.UH_tl], writes=[Gtl[c]])
        ins = [H[:, c, HALO:HALO + TT] for c in range(DC)]
        state = {}

        def ep_pw1(n, ps, pst):
            c = n // 2
            if n % 2 == 0:
                state["a"] = (ps, pst)
                return
            aps, apst = state["a"]
            sig = self.tmpA
            P.op("act", lambda e: e.activation(out=sig[:, :TT], in_=ps, func=AF.Sigmoid,
                                               bias=self.vec(VJ["b_pw1g"], c), scale=1.0),
                 reads=[pst, self.vecs_tl], writes=[self.tmpA_tl])
            P.op("dve", lambda e: e.scalar_tensor_tensor(
                out=U(c)[:, HALO:HALO + TT], in0=aps, scalar=self.vec(VJ["b_pw1a"], c), in1=sig[:, :TT],
                op0=ALU.add, op1=ALU.mult),
                reads=[apst, self.tmpA_tl, self.vecs_tl], writes=[Gtl[c]])

        self.linear("pw1", DC, 32, 2, ins, Htl, TT, ep_pw1)
        if ti == 0:
            for c in range(DC):
                self.mask_pre(U(c)[:, HALO:HALO + TT], [Gtl[c]])
        P.op("pool", lambda e: e.tensor_copy(
            out=self.UH[:, 0, :], in_=U(0)[:, TT:TT + HALO]), reads=[Gtl[0]], writes=[self.UH_tl])
        for c in range(1, DC):
            P.op("pool", lambda e, c=c: e.tensor_copy(out=self.UH[:, c, :], in_=U(c)[:, TT:TT + HALO]),
                 reads=[Gtl[c], self.UH_tl], writes=[self.UH_tl])
        for c in range(DC):
            u = U(c)
            acc = ACC(c)
            for k in range(CONVW):
                off = HALO - (CONVW - 1) + k
                wcol = self.convw[:, c * CONVW + k: c * CONVW + k + 1]
                if k == 0:
                    P.op("dve", lambda e, u=u, acc=acc, off=off, wcol=wcol, c=c: e.tensor_scalar(
                        out=acc[:, :TT], in0=u[:, off:off + TT], scalar1=wcol, scalar2=self.vec(VJ["b_dw"], c),
                        op0=ALU.mult, op1=ALU.add),
                        reads=[Gtl[c], self.convw_tl, self.vecs_tl], writes=[Gtl[c]])
                else:
                    P.op("dve", lambda e, u=u, acc=acc, off=off, wcol=wcol: e.scalar_tensor_tensor(
                        out=acc[:, :TT], in0=u[:, off:off + TT], scalar=wcol, in1=acc[:, :TT],
                        op0=ALU.mult, op1=ALU.add),
                        reads=[Gtl[c], self.convw_tl], writes=[Gtl[c]])
        for c in range(DC):
            P.op("act", lambda e, c=c: e.activation(out=H[:, c, HALO:HALO + TT], in_=ACC(c)[:, :TT], func=AF.Copy),
                 reads=[Gtl[c]], writes=[Htl[c]])
            P.op("act", lambda e, c=c: e.activation(out=G[c][:, 0:TT], in_=ACC(c)[:, :TT], func=AF.Square),
                 reads=[Gtl[c]], writes=[Gtl[c]])
        psm, psmt = self.bank()
        pss, psst = self.bank()
        for c in range(DC):
            P.op("pe", lambda e, c=c: e.matmul(psm[:, :TT], lhsT=self.onesm, rhs=H[:, c, HALO:HALO + TT],
                                               start=(c == 0), stop=(c == DC - 1)),
                 reads=[Htl[c], self.cbf_tl], writes=[psmt])
        for c in range(DC):
            P.op("pe", lambda e, c=c: e.matmul(pss[:, :TT], lhsT=self.onesm, rhs=G[c][:, 0:TT],
                                               start=(c == 0), stop=(c == DC - 1)),
                 reads=[Gtl[c], self.cbf_tl], writes=[psst])
        mean = self.tmpA
        var = self.tmpB
        rs = self.rstd
        P.op("act", lambda e: e.activation(out=mean[:, :TT], in_=psm[:, :TT], func=AF.Copy),
             reads=[psmt], writes=[self.tmpA_tl])
        P.op("dve", lambda e: e.tensor_tensor(out=var[:, :TT], in0=mean[:, :TT], in1=mean[:, :TT], op=ALU.mult),
             reads=[self.tmpA_tl], writes=[self.tmpB_tl])
        P.op("dve", lambda e: e.tensor_tensor(out=var[:, :TT], in0=pss[:, :TT], in1=var[:, :TT], op=ALU.subtract),
             reads=[psst, self.tmpB_tl], writes=[self.tmpB_tl])
        P.op("act", lambda e: e.activation(out=rs[:, :TT], in_=var[:, :TT], func=AF.Sqrt,
                                           bias=self.epsc, scale=1.0),
             reads=[self.tmpB_tl, self.smalls_tl], writes=[self.rstd_tl])
        P.op("dve", lambda e: e.reciprocal(out=rs[:, :TT], in_=rs[:, :TT]),
             reads=[self.rstd_tl], writes=[self.rstd_tl])
        for c in range(DC):
            acc = ACC(c)
            P.op("dve", lambda e, acc=acc: e.tensor_tensor(out=acc[:, :TT], in0=acc[:, :TT], in1=mean[:, :TT],
                                                           op=ALU.subtract),
                 reads=[Gtl[c], self.tmpA_tl], writes=[Gtl[c]])
            P.op("dve", lambda e, acc=acc, c=c: e.scalar_tensor_tensor(
                out=acc[:, :TT], in0=acc[:, :TT], scalar=self.vec(VJ["ln_g"], c), in1=rs[:, :TT],
                op0=ALU.mult, op1=ALU.mult),
                reads=[Gtl[c], self.rstd_tl, self.vecs_tl], writes=[Gtl[c]])
            P.op("act", lambda e, acc=acc, c=c: e.activation(
                out=H[:, c, HALO:HALO + TT], in_=acc[:, :TT], func=AF.Silu, bias=self.vec(VJ["ln_b"], c), scale=1.0),
                reads=[Gtl[c], self.vecs_tl], writes=[Htl[c]])

        def ep_pw2(n, ps, pst):
            P.op("dve", lambda e: e.scalar_tensor_tensor(
                out=X[:, n, :TT], in0=ps, scalar=self.vec(VJ["b_pw2"], n), in1=X[:, n, :TT],
                op0=ALU.add, op1=ALU.add),
                reads=[pst, Xtl[n], self.vecs_tl], writes=[Xtl[n]])

        self.linear("pw2", DC, DC, 4, ins, Htl, TT, ep_pw2)

    def final_norm(self, TT):
        X, Xtl = self.X, self.X_tl
        G, Gtl = self.G, self.G_tl
        self.rmsnorm(TT, self.VJ["norm_final"],
                     lambda c: X[:, c, :TT], lambda c: Xtl[c],
                     lambda c: G[c][:, 0:512], lambda c: Gtl[c])


    def dsa_proj(self, src, l=1):
        P = self.P
        X, Xtl, H, Htl = self.X, self.X_tl, self.H, self.H_tl
        G, Gtl = self.G, self.G_tl
        D_ = self.dsa
        tabs = [self.alloc(512 * 4) for _ in range(4)]
        tabs_tl = [Tl(f"tab{i}") for i in range(4)]
        wvv = self.arena[:, self.G_off:self.G_off + 16 * 1088].bitcast(BF16).rearrange(
            "p (k n) -> p k n", k=16)[:, :, 1024:1536]
        wiw = self.alloc(256 * 2, BF16)
        wiw_tl = Tl("wiw")
        dst, _, _ = self.wscr["wv"]
        self.dma(wvv, dst[:, :].rearrange("p (k n) -> p k n", k=16),
                 reads=[self.dT(("wv", "c", 0)), self.dT(("wv", "c", 1))], writes=Gtl)
        dst, _, _ = self.wscr["wiw"]
        self.dma(wiw, dst[:, :], reads=[self.dT(("wiw", "c", 0))], writes=[wiw_tl])
        wiwv = wiw.rearrange("p (k n) -> p k n", k=16)
        stg = [self.alloc(512 * 2, BF16) for _ in range(4)]
        stg_tl = [Tl(f"stg{i}") for i in range(4)]
        iwst = self.alloc(16 * 4)
        iwst_tl = Tl("iwst")
        sk = [0]
        own = self.own
        QOFF = own - 512
        for ti, (t0, TT) in enumerate(self.tiles0):
            self.load_x(src, t0, TT)
            full = (t0 >= QOFF)
            EXd = D_ if t0 >= own else D_["lowv"]
            kcol = t0 - own if t0 >= own else t0
            for i, nm in enumerate(("cos128", "sin128", "cos64", "sin64")):
                self.dma(tabs[i][:, :TT], self.dram[nm][:, t0:t0 + TT], writes=[tabs_tl[i]], eng="pool")
            self.rmsnorm(TT, self.VJ["norm_mix"] + l,
                         lambda c: H[:, c, HALO:HALO + TT], lambda c: Htl[c],
                         lambda c: G[c][:, 0:512], lambda c: Gtl[c])
            ins = [H[:, c, HALO:HALO + TT] for c in range(DC)]
            state = {}

            def ep(n, ps, pst, ti=ti, t0=t0, TT=TT, EXd=EXd, kcol=kcol):
                p = n // 2
                if n % 2 == 0:
                    state["a"] = (ps, pst)
                    return
                aps, apst = state["a"]
                big = p < 20
                ct, st_ = (tabs[0], tabs[1]) if big else (tabs[2], tabs[3])
                ctl, stl = (tabs_tl[0], tabs_tl[1]) if big else (tabs_tl[2], tabs_tl[3])
                P.op("dve", lambda e: e.tensor_tensor(out=self.tmpA[:, :TT], in0=aps, in1=ct[:, :TT], op=ALU.mult),
                     reads=[apst, ctl], writes=[self.tmpA_tl])
                P.op("dve", lambda e: e.tensor_tensor(out=self.tmpB[:, :TT], in0=ps, in1=st_[:, :TT], op=ALU.mult),
                     reads=[pst, stl], writes=[self.tmpB_tl])
                i = sk[0] % 4
                sk[0] += 1
                P.op("pool", lambda e: e.tensor_tensor(out=stg[i][:, :TT], in0=self.tmpA[:, :TT],
                                                       in1=self.tmpB[:, :TT], op=ALU.add),
                     reads=[self.tmpA_tl, self.tmpB_tl], writes=[stg_tl[i]])
                if p < 16:
                    self.dma(D_["qT"][:, p, t0 - QOFF:t0 - QOFF + TT], stg[i][:, :TT], reads=[stg_tl[i]],
                             writes=[self.dT(("qT", p, t0))], eng="pool")
                elif p < 20:
                    self.dma(EXd["kT"][:, p - 16, kcol:kcol + TT], stg[i][:, :TT], reads=[stg_tl[i]],
                             writes=[self.dT(("kT", p - 16, t0))], eng="pool")
                elif p < 28:
                    self.dma(D_["iqT"][:, p - 20, t0 - QOFF:t0 - QOFF + TT], stg[i][:, :TT], reads=[stg_tl[i]],
                             writes=[self.dT(("iqT", p - 20, t0))], eng="pool")
                else:
                    self.dma(EXd["ikT"][:, kcol:kcol + TT], stg[i][:, :TT], reads=[stg_tl[i]],
                             writes=[self.dT(("ikT", t0))], eng="pool")

            if full:
                self.linear("win", DC, 58, 2, ins, Htl, TT, ep)
            else:
                self.linear("win", DC, 8, 2, ins, Htl, TT, lambda n, ps, pst, ep=ep: ep(n + 32, ps, pst), blk0=16)
                self.linear("win", DC, 2, 2, ins, Htl, TT, lambda n, ps, pst, ep=ep: ep(n + 56, ps, pst), blk0=28)
            for tb in range(TT // 128):
                blk = (kcol + tb * 128) // 128
                hb = lambda kc, tb=tb: H[:, kc, HALO + tb * 128:HALO + tb * 128 + 128]
                if True:
                    ps, pst = self.bank()
                    for kc in range(DC):
                        P.op("pe", lambda e, ps=ps, kc=kc, hb=hb: e.matmul(ps[:, :512], lhsT=hb(kc), rhs=wvv[:, kc, :],
                                                                          start=(kc == 0), stop=(kc == DC - 1)),
                             reads=[Htl[kc], Gtl[kc]], writes=[pst])
                    i = sk[0] % 4
                    sk[0] += 1
                    P.op("act", lambda e, ps=ps, i=i: e.activation(out=stg[i][:, :512], in_=ps[:, :512], func=AF.Copy),
                         reads=[pst], writes=[stg_tl[i]])
                    self.dma(EXd["v"][:, blk, :], stg[i][:, :512], reads=[stg_tl[i]],
                             writes=[self.dT(("v", t0, tb))], eng="pool")
                if not full:
                    continue
                qblk = (t0 - QOFF) // 128 + tb
                ps, pst = self.bank()
                for kc in range(DC):
                    P.op("pe", lambda e, ps=ps, kc=kc, hb=hb: e.matmul(ps[:, :16], lhsT=hb(kc), rhs=wiwv[:, kc, :],
                                                                      start=(kc == 0), stop=(kc == DC - 1)),
                         reads=[Htl[kc], wiw_tl], writes=[pst])
                P.op("act", lambda e, ps=ps: e.activation(out=iwst[:, :16], in_=ps[:, :16], func=AF.Copy,
                                                          scale=1.0 / 32.0),
                     reads=[pst], writes=[iwst_tl])
                self.dma(D_["iw"][:, qblk, :], iwst[:, :16], reads=[iwst_tl], writes=[self.dT(("iw", qblk))], eng="pool")

    def exchange(self):
        own = self.nt - PRE
        D_ = self.dsa
        EX = D_["EX"]
        gath = self.dscr("gath", [256, 9 * own], BF16)
        D_["low"] = gath[0:128, :]
        rd = [self.dT(("kT", g, t0)) for g in range(4) for (t0, TT) in self.tiles[1:]]
        rd += [self.dT(("v", b)) for b in range(1, own // 128 + 1)]
        rd += [self.dT(("ikT", t0)) for (t0, TT) in self.tiles[1:]]
        groups = [[2 * i, 2 * i + 1] for i in range(self.ncores // 2)]
        self.P.op("pool", lambda e: e.collective_compute("AllGather", op=ALU.bypass, replica_groups=groups,
                                                         ins=[EX[:, :]], outs=[gath[:, :]]),
                  reads=rd, writes=[self.dT(("low",))], dma=True)

    def dsa_attn(self, nsteps=22):
        P = self.P
        D_ = self.dsa
        nqb = self.nt // 128
        own = self.nt - PRE
        nob = own // 128
        LOWK = D_["low"][:, 0:4 * own].rearrange("p (g t) -> p g t", g=4)
        LOWV = D_["low"][:, 4 * own:8 * own].rearrange("p (b c) -> p b c", b=nob)
        OWNK = D_["kT"]
        OWNV = D_["v"]
        self.arena_reset(self.const_end)
        ikT = self.alloc(8192 * 2, BF16)
        ikT_tl = Tl("ikT")
        self.dma(ikT[:, 0:own], D_["low"][:, 8 * own:9 * own], writes=[ikT_tl])
        self.dma(ikT[:, own:2 * own], D_["ikT"][:, :], writes=[ikT_tl])
        kval = self.alloc(8192 * 2, BF16)
        kval_tl = Tl("kval")
        self.dma(kval[:, :2 * own], self.dram["kvalid"][:, :], writes=[kval_tl])
        tri = self.alloc(128 * 2, BF16)
        tri_tl = Tl("tri")
        self.dma(tri, self.dram["tri"][:, :], writes=[tri_tl])
        score = self.alloc(8192 * 4)
        score_tl = Tl("score")
        M = self.alloc(8192 * 2, BF16)
        M_tl = Tl("M")
        MT2 = [self.alloc(8192 * 2, BF16).rearrange("p (b q) -> p b q", b=64) for _ in range(2)]
        MT2_tl = [Tl("MT0"), Tl("MT1")]
        QT2 = [self.alloc(16 * 128 * 2, BF16).rearrange("p (h q) -> p h q", h=16) for _ in range(2)]
        QT2_tl = [Tl("QT0"), Tl("QT1")]
        NR = 4
        Rr = [self.alloc(512 * 2, BF16) for _ in range(NR)]
        Rr_tl = [Tl(f"R{i}") for i in range(NR)]
        NE = 4
        ETr = [self.alloc(512 * 2, BF16) for _ in range(NE)]
        ETr_tl = [Tl(f"ET{i}") for i in range(NE)]
        PTr = [self.alloc(512 * 2, BF16) for _ in range(NE)]
        PTr_tl = [Tl(f"PT{i}") for i in range(NE)]
        Kr = [self.alloc(512 * 2, BF16) for _ in range(4)]
        Kr_tl = [Tl(f"K{i}") for i in range(4)]
        Vr = [self.alloc(512 * 2, BF16).rearrange("p (b d) -> p b d", b=4) for _ in range(4)]
        Vr_tl = [Tl(f"V{i}") for i in range(4)]
        IQ = self.alloc(8 * 128 * 2, BF16).rearrange("p (h q) -> p h q", h=8)
        IQ_tl = Tl("IQ")
        IW = self.alloc(16 * 4)
        IW_tl = Tl("IW")
        diag = self.alloc(16 * 128 * 2, BF16).rearrange("p (h q) -> p h q", h=16)
        diag_tl = [Tl(f"diag{h}") for h in range(16)]
        osb = [self.alloc(512 * 4) for _ in range(2)]
        osb_tl = [Tl("osb0"), Tl("osb1")]
        lnd = [self.alloc(512 * 4) for _ in range(2)]
        lnd_tl = [Tl("lnd0"), Tl("lnd1")]
        OTs = [self.alloc(512 * 2, BF16) for _ in range(2)]
        OTs_tl = [Tl("OT0"), Tl("OT1")]
        sm = self.alloc(64 * 4)
        lo, hi, mid, cnt, u, rng = (sm[:, i:i + 1] for i in range(6))
        hsx = sm[:, 8:8 + nsteps + 2]
        sm_tl = {k: Tl("sm_" + k) for k in ("lo", "hi", "mid", "cnt", "u", "rng", "hsx")}
        pw = self.smalls[:, 132:132 + nsteps + 2]
        B_IDX = (0, 1, 2)
        B_SC = 3
        B_ST = (4, 5)
        B_O = 6
        B_DEN = 7
        rk = [0, 0, 0, 0, 0, 0]

        def s1a(qb):
            c0 = qb * 128
            nkb = nob + qb
            nk = nkb * 128
            qc = 512 - PRE + c0
            QT, QT_tl = QT2[qb % 2], QT2_tl[qb % 2]
            self.dma(QT[:, :, :], D_["qT"][:, :, qc:qc + 128], writes=[QT_tl])
            self.dma(IQ[:, :, :], D_["iqT"][:, :, qc:qc + 128], writes=[IQ_tl])
            self.dma(IW[:, :], D_["iw"][:, qc // 128, :], writes=[IW_tl])
            for h in range(16):
                P.op("pool", lambda e, h=h: e.tensor_scalar(out=diag[:, h, :], in0=self.ident, scalar1=IW[:, h:h + 1],
                                                            scalar2=None, op0=ALU.mult),
                     reads=[IW_tl, self.cbf_tl], writes=[diag_tl[h]])
            items = [(k0, h) for k0 in range(0, nk, 512) for h in range(16)]
            st = {}

            def stage_a(t):
                k0, h = items[t]
                w = min(512, nk - k0)
                ps, pst = self.bank(B_IDX[rk[5] % 3])
                rk[5] += 1
                hp = 64 * (h % 2)
                P.op("pe", lambda e: e.matmul(
                    ps[:, :w], lhsT=IQ[hp:hp + 64, h // 2, :], rhs=ikT[hp:hp + 64, k0:k0 + w], start=True, stop=True),
                    reads=[IQ_tl, ikT_tl], writes=[pst])
                ri = rk[0] % NR
                rk[0] += 1
                P.op("dve", lambda e: e.tensor_scalar(out=Rr[ri][:, :w], in0=ps[:, :w], scalar1=0.0,
                                                      scalar2=None, op0=ALU.max),
                     reads=[pst], writes=[Rr_tl[ri]])
                st[t] = ri

            def stage_b(t):
                k0, h = items[t]
                w = min(512, nk - k0)
                ri = st.pop(t)
                scp, scpt = self.bank(B_SC)
                P.op("pe", lambda e: e.matmul(
                    scp[:, :w], lhsT=diag[:, h, :], rhs=Rr[ri][:, :w], start=(h == 0), stop=(h == 15)),
                    reads=[diag_tl[h], Rr_tl[ri]], writes=[scpt])
                if h == 15:
                    P.op("act", lambda e: e.activation(out=score[:, k0:k0 + w], in_=scp[:, :w], func=AF.Copy),
                         reads=[scpt], writes=[score_tl])

            LA = 2
            for t in range(len(items) + LA):
                if t < len(items):
                    stage_a(t)
                if t - LA >= 0:
                    stage_b(t - LA)

        def s1b(qb):
            nkb = nob + qb
            nk = nkb * 128
            P.op("dve", lambda e: e.tensor_reduce(out=lo, in_=score[:, :nk], axis=mybir.AxisListType.X, op=ALU.min),
                 reads=[score_tl], writes=[sm_tl["lo"]])
            P.op("dve", lambda e: e.tensor_reduce(out=hi, in_=score[:, :nk], axis=mybir.AxisListType.X, op=ALU.max),
                 reads=[score_tl], writes=[sm_tl["hi"]])
            P.op("dve", lambda e: e.tensor_tensor(out=score[:, :nk], in0=score[:, :nk], in1=kval[:, :nk], op=ALU.add),
                 reads=[score_tl, kval_tl], writes=[score_tl])
            P.op("dve", lambda e: e.tensor_tensor(out=score[:, nk - 128:nk], in0=score[:, nk - 128:nk], in1=tri[:, :],
                                                  op=ALU.add),
                 reads=[score_tl, tri_tl], writes=[score_tl])
            P.op("dve", lambda e: e.tensor_tensor(out=rng, in0=hi, in1=lo, op=ALU.subtract),
                 reads=[sm_tl["lo"], sm_tl["hi"]], writes=[sm_tl["rng"]])
            P.op("dve", lambda e: e.tensor_scalar(out=hsx, in0=pw, scalar1=rng, scalar2=None, op0=ALU.mult),
                 reads=[sm_tl["rng"], self.smalls_tl], writes=[sm_tl["hsx"]])
            P.op("dve", lambda e: e.tensor_tensor(out=mid, in0=lo, in1=hsx[:, 1:2], op=ALU.add),
                 reads=[sm_tl["lo"], sm_tl["hsx"]], writes=[sm_tl["mid"]])
            for s in range(nsteps):
                P.op("dve", lambda e: e.tensor_scalar(
                    out=M[:, :nk], in0=score[:, :nk], scalar1=mid, scalar2=0.0, op0=ALU.is_ge, op1=ALU.add,
                    accum_out=cnt),
                    reads=[score_tl, sm_tl["mid"]], writes=[M_tl, sm_tl["cnt"]])
                last = (s == nsteps - 1)
                ha = hsx[:, s + 1:s + 2] if not last else hsx[:, s + 1:s + 2]
                hb = hsx[:, s + 2:s + 3] if not last else hsx[:, s + 1:s + 2]
                P.op("dve", lambda e, ha=ha: e.scalar_tensor_tensor(out=u, in0=cnt, scalar=255.5, in1=ha,
                                                                    op0=ALU.is_ge, op1=ALU.mult),
                     reads=[sm_tl["cnt"], sm_tl["hsx"]], writes=[sm_tl["u"]])
                P.op("dve", lambda e, hb=hb: e.scalar_tensor_tensor(out=mid, in0=mid, scalar=hb, in1=u,
                                                                    op0=ALU.subtract, op1=ALU.add),
                     reads=[sm_tl["mid"], sm_tl["u"], sm_tl["hsx"]], writes=[sm_tl["mid"]])
            P.op("dve", lambda e: e.tensor_scalar(out=M[:, :nk], in0=score[:, :nk], scalar1=mid, scalar2=None, op0=ALU.is_ge),
                 reads=[score_tl, sm_tl["mid"]], writes=[M_tl])

        def s1c(qb):
            nkb = nob + qb
            MT, MT_tl = MT2[qb % 2], MT2_tl[qb % 2]
            for kb0 in range(0, nkb, 4):
                nb = min(4, nkb - kb0)
                trp, trpt = self.bank(B_SC)
                trb = trp.bitcast(BF16)
                for j in range(nb):
                    kb = kb0 + j
                    P.op("pe", lambda e, trb=trb, j=j, kb=kb: e.transpose(
                        out=trb[:, j * 128:(j + 1) * 128], in_=M[:, kb * 128:(kb + 1) * 128], identity=self.ident),
                        reads=[M_tl, self.cbf_tl], writes=[trpt])
                P.op("act", lambda e, trb=trb, kb0=kb0, nb=nb: e.activation(
                    out=MT[:, kb0:kb0 + nb, :], in_=trb[:, :nb * 128].rearrange("p (b q) -> p b q", b=nb), func=AF.Copy),
                    reads=[trpt], writes=[MT_tl])

        def s2(qb):
            c0 = qb * 128
            nkb = nob + qb
            nk = nkb * 128
            QT, QT_tl = QT2[qb % 2], QT2_tl[qb % 2]
            MT, MT_tl = MT2[qb % 2], MT2_tl[qb % 2]
            ops_, opst = self.bank(B_O)
            dps, dpst = self.bank(B_DEN)
            items = [(g, kb) for g in range(4) for kb in range(nkb)]
            st = {}
            kv = {}

            def stage_a(t):
                g, kb = items[t]
                if kb % 4 == 0:
                    k0 = kb * 128
                    w = min(512, nk - k0)
                    nb = w // 128
                    ki = rk[1] % 4
                    rk[1] += 1
                    if k0 < own:
                        ksrc = LOWK[:, g, k0:k0 + w]
                        vsrc = LOWV[:, k0 // 128:k0 // 128 + nb, g * 128:(g + 1) * 128]
                    else:
                        ksrc = OWNK[:, g, k0 - own:k0 - own + w]
                        vsrc = OWNV[:, (k0 - own) // 128:(k0 - own) // 128 + nb, g * 128:(g + 1) * 128]
                    self.dma(Kr[ki][:, :w], ksrc, writes=[Kr_tl[ki]])
                    self.dma(Vr[ki][:, :nb, :], vsrc, writes=[Vr_tl[ki]])
                    kv[(g, kb // 4)] = ki
                ki = kv[(g, kb // 4)]
                j = kb % 4
                stp, stpt = self.bank(B_ST[rk[2] % 2])
                rk[2] += 1
                P.op("pe", lambda e: e.matmul(
                    stp[:, :], lhsT=Kr[ki][:, j * 128:(j + 1) * 128],
                    rhs=QT[:, 4 * g:4 * g + 4, :], start=True, stop=True),
                    reads=[Kr_tl[ki], QT_tl], writes=[stpt])
                ei = rk[3] % NE
                rk[3] += 1
                P.op("act", lambda e: e.activation(out=ETr[ei][:, :], in_=stp[:, :], func=AF.Exp,
                                                   scale=float(128 ** -0.5)),
                     reads=[stpt], writes=[ETr_tl[ei]])
                P.op("pool", lambda e: e.tensor_tensor(
                    out=PTr[ei][:, :].rearrange("p (r q) -> p r q", r=4),
                    in0=ETr[ei][:, :].rearrange("p (r q) -> p r q", r=4),
                    in1=MT[:, kb, :].unsqueeze(1).to_broadcast([128, 4, 128]), op=ALU.mult),
                    reads=[ETr_tl[ei], MT_tl], writes=[PTr_tl[ei]])
                st[t] = (ei, ki, j)

            def stage_b(t):
                g, kb = items[t]
                ei, ki, j = st.pop(t)
                P.op("pe", lambda e: e.matmul(
                    ops_[:, :], lhsT=Vr[ki][:, j, :], rhs=PTr[ei][:, :], start=(kb == 0), stop=(kb == nkb - 1)),
                    reads=[Vr_tl[ki], PTr_tl[ei]], writes=[opst])
                P.op("pe", lambda e: e.matmul(
                    dps[:, :], lhsT=self.ones, rhs=PTr[ei][:, :], start=(kb == 0), stop=(kb == nkb - 1)),
                    reads=[self.cbf_tl, PTr_tl[ei]], writes=[dpst])
                if kb == nkb - 1:
                    oi = rk[4] % 2
                    rk[4] += 1
                    P.op("act", lambda e: e.activation(out=lnd[oi][:, :], in_=dps[:, :], func=AF.Ln,
                                                       bias=self.tinyc, scale=1.0),
                         reads=[dpst, self.smalls_tl], writes=[lnd_tl[oi]])
                    P.op("act", lambda e: e.activation(out=osb[oi][:, :], in_=ops_[:, :], func=AF.Copy),
                         reads=[opst], writes=[osb_tl[oi]])
                    P.op("act", lambda e: e.activation(out=lnd[oi][:, :], in_=lnd[oi][:, :], func=AF.Exp, scale=-1.0),
                         reads=[lnd_tl[oi]], writes=[lnd_tl[oi]])
                    P.op("pool", lambda e: e.tensor_tensor(out=OTs[oi][:, :], in0=osb[oi][:, :], in1=lnd[oi][:, :],
                                                           op=ALU.mult),
                         reads=[osb_tl[oi], lnd_tl[oi]], writes=[OTs_tl[oi]])
                    self.dma(D_["oT"][:, 4 * g:4 * g + 4, c0:c0 + 128], OTs[oi][:, :].rearrange("p (r q) -> p r q", r=4),
                             reads=[OTs_tl[oi]], writes=[self.dT(("oT", g, qb))], eng="pool")

            LA = 2
            for t in range(len(items) + LA):
                if t < len(items):
                    stage_a(t)
                if t - LA >= 0:
                    stage_b(t - LA)

        s1a(0)
        s1b(0)
        s1c(0)
        for qb in range(nqb):
            if qb + 1 < nqb:
                s1a(qb + 1)
                s1b(qb + 1)
            s2(qb)
            if qb + 1 < nqb:
                s1c(qb + 1)

    def dsa_out(self, t0, TT):
        P = self.P
        X, Xtl, H, Htl = self.X, self.X_tl, self.H, self.H_tl
        rd = [self.dT(("oT", g, qb)) for g in range(4) for qb in range(t0 // 128, (t0 + TT) // 128)]
        self.dma(H[:, :, HALO:HALO + TT], self.dsa["oT"][:, :, t0:t0 + TT], reads=rd, writes=Htl, eng="pool")
        ins = [H[:, c, HALO:HALO + TT] for c in range(DC)]

        def ep(n, ps, pst):
            P.op("dve", lambda e: e.tensor_tensor(out=X[:, n, :TT], in0=X[:, n, :TT], in1=ps, op=ALU.add),
                 reads=[pst, Xtl[n]], writes=[Xtl[n]])

        self.linear("wout", DC, DC, 4, ins, Htl, TT, ep)


def vec_layout():
    VJ = {}
    j = 0
    for name, n in (("norm_mix", 4), ("norm_mlp", 4), ("pool_scale", 2), ("b_pw1a", 1), ("b_pw1g", 1),
                    ("b_dw", 1), ("ln_g", 1), ("ln_b", 1), ("b_pw2", 1), ("norm_final", 1)):
        VJ[name] = j
        j += n
    return VJ, j


def fm(v):
    return np.ascontiguousarray(np.asarray(v, np.float32).reshape(16, 128).T)


def weight_specs(layers):
    specs = []
    for l in layers:
        k, j = l % 3, l // 3
        if k == 0:
            specs.append((f"pool{j}", 1, 8192))
        elif k == 2:
            specs.append(("pw1", 16, 4096))
            specs.append(("pw2", 4, 8192))
        specs.append((f"up{l}", 16, 8192))
        specs.append((f"dn{l}", 16, 8192))
    return specs


def dsa_tensors(B, mode):
    nt = B.nt
    own = nt - PRE
    nq = own + 512

    def exv(EX):
        return {"kT": EX[:, 0:4 * own].rearrange("p (g t) -> p g t", g=4),
                "v": EX[:, 4 * own:8 * own].rearrange("p (b c) -> p b c", b=own // 128),
                "ikT": EX[:, 8 * own:9 * own]}
    EX = B.dscr("EX", [128, 9 * own], BF16)
    EXL = B.dscr("EXL", [128, 9 * own], BF16)
    D_ = {"EX": EX, "low": EXL, "lowv": exv(EXL)}
    D_.update(exv(EX))
    D_["qT"] = B.dscr("qT", [128, 16, nq], BF16)
    D_["iqT"] = B.dscr("iqT", [128, 8, nq], BF16)
    D_["iw"] = B.dscr("iw", [128, nq // 128, 16], F32)
    D_["oT"] = B.dscr("oT", [128, 16, nt], BF16)
    B.din("kvalid", [128, 2 * own], BF16)
    B.din("tri", [128, 128], BF16)
    for nm in ("cos128", "sin128", "cos64", "sin64"):
        B.din(nm, [128, 2 * own])
    return D_


def mode_layers(mode):
    return {"A": (0, 1), "B": (1, 2, 3), "fused": (0, 1, 2, 3)}[mode]


def mode_specs(mode):
    specs = []
    if mode in ("A", "fused"):
        specs += [("pool0", 1, 8192), ("up0", 16, 8192), ("dn0", 16, 8192),
                  ("win", 29, 4096), ("wv", 1, 8192), ("wiw", 1, 256)]
    if mode in ("B", "fused"):
        specs += [("wout", 4, 8192), ("up1", 16, 8192), ("dn1", 16, 8192),
                  ("pw1", 16, 4096), ("pw2", 4, 8192), ("up2", 16, 8192), ("dn2", 16, 8192),
                  ("pool1", 1, 8192), ("up3", 16, 8192), ("dn3", 16, 8192)]
    return specs


def build_program(mode="fused", nt=NT, upto=3, final=True, ncores=NCORES):
    B = Builder(nt=nt, mode=mode, ncores=ncores)
    own = B.own
    B.arena_init()
    B.VJ, nv = vec_layout()
    B.NVEC = nv * 16
    B.wscr = {}
    B.load_consts()
    B.dsa = dsa_tensors(B, mode)
    xin = B.din("xin", [128, DC, 2 * own])
    x1 = B.dscr("x1", [128, DC, 2 * own], F32)
    xs = [B.dscr("xsA", [128, DC, nt], F32), B.dscr("xsB", [128, DC, nt], F32)]
    out = B.dout("out", [128, DC, own])
    B.cast_weights(mode_specs("fused"))
    B.barrier()
    B.tile_bufs()
    for ti, (t0, TT) in enumerate(B.tiles0):
        B.load_x(xin, t0, TT)
        icol = 66 if ti == 0 else (1 if t0 == own else None)
        B.pool_mixer(0, 0, ti, TT, icol=icol)
        B.mlp(0, TT)
        B.store_x(x1, t0, TT)
    B.dsa_proj(x1)
    B.barrier()
    B.dsa_attn()
    B.barrier()
    B.tile_bufs()

    def layer_loop(l, src, dst, last, off=0):
        kind = l % 3
        for ti, (t0, TT) in enumerate(B.tiles):
            B.load_x(src, t0, TT, off=off)
            if ti == 0:
                B.mask_pre(B.X[:, :, :TT], B.X_tl)
            if kind == 0:
                B.pool_mixer(l, l // 3, ti, TT, icol=(1 if ti == 1 else None))
            elif kind == 2:
                B.conv_mixer(l, ti, TT)
            else:
                B.dsa_out(t0, TT)
            B.mlp(l, TT)
            if last:
                if ti == 0:
                    continue
                if final:
                    B.final_norm(TT)
                B.store_x(dst, t0, TT, c0=PRE)
            else:
                B.store_x(dst, t0, TT)

    seq = [(1, x1, xs[0], own - PRE), (2, xs[0], xs[1], 0), (3, xs[1], out, 0)]
    seq = [s for s in seq if s[0] <= upto]
    for i, (l, s, d, off) in enumerate(seq):
        lastl = (i == len(seq) - 1)
        layer_loop(l, s, out if lastl else d, lastl, off=off)
    B.P.emit()
    return B


def prep_common(inp, layers):
    VJ, nv = vec_layout()
    vecs = np.zeros((128, nv * 16), np.float32)

    def put(name, idx, v):
        j = VJ[name] + idx
        vecs[:, j * 16:(j + 1) * 16] = fm(v)

    for i in range(4):
        put("norm_mix", i, inp["norm_mix"][i])
        put("norm_mlp", i, inp["norm_mlp"][i])
    for i in range(2):
        put("pool_scale", i, inp["pool_scale"][i])
    put("b_pw1a", 0, inp["conv_b_pw1"][0][:D])
    put("b_pw1g", 0, inp["conv_b_pw1"][0][D:])
    put("b_dw", 0, inp["conv_b_dw"][0])
    put("ln_g", 0, inp["conv_ln_g"][0])
    put("ln_b", 0, inp["conv_ln_b"][0])
    put("b_pw2", 0, inp["conv_b_pw2"][0])
    put("norm_final", 0, inp["norm_final"])
    cbf = np.zeros((128, 384), np.float32)
    cbf[:, 0:128] = np.eye(128, dtype=np.float32)
    cbf[:, 128:256] = 1.0 / 2048.0
    cbf[:, 256:384] = 1.0
    cbf = cbf.astype(ml_dtypes.bfloat16)
    wdw = np.asarray(inp["conv_w_dw"][0], np.float32)
    convw = np.ascontiguousarray(wdw.reshape(CONVW, 16, 128).transpose(2, 1, 0)).reshape(128, 16 * CONVW)
    com = {"vecs": vecs, "cbf": cbf, "convw": convw}
    for l in layers:
        k, j = l % 3, l // 3
        if k == 0:
            w = np.asarray(inp["pool_w"][j], np.float32)
            com[f"pool{j}_f"] = np.ascontiguousarray(
                w.reshape(4, 4, 128, 512).transpose(2, 0, 1, 3)).reshape(128, 8192)
        elif k == 2:
            w = np.asarray(inp["conv_w_pw1"][0], np.float32)
            w4 = w.reshape(16, 128, 2, 16, 128)
            com["pw1_f"] = np.ascontiguousarray(w4.transpose(1, 3, 0, 2, 4)).reshape(128, 16 * 4096)
            w = np.asarray(inp["conv_w_pw2"][0], np.float32)
            com["pw2_f"] = np.ascontiguousarray(
                w.reshape(16, 128, 4, 512).transpose(1, 2, 0, 3)).reshape(128, 4 * 8192)
        w = np.asarray(inp["mlp_up"][l], np.float32)
        com[f"up{l}_f"] = np.ascontiguousarray(
            w.reshape(16, 128, 16, 512).transpose(1, 2, 0, 3)).reshape(128, 16 * 8192)
        w = np.asarray(inp["mlp_down"][l], np.float32)
        com[f"dn{l}_f"] = np.ascontiguousarray(
            w.reshape(64, 128, 16, 128).transpose(1, 2, 0, 3)).reshape(128, 16 * 8192)
    return com


def core_tokens(x, k, nt=NT):
    b, half = k // 2, k % 2
    own = np.asarray(x[b, half * OWN: half * OWN + (nt - PRE)], np.float32)
    if half == 1:
        pre = np.asarray(x[b, half * OWN - PRE: half * OWN], np.float32)
    else:
        pre = np.zeros((PRE, D), np.float32)
    return np.concatenate([pre, own], axis=0)


def to_fm(a):
    n = a.shape[0]
    return np.ascontiguousarray(a.T.reshape(16, 128, n).transpose(1, 0, 2))


def from_fm(a):
    n = a.shape[2]
    return np.ascontiguousarray(a.transpose(1, 0, 2).reshape(2048, n).T)


def smalls_for(k):
    half = k % 2
    s = np.zeros((128, 192), np.float32)
    s[:, 0] = float(half)
    s[:, 65] = EPS
    s[:, 130] = 1e-18
    for j in range(40):
        s[:, 132 + j] = 2.0 ** (-j)
    for g, w in enumerate(POOL_W):
        for t in range(16):
            start = 1.0 / min(t + 1, w)
            s[:, 1 + 16 * g + t] = start if half == 0 else 1.0 / w
            s[:, 66 + 16 * g + t] = start
    return s


def rope_tables(k, nt=NT):
    half = k % 2
    own = nt - PRE
    pos = (np.arange(2 * own) - (1 - half) * own).astype(np.float32)
    out = {}
    for nm, d in (("128", 128), ("64", 64)):
        inv = (np.float32(10000.0) ** (-np.arange(0, d, 2, dtype=np.float32) / np.float32(d))).astype(np.float32)
        ang = (pos[:, None] * inv[None, :]).astype(np.float32)
        cos = np.cos(ang).astype(np.float32)
        sin = np.sin(ang).astype(np.float32)
        dd = np.arange(128) % d
        idx = dd % (d // 2)
        sign = np.where(dd < d // 2, -1.0, 1.0).astype(np.float32)
        out["cos" + nm] = np.ascontiguousarray(cos[:, idx].T)
        out["sin" + nm] = np.ascontiguousarray((sin[:, idx] * sign[None, :]).T)
    return out


def dsa_weights(inp):
    w = np.asarray(inp["dsa_w_in"][0], np.float32)
    cols = []
    d = np.arange(128)
    for h in range(16):
        cols.append(h * 128 + d)
        cols.append(h * 128 + (d + 64) % 128)
    for g in range(4):
        cols.append(2048 + g * 128 + d)
        cols.append(2048 + g * 128 + (d + 64) % 128)
    for m in range(8):
        head = 2 * m + d // 64
        dd = d % 64
        cols.append(3072 + head * 64 + dd)
        cols.append(3072 + head * 64 + (dd + 32) % 64)
    dd = d % 64
    cols.append(4096 + dd)
    cols.append(4096 + (dd + 32) % 64)
    cols = np.concatenate(cols)
    wp = w[:, cols]
    win = np.ascontiguousarray(wp.reshape(16, 128, 29, 256).transpose(1, 2, 0, 3)).reshape(128, 29 * 4096)
    wv = np.ascontiguousarray(w[:, 2560:3072].reshape(16, 128, 512).transpose(1, 0, 2)).reshape(128, 8192)
    wiw = np.ascontiguousarray(w[:, 4160:4176].reshape(16, 128, 16).transpose(1, 0, 2)).reshape(128, 256)
    wo = np.asarray(inp["dsa_w_out"][0], np.float32)
    wout = np.ascontiguousarray(wo.reshape(16, 128, 4, 512).transpose(1, 2, 0, 3)).reshape(128, 4 * 8192)
    return {"win_f": win, "wv_f": wv, "wiw_f": wiw, "wout_f": wout}


def attn_masks(k, nt=NT):
    half = k % 2
    own = nt - PRE
    kv = np.zeros((128, 2 * own), np.float32)
    if half == 0:
        kv[:, :own] = NEG
    q = np.arange(128)[:, None]
    s = np.arange(128)[None, :]
    tri = np.where(s <= q, 0.0, NEG).astype(np.float32)
    return {"kvalid": kv.astype(ml_dtypes.bfloat16), "tri": tri.astype(ml_dtypes.bfloat16)}


def kernel(**inp):
    inp = {k: np.asarray(v) for k, v in inp.items()}
    com = prep_common(inp, (0, 1, 2, 3))
    com.update(dsa_weights(inp))
    x = inp["x"]
    maps = []
    for k in range(NCORES):
        b, half = k // 2, k % 2
        m = dict(com)
        seq = np.asarray(x[b], np.float32)
        if half == 0:
            seq = np.concatenate([np.zeros((OWN, D), np.float32), seq[:OWN]], axis=0)
        m["xin"] = to_fm(seq)
        m["smalls"] = smalls_for(k)
        m.update(rope_tables(k))
        m.update(attn_masks(k))
        maps.append(m)
    cores = list(range(NCORES))
    BF = build_program("fused")
    res = run_bass_kernel_spmd(BF.nc, [{n: m[n] for n in BF.in_names} for m in maps], core_ids=cores)
    out = np.empty((4, SEQ, D), np.float32)
    for k in cores:
        b, half = k // 2, k % 2
        out[b, half * OWN:(half + 1) * OWN] = from_fm(np.asarray(res.results[k]["out"]))
    return out
```

```python
import numpy as np
import ml_dtypes
import concourse.bass as bass
import concourse.mybir as mybir
from concourse.bass_utils import run_bass_kernel_spmd

F32 = mybir.dt.float32
BF16 = mybir.dt.bfloat16
AF = mybir.ActivationFunctionType
ALU = mybir.AluOpType

D = 2048
DC = 16
DFF = 8192
FC = 64
SEQ = 8192
OWN = 4096
PRE = 128
NT = OWN + PRE
HALO = 32
EPS = 1e-6
NEG = -1.0e30
POOL_W = (2, 4, 8, 16)
CONVW = 31
NCORES = 8


class Tl:
    __slots__ = ("name", "last_w", "readers")

    def __init__(self, name=""):
        self.name = name
        self.last_w = None
        self.readers = {}


class Op:
    __slots__ = ("eng", "fn", "deps", "dma", "idx")


ENG_BLOCK = {"pe": "tensor", "act": "scalar", "dve": "vector", "pool": "gpsimd", "sp": "sync"}


class Prog:
    NDS = 8

    def __init__(self, nc):
        self.nc = nc
        self.ops = []
        self.dma_since_barrier = []

    def op(self, eng, fn, reads=(), writes=(), dma=False, extra_deps=()):
        idx = len(self.ops)
        deps = set(extra_deps)
        for t in reads:
            if t.last_w is not None:
                deps.add(t.last_w)
        for t in writes:
            if t.last_w is not None:
                deps.add(t.last_w)
            deps.update(t.readers.values())
        key = ("d", idx) if dma else eng
        for t in reads:
            t.readers[key] = idx
        for t in writes:
            t.last_w = idx
            t.readers = {}
        deps.discard(idx)
        o = Op()
        o.eng = eng
        o.fn = fn
        o.dma = dma
        o.idx = idx
        ops = self.ops
        if eng == "pe" and not dma:
            o.deps = [d for d in deps if not (ops[d].eng == "pe" and not ops[d].dma)]
        else:
            o.deps = list(deps)
        ops.append(o)
        if dma:
            self.dma_since_barrier.append(idx)
        return idx

    def emit(self):
        nc = self.nc
        ops = self.ops
        need = [False] * len(ops)
        for o in ops:
            for d in o.deps:
                need[d] = True
        engs = []
        for o in ops:
            if o.eng not in engs:
                engs.append(o.eng)
        csem = {}
        ccnt = {}
        dsem = {}
        dcnt = {}
        drr = {}
        for e in engs:
            csem[e] = nc.alloc_semaphore(name=f"c_{e}")
            ccnt[e] = 0
            dsem[e] = [nc.alloc_semaphore(name=f"d_{e}_{i}") for i in range(self.NDS)]
            drr[e] = 0
            for i in range(self.NDS):
                dcnt[(e, i)] = 0
        sig = {}
        pre_wait = {}
        for o in ops:
            if o.dma:
                i = drr[o.eng]
                drr[o.eng] = (i + 1) % self.NDS
                prev = dcnt[(o.eng, i)]
                pre_wait[o.idx] = (dsem[o.eng][i], prev)
                dcnt[(o.eng, i)] = prev + 16
                sig[o.idx] = (dsem[o.eng][i], prev + 16)
            elif need[o.idx]:
                ccnt[o.eng] += 1
                sig[o.idx] = (csem[o.eng], ccnt[o.eng])
        self.max_counts = dict(ccnt)
        final_dma = [(dsem[e][i], dcnt[(e, i)]) for e in engs for i in range(self.NDS) if dcnt[(e, i)] > 0]
        with nc.Block() as block:
            for e in engs:
                my = [o for o in ops if o.eng == e]

                def body(eng, my=my, e=e):
                    waited = {}

                    def wait(sem, val):
                        if val <= 0:
                            return
                        k = sem.num
                        if waited.get(k, 0) >= val:
                            return
                        eng.wait_ge(sem, val)
                        waited[k] = val

                    for o in my:
                        for d in sorted(o.deps):
                            wait(*sig[d])
                        if o.dma:
                            wait(*pre_wait[o.idx])
                        ins = o.fn(eng)
                        if o.idx in sig:
                            ins.then_inc(sig[o.idx][0], 16 if o.dma else 1)
                    if e == "sp":
                        for s, v in final_dma:
                            wait(s, v)

                getattr(block, ENG_BLOCK[e])(body)


def make_tiles(nt):
    tiles = [(0, PRE)]
    t = PRE
    while t < nt:
        tiles.append((t, 512))
        t += 512
    return tiles


class Builder:
    def __init__(self, nt=NT, layers=(0, 1, 2, 3), mode="full", ncores=NCORES):
        self.ncores = ncores
        self.nt = nt
        self.tiles = make_tiles(nt)
        self.own = nt - PRE
        self.tiles0 = [(512 * i, 512) for i in range(2 * self.own // 512)]
        self.layers = layers
        self.mode = mode
        self.nc = bass.Bass("TRN2", target_bir_lowering=False)
        self.P = Prog(self.nc)
        self.dram = {}
        self.dtl = {}

    def din(self, name, shape, dt=F32):
        t = self.nc.dram_tensor(name, list(shape), dt, kind="ExternalInput")
        self.dram[name] = t
        if not hasattr(self, "in_names"):
            self.in_names = []
        self.in_names.append(name)
        return t

    def dout(self, name, shape, dt=F32):
        t = self.nc.dram_tensor(name, list(shape), dt, kind="ExternalOutput")
        self.dram[name] = t
        return t

    def dscr(self, name, shape, dt):
        t = self.nc.dram_tensor(name, list(shape), dt)
        self.dram[name] = t
        return t

    def dT(self, key):
        if key not in self.dtl:
            self.dtl[key] = Tl(str(key))
        return self.dtl[key]

    def arena_init(self):
        nc = self.nc
        self.ARENA_W = 52800
        self.arena = nc.alloc_sbuf_tensor("arena", [128, self.ARENA_W], F32)
        self.arena_off = 0
        self.psum = nc.alloc_psum_tensor("psum", [128, 4096], F32)
        self.pbank = [Tl(f"bank{i}") for i in range(8)]
        self.pb_rr = 0

    def arena_reset(self, keep=0):
        self.arena_off = keep

    def alloc(self, nbytes, dt=F32):
        words = (nbytes + 3) // 4
        words = (words + 7) // 8 * 8
        a = self.arena[:, self.arena_off:self.arena_off + words]
        self.arena_off += words
        assert self.arena_off <= self.ARENA_W, ("SBUF arena overflow", self.arena_off * 4)
        if dt == BF16:
            a = a.bitcast(BF16)
        return a

    def bank(self, i=None):
        if i is None:
            i = self.pb_rr
            self.pb_rr = (self.pb_rr + 1) % 8
        return self.psum[:, i * 512:(i + 1) * 512], self.pbank[i]

    def barrier(self):
        P = self.P
        nc = self.nc
        if not hasattr(self, "bar_sb"):
            self.bar_sb = nc.alloc_sbuf_tensor("bar_sb", [128, 64], F32)
            self.bar_tl = {e: Tl("bar_" + e) for e in ("act", "dve", "pool", "pe")}
        sb = self.bar_sb
        tl = self.bar_tl
        dmas = list(P.dma_since_barrier)
        P.dma_since_barrier = []
        i1 = P.op("act", lambda e: e.activation(out=sb[:, 0:8], in_=sb[:, 32:40], func=AF.Copy),
                  writes=[tl["act"]], extra_deps=dmas)
        i2 = P.op("dve", lambda e: e.memset(sb[:, 8:16], 0.0), writes=[tl["dve"]], extra_deps=dmas)
        i3 = P.op("pool", lambda e: e.memset(sb[:, 16:24], 0.0), writes=[tl["pool"]], extra_deps=dmas)
        pb, pbt = self.bank(7)
        i4 = P.op("pe", lambda e: e.matmul(pb[:, 0:8], lhsT=self.ident[:, 0:128], rhs=self.ident[:, 0:8],
                                           start=True, stop=True),
                  reads=[self.cbf_tl], writes=[pbt, tl["pe"]], extra_deps=dmas)
        allb = [i1, i2, i3, i4]
        P.op("act", lambda e: e.activation(out=sb[:, 0:8], in_=sb[:, 32:40], func=AF.Copy),
             writes=[tl["act"]], extra_deps=allb)
        P.op("dve", lambda e: e.memset(sb[:, 8:16], 0.0), writes=[tl["dve"]], extra_deps=allb)
        P.op("pool", lambda e: e.memset(sb[:, 16:24], 0.0), writes=[tl["pool"]], extra_deps=allb)
        P.op("pe", lambda e: e.matmul(pb[:, 0:8], lhsT=self.ident[:, 0:128], rhs=self.ident[:, 0:8],
                                      start=True, stop=True),
             writes=[pbt, tl["pe"]], extra_deps=allb)
        self.sp_fence = allb

    def dma(self, out, in_, reads=(), writes=(), eng="sp"):
        fence = getattr(self, "sp_fence", ())
        return self.P.op(eng, lambda e: e.dma_start(out=out, in_=in_), reads=reads, writes=writes,
                         dma=True, extra_deps=fence)

    def load_consts(self):
        P = self.P
        nc = self.nc
        vec_in = self.din("vecs", [128, self.NVEC])
        self.vecs = self.alloc(self.NVEC * 4)
        self.vecs_tl = Tl("vecs")
        self.dma(self.vecs, vec_in[:, :], writes=[self.vecs_tl])
        cb_in = self.din("cbf", [128, 384], BF16)
        self.cbf = self.alloc(384 * 2, BF16)
        self.cbf_tl = Tl("cbf")
        self.dma(self.cbf, cb_in[:, :], writes=[self.cbf_tl])
        self.ident = self.cbf[:, 0:128]
        self.onesm = self.cbf[:, 128:256]
        self.ones = self.cbf[:, 256:384]
        cw_in = self.din("convw", [128, 16 * CONVW])
        self.convw = self.alloc(16 * CONVW * 4)
        self.convw_tl = Tl("convw")
        self.dma(self.convw, cw_in[:, :], writes=[self.convw_tl])
        sm_in = self.din("smalls", [128, 192])
        self.smalls = self.alloc(192 * 4)
        self.smalls_tl = Tl("smalls")
        self.dma(self.smalls, sm_in[:, :], writes=[self.smalls_tl])
        self.pm = self.smalls[:, 0:1]
        self.epsc = self.smalls[:, 65:66]
        self.tinyc = self.smalls[:, 130:131]
        self.negc = self.smalls[:, 131:132]
        self.const_end = self.arena_off

    def vec(self, j, c):
        return self.vecs[:, j * 16 + c: j * 16 + c + 1]

    def cast_weights(self, specs):
        P = self.P
        CH = 4096
        NB = 3
        stage = [self.alloc(CH * 4) for _ in range(NB)]
        stage_tl = [Tl(f"stg{i}") for i in range(NB)]
        outb = [self.alloc(CH * 2, BF16) for _ in range(NB)]
        outb_tl = [Tl(f"cst{i}") for i in range(NB)]
        k = 0
        for name, nblk, E in specs:
            src = self.din(name + "_f", [128, nblk * E])
            dst = self.dscr(name, [128, nblk * E], BF16)
            self.wscr[name] = (dst, nblk, E)
            tot = nblk * E
            for o in range(0, tot, CH):
                w = min(CH, tot - o)
                i = k % NB
                self.dma(stage[i][:, :w], src[:, o:o + w], writes=[stage_tl[i]])
                ce = ("dve", "act", "pool")[k % 3]
                so, oo = stage[i][:, :w], outb[i][:, :w]
                if ce == "act":
                    P.op("act", lambda e, so=so, oo=oo: e.activation(out=oo, in_=so, func=AF.Copy),
                         reads=[stage_tl[i]], writes=[outb_tl[i]])
                elif ce == "dve":
                    P.op("dve", lambda e, so=so, oo=oo: e.tensor_copy(out=oo, in_=so),
                         reads=[stage_tl[i]], writes=[outb_tl[i]])
                else:
                    P.op("pool", lambda e, so=so, oo=oo: e.tensor_copy(out=oo, in_=so),
                         reads=[stage_tl[i]], writes=[outb_tl[i]])
                self.dma(dst[:, o:o + w], outb[i][:, :w], reads=[outb_tl[i]],
                         writes=[self.dT((name, "c", o // CH))])
                k += 1

    def wring_init(self, nbuf=3, E=8192):
        self.wr = [self.alloc(E * 2, BF16) for _ in range(nbuf)]
        self.wr_tl = [Tl(f"wr{i}") for i in range(nbuf)]
        self.wr_i = 0

    def wload(self, name, blk):
        dst, nblk, E = self.wscr[name]
        i = self.wr_i
        self.wr_i = (self.wr_i + 1) % len(self.wr)
        buf = self.wr[i][:, :E]
        rd = [self.dT((name, "c", q // 4096)) for q in range(blk * E, (blk + 1) * E, 4096)]
        self.dma(buf, dst[:, blk * E:(blk + 1) * E], reads=rd, writes=[self.wr_tl[i]])
        return buf, self.wr_tl[i]

    def linear(self, wname, KC, nchunks, NBC, ins, ins_tl, TT, epilogue, blk0=0):
        P = self.P
        nb = NBC * 128
        for b in range(nchunks // NBC):
            wt, wtl = self.wload(wname, blk0 + b)
            wv = wt.rearrange("p (k n) -> p k n", k=KC)
            for jj in range(NBC):
                n = b * NBC + jj
                ps, pst = self.bank()
                for kc in range(KC):
                    P.op("pe", lambda e, ps=ps, wv=wv, kc=kc, jj=jj, r=ins[kc]: e.matmul(
                        ps[:, :TT], lhsT=wv[:, kc, jj * 128:(jj + 1) * 128], rhs=r,
                        start=(kc == 0), stop=(kc == KC - 1)),
                        reads=[wtl, ins_tl[kc]], writes=[pst])
                epilogue(n, ps[:, :TT], pst)

    def rmsnorm(self, TT, gj, out_ap_fn, out_tl_fn, sq_view, sq_tl):
        P = self.P
        X, Xtl = self.X, self.X_tl
        sqs = [sq_view(c)[:, :TT] for c in range(DC)]
        sqt = [sq_tl(c) for c in range(DC)]
        oaps = [out_ap_fn(c) for c in range(DC)]
        otls = [out_tl_fn(c) for c in range(DC)]
        for c in range(DC):
            P.op("act", lambda e, c=c: e.activation(out=sqs[c], in_=X[:, c, :TT], func=AF.Square),
                 reads=[Xtl[c]], writes=[sqt[c]])
        ps, pst = self.bank()
        for c in range(DC):
            P.op("pe", lambda e, c=c, ps=ps: e.matmul(ps[:, :TT], lhsT=self.onesm, rhs=sqs[c],
                                                      start=(c == 0), stop=(c == DC - 1)),
                 reads=[sqt[c], self.cbf_tl], writes=[pst])
        rs = self.rstd
        P.op("act", lambda e, ps=ps: e.activation(out=rs[:, :TT], in_=ps[:, :TT], func=AF.Sqrt,
                                                  bias=self.epsc, scale=1.0),
             reads=[pst, self.smalls_tl], writes=[self.rstd_tl])
        P.op("dve", lambda e: e.reciprocal(out=rs[:, :TT], in_=rs[:, :TT]),
             reads=[self.rstd_tl], writes=[self.rstd_tl])
        for c in range(DC):
            P.op("dve", lambda e, c=c: e.scalar_tensor_tensor(
                out=oaps[c], in0=X[:, c, :TT], scalar=self.vec(gj, c), in1=rs[:, :TT],
                op0=ALU.mult, op1=ALU.mult),
                reads=[Xtl[c], self.rstd_tl, self.vecs_tl], writes=[otls[c]])

    def mlp(self, l, TT):
        P = self.P
        X, Xtl = self.X, self.X_tl
        H, Htl = self.H, self.H_tl
        G = self.G
        Gtl = self.G_tl

        def gv(j):
            return G[j // 4][:, (j % 4) * 512:(j % 4) * 512 + 512]

        self.rmsnorm(TT, self.VJ["norm_mlp"] + l,
                     lambda c: H[:, c, HALO:HALO + TT], lambda c: Htl[c],
                     lambda c: gv(c), lambda c: Gtl[c // 4])
        ins = [H[:, c, HALO:HALO + TT] for c in range(DC)]

        def ep_up(n, ps, pst):
            r, rtl = self.R[n % 2], self.R_tl[n % 2]
            P.op("act", lambda e: e.activation(out=r[:, :TT], in_=ps, func=AF.Relu),
                 reads=[pst], writes=[rtl])
            P.op("pool" if (n % 2) else "dve",
                 lambda e: e.tensor_tensor(out=gv(n)[:, :TT], in0=r[:, :TT], in1=r[:, :TT], op=ALU.mult),
                 reads=[rtl], writes=[Gtl[n // 4]])

        self.linear(f"up{l}", DC, FC, 4, ins, Htl, TT, ep_up)
        gins = [gv(j)[:, :TT] for j in range(FC)]
        gtl = [Gtl[j // 4] for j in range(FC)]

        def ep_dn(n, ps, pst):
            P.op("dve", lambda e: e.tensor_tensor(out=X[:, n, :TT], in0=X[:, n, :TT], in1=ps, op=ALU.add),
                 reads=[pst, Xtl[n]], writes=[Xtl[n]])

        self.linear(f"dn{l}", FC, DC, 1, gins, gtl, TT, ep_dn)

    def tile_bufs(self):
        self.arena_reset(self.const_end)
        Xf = self.alloc(DC * 512 * 4)
        self.X = Xf.rearrange("p (c t) -> p c t", c=DC)
        self.X_tl = [Tl(f"X{c}") for c in range(DC)]
        Hf = self.alloc(DC * (HALO + 512) * 2, BF16)
        self.H = Hf.rearrange("p (c t) -> p c t", c=DC)
        self.H_tl = [Tl(f"H{c}") for c in range(DC)]
        RB = 4352
        self.G_off = self.arena_off
        self.G = [self.alloc(RB, BF16) for _ in range(16)]
        self.G_tl = [Tl(f"G{i}") for i in range(16)]
        self.rstd = self.alloc(512 * 4)
        self.rstd_tl = Tl("rstd")
        self.tmpA = self.alloc(512 * 4)
        self.tmpA_tl = Tl("tmpA")
        self.tmpB = self.alloc(512 * 4)
        self.tmpB_tl = Tl("tmpB")
        self.UH = self.alloc(DC * HALO * 4).rearrange("p (c t) -> p c t", c=DC)
        self.UH_tl = Tl("UH")
        self.HH = self.alloc(DC * HALO * 2, BF16).rearrange("p (c t) -> p c t", c=DC)
        self.HH_tl = Tl("HH")
        self.R = [self.alloc(512 * 2, BF16) for _ in range(2)]
        self.R_tl = [Tl("R0"), Tl("R1")]
        self.wring_init(3, 8192)

    def load_x(self, src, t0, TT, off=0):
        self.dma(self.X[:, :, :TT], src[:, :, off + t0:off + t0 + TT], reads=[self.dT((src.name, off + t0))],
                 writes=self.X_tl, eng="pool")

    def store_x(self, dst, t0, TT, c0=0):
        self.dma(dst[:, :, t0 - c0:t0 - c0 + TT], self.X[:, :, :TT], reads=self.X_tl,
                 writes=[self.dT((dst.name, t0))], eng="pool")

    def mask_pre(self, ap3, tls):
        P = self.P
        P.op("dve", lambda e: e.tensor_scalar(out=ap3, in0=ap3, scalar1=self.pm, scalar2=None, op0=ALU.mult),
             reads=list(tls) + [self.smalls_tl], writes=list(tls))

    def pool_mixer(self, l, j, ti, TT, icol=None):
        P = self.P
        X, Xtl, H, Htl = self.X, self.X_tl, self.H, self.H_tl
        G, Gtl = self.G, self.G_tl
        if ti == 0:
            P.op("pool", lambda e: e.memset(H[:, :, 0:HALO], 0.0), writes=Htl)
        else:
            P.op("pool", lambda e: e.tensor_copy(out=H[:, :, 0:HALO], in_=self.HH[:, :, :]),
                 reads=[self.HH_tl], writes=Htl)
        self.rmsnorm(TT, self.VJ["norm_mix"] + l,
                     lambda c: H[:, c, HALO:HALO + TT], lambda c: Htl[c],
                     lambda c: G[c][:, 0:512], lambda c: Gtl[c])
        P.op("pool", lambda e: e.tensor_copy(out=self.HH[:, :, :], in_=H[:, :, TT:TT + HALO]),
             reads=Htl, writes=[self.HH_tl])
        W = HALO + TT
        for g in range(4):
            nsteps = g + 1
            w = POOL_W[g]
            for k in range(4):
                r = 4 * g + k
                reg32 = G[r].bitcast(F32)
                rtl = Gtl[r]
                eng = "pool" if (k % 2) else "dve"
                sh = 1
                for s in range(nsteps):
                    lo = 2 * sh - 1
                    dst = reg32[:, (s % 2) * 544:(s % 2) * 544 + 544]
                    if s == 0:
                        a = H[:, r, lo:W]
                        b = H[:, r, lo - sh:W - sh]
                        rd = [Htl[r]]
                    else:
                        srcb = reg32[:, ((s - 1) % 2) * 544:((s - 1) % 2) * 544 + 544]
                        a = srcb[:, lo:W]
                        b = srcb[:, lo - sh:W - sh]
                        rd = [rtl]
                    P.op(eng, lambda e, dst=dst, a=a, b=b, lo=lo: e.tensor_tensor(out=dst[:, lo:W], in0=a, in1=b, op=ALU.add),
                         reads=rd, writes=[rtl])
                    sh *= 2
                fin = reg32[:, ((nsteps - 1) % 2) * 544:((nsteps - 1) % 2) * 544 + 544]
                y = G[r][:, (nsteps % 2) * 1088:(nsteps % 2) * 1088 + 512]
                P.op("dve", lambda e, y=y, fin=fin, r=r, w=w: e.scalar_tensor_tensor(
                    out=y[:, :TT], in0=fin[:, HALO:HALO + TT], scalar=1.0 / w, in1=H[:, r, HALO:HALO + TT],
                    op0=ALU.mult, op1=ALU.subtract),
                    reads=[rtl, Htl[r]], writes=[rtl])
                if icol is not None:
                    ic = self.smalls[:, icol + 16 * g: icol + 16 * g + 16]
                    tmp = self.tmpA[:, 0:16]
                    P.op("dve", lambda e, fin=fin, ic=ic, tmp=tmp: e.tensor_tensor(
                        out=tmp, in0=fin[:, HALO:HALO + 16], in1=ic, op=ALU.mult),
                        reads=[rtl, self.smalls_tl], writes=[self.tmpA_tl])
                    P.op("dve", lambda e, y=y, tmp=tmp, r=r: e.tensor_tensor(
                        out=y[:, 0:16], in0=tmp, in1=H[:, r, HALO:HALO + 16], op=ALU.subtract),
                        reads=[self.tmpA_tl, Htl[r]], writes=[rtl])
        wt, wtl = self.wload(f"pool{j}", 0)
        wv = wt.rearrange("p (g k n) -> p g k n", g=4, k=4)
        for g in range(4):
            nsteps = g + 1
            for jj in range(4):
                n = 4 * g + jj
                ps, pst = self.bank()
                for kc in range(4):
                    y = G[4 * g + kc][:, (nsteps % 2) * 1088:(nsteps % 2) * 1088 + 512]
                    P.op("pe", lambda e, ps=ps, g=g, kc=kc, jj=jj, y=y: e.matmul(
                        ps[:, :TT], lhsT=wv[:, g, kc, jj * 128:(jj + 1) * 128], rhs=y[:, :TT],
                        start=(kc == 0), stop=(kc == 3)),
                        reads=[wtl, Gtl[4 * g + kc]], writes=[pst])
                P.op("dve", lambda e, ps=ps, n=n: e.scalar_tensor_tensor(
                    out=X[:, n, :TT], in0=ps[:, :TT], scalar=self.vec(self.VJ["pool_scale"] + j, n), in1=X[:, n, :TT],
                    op0=ALU.mult, op1=ALU.add),
                    reads=[pst, Xtl[n], self.vecs_tl], writes=[Xtl[n]])

    def conv_mixer(self, l, ti, TT):
        P = self.P
        X, Xtl, H, Htl = self.X, self.X_tl, self.H, self.H_tl
        G, Gtl = self.G, self.G_tl
        VJ = self.VJ
        UW = HALO + 512

        def U(c):
            return G[c].bitcast(F32)[:, 0:UW]

        def ACC(c):
            return G[c].bitcast(F32)[:, UW:UW + 512]

        self.rmsnorm(TT, VJ["norm_mix"] + l,
                     lambda c: H[:, c, HALO:HALO + TT], lambda c: Htl[c],
                     lambda c: G[c][:, 0:512], lambda c: Gtl[c])
        for c in range(DC):
            if ti == 0:
                P.op("pool", lambda e, c=c: e.memset(U(c)[:, 0:HALO], 0.0), writes=[Gtl[c]])
            else:
                P.op("pool", lambda e, c=c: e.tensor_copy(out=U(c)[:, 0:HALO], in_=self.UH[:, c, :]),
                     reads=[self.UH_tl], writes=[Gtl[c]])
        ins = [H[:, c, HALO:HALO + TT] for c in range(DC)]
        state = {}

        def ep_pw1(n, ps, pst):
            c = n // 2
            if n % 2 == 0:
                state["a"] = (ps, pst)
                return
            aps, apst = state["a"]
            sig = self.tmpA
            P.op("act", lambda e: e.activation(out=sig[:, :TT], in_=ps, func=AF.Sigmoid,
                                               bias=self.vec(VJ["b_pw1g"], c), scale=1.0),
                 reads=[pst, self.vecs_tl], writes=[self.tmpA_tl])
            P.op("dve", lambda e: e.scalar_tensor_tensor(
                out=U(c)[:, HALO:HALO + TT], in0=aps, scalar=self.vec(VJ["b_pw1a"], c), in1=sig[:, :TT],
                op0=ALU.add, op1=ALU.mult),
                reads=[apst, self.tmpA_tl, self.vecs_tl], writes=[Gtl[c]])

        self.linear("pw1", DC, 32, 2, ins, Htl, TT, ep_pw1)
        if ti == 0:
            for c in range(DC):
                self.mask_pre(U(c)[:, HALO:HALO + TT], [Gtl[c]])
        P.op("pool", lambda e: e.tensor_copy(
            out=self.UH[:, 0, :], in_=U(0)[:, TT:TT + HALO]), reads=[Gtl[0]], writes=[self.UH_tl])
        for c in range(1, DC):
            P.op("pool", lambda e, c=c: e.tensor_copy(out=self.UH[:, c, :], in_=U(c)[:, TT:TT + HALO]),
                 reads=[Gtl[c], self.UH_tl], writes=[self.UH_tl])
        for c in range(DC):
            u = U(c)
            acc = ACC(c)
            for k in range(CONVW):
                off = HALO - (CONVW - 1) + k
                wcol = self.convw[:, c * CONVW + k: c * CONVW + k + 1]
                if k == 0:
                    P.op("dve", lambda e, u=u, acc=acc, off=off, wcol=wcol, c=c: e.tensor_scalar(
                        out=acc[:, :TT], in0=u[:, off:off + TT], scalar1=wcol, scalar2=self.vec(VJ["b_dw"], c),
                        op0=ALU.mult, op1=ALU.add),
                        reads=[Gtl[c], self.convw_tl, self.vecs_tl], writes=[Gtl[c]])
                else:
                    P.op("dve", lambda e, u=u, acc=acc, off=off, wcol=wcol: e.scalar_tensor_tensor(
                        out=acc[:, :TT], in0=u[:, off:off + TT], scalar=wcol, in1=acc[:, :TT],
                        op0=ALU.mult, op1=ALU.add),
                        reads=[Gtl[c], self.convw_tl], writes=[Gtl[c]])
        for c in range(DC):
            P.op("act", lambda e, c=c: e.activation(out=H[:, c, HALO:HALO + TT], in_=ACC(c)[:, :TT], func=AF.Copy),
                 reads=[Gtl[c]], writes=[Htl[c]])
            P.op("act", lambda e, c=c: e.activation(out=G[c][:, 0:TT], in_=ACC(c)[:, :TT], func=AF.Square),
                 reads=[Gtl[c]], writes=[Gtl[c]])
        psm, psmt = self.bank()
        pss, psst = self.bank()
        for c in range(DC):
            P.op("pe", lambda e, c=c: e.matmul(psm[:, :TT], lhsT=self.onesm, rhs=H[:, c, HALO:HALO + TT],
                                               start=(c == 0), stop=(c == DC - 1)),
                 reads=[Htl[c], self.cbf_tl], writes=[psmt])
        for c in range(DC):
            P.op("pe", lambda e, c=c: e.matmul(pss[:, :TT], lhsT=self.onesm, rhs=G[c][:, 0:TT],
                                               start=(c == 0), stop=(c == DC - 1)),
                 reads=[Gtl[c], self.cbf_tl], writes=[psst])
        mean = self.tmpA
        var = self.tmpB
        rs = self.rstd
        P.op("act", lambda e: e.activation(out=mean[:, :TT], in_=psm[:, :TT], func=AF.Copy),
             reads=[psmt], writes=[self.tmpA_tl])
        P.op("dve", lambda e: e.tensor_tensor(out=var[:, :TT], in0=mean[:, :TT], in1=mean[:, :TT], op=ALU.mult),
             reads=[self.tmpA_tl], writes=[self.tmpB_tl])
        P.op("dve", lambda e: e.tensor_tensor(out=var[:, :TT], in0=pss[:, :TT], in1=var[:, :TT], op=ALU.subtract),
             reads=[psst, self.tmpB_tl], writes=[self.tmpB_tl])
        P.op("act", lambda e: e.activation(out=rs[:, :TT], in_=var[:, :TT], func=AF.Sqrt,
                                           bias=self.epsc, scale=1.0),
             reads=[self.tmpB_tl, self.smalls_tl], writes=[self.rstd_tl])
        P.op("dve", lambda e: e.reciprocal(out=rs[:, :TT], in_=rs[:, :TT]),
             reads=[self.rstd_tl], writes=[self.rstd_tl])
        for c in range(DC):
            acc = ACC(c)
            P.op("dve", lambda e, acc=acc: e.tensor_tensor(out=acc[:, :TT], in0=acc[:, :TT], in1=mean[:, :TT],
                                                           op=ALU.subtract),
                 reads=[Gtl[c], self.tmpA_tl], writes=[Gtl[c]])
            P.op("dve", lambda e, acc=acc, c=c: e.scalar_tensor_tensor(
                out=acc[:, :TT], in0=acc[:, :TT], scalar=self.vec(VJ["ln_g"], c), in1=rs[:, :TT],
                op0=ALU.mult, op1=ALU.mult),
                reads=[Gtl[c], self.rstd_tl, self.vecs_tl], writes=[Gtl[c]])
            P.op("act", lambda e, acc=acc, c=c: e.activation(
                out=H[:, c, HALO:HALO + TT], in_=acc[:, :TT], func=AF.Silu, bias=self.vec(VJ["ln_b"], c), scale=1.0),
                reads=[Gtl[c], self.vecs_tl], writes=[Htl[c]])

        def ep_pw2(n, ps, pst):
            P.op("dve", lambda e: e.scalar_tensor_tensor(
                out=X[:, n, :TT], in0=ps, scalar=self.vec(VJ["b_pw2"], n), in1=X[:, n, :TT],
                op0=ALU.add, op1=ALU.add),
                reads=[pst, Xtl[n], self.vecs_tl], writes=[Xtl[n]])

        self.linear("pw2", DC, DC, 4, ins, Htl, TT, ep_pw2)

    def final_norm(self, TT):
        X, Xtl = self.X, self.X_tl
        G, Gtl = self.G, self.G_tl
        self.rmsnorm(TT, self.VJ["norm_final"],
                     lambda c: X[:, c, :TT], lambda c: Xtl[c],
                     lambda c: G[c][:, 0:512], lambda c: Gtl[c])


    def dsa_proj(self, src, l=1):
        P = self.P
        X, Xtl, H, Htl = self.X, self.X_tl, self.H, self.H_tl
        G, Gtl = self.G, self.G_tl
        D_ = self.dsa
        tabs = [self.alloc(512 * 4) for _ in range(4)]
        tabs_tl = [Tl(f"tab{i}") for i in range(4)]
        wvv = self.arena[:, self.G_off:self.G_off + 16 * 1088].bitcast(BF16).rearrange(
            "p (k n) -> p k n", k=16)[:, :, 1024:1536]
        wiw = self.alloc(256 * 2, BF16)
        wiw_tl = Tl("wiw")
        dst, _, _ = self.wscr["wv"]
        self.dma(wvv, dst[:, :].rearrange("p (k n) -> p k n", k=16),
                 reads=[self.dT(("wv", "c", 0)), self.dT(("wv", "c", 1))], writes=Gtl)
        dst, _, _ = self.wscr["wiw"]
        self.dma(wiw, dst[:, :], reads=[self.dT(("wiw", "c", 0))], writes=[wiw_tl])
        wiwv = wiw.rearrange("p (k n) -> p k n", k=16)
        stg = [self.alloc(512 * 2, BF16) for _ in range(4)]
        stg_tl = [Tl(f"stg{i}") for i in range(4)]
        iwst = self.alloc(16 * 4)
        iwst_tl = Tl("iwst")
        sk = [0]
        own = self.own
        QOFF = own - 512
        for ti, (t0, TT) in enumerate(self.tiles0):
            self.load_x(src, t0, TT)
            full = (t0 >= QOFF)
            EXd = D_ if t0 >= own else D_["lowv"]
            kcol = t0 - own if t0 >= own else t0
            for i, nm in enumerate(("cos128", "sin128", "cos64", "sin64")):
                self.dma(tabs[i][:, :TT], self.dram[nm][:, t0:t0 + TT], writes=[tabs_tl[i]], eng="pool")
            self.rmsnorm(TT, self.VJ["norm_mix"] + l,
                         lambda c: H[:, c, HALO:HALO + TT], lambda c: Htl[c],
                         lambda c: G[c][:, 0:512], lambda c: Gtl[c])
            ins = [H[:, c, HALO:HALO + TT] for c in range(DC)]
            state = {}

            def ep(n, ps, pst, ti=ti, t0=t0, TT=TT, EXd=EXd, kcol=kcol):
                p = n // 2
                if n % 2 == 0:
                    state["a"] = (ps, pst)
                    return
                aps, apst = state["a"]
                big = p < 20
                ct, st_ = (tabs[0], tabs[1]) if big else (tabs[2], tabs[3])
                ctl, stl = (tabs_tl[0], tabs_tl[1]) if big else (tabs_tl[2], tabs_tl[3])
                P.op("dve", lambda e: e.tensor_tensor(out=self.tmpA[:, :TT], in0=aps, in1=ct[:, :TT], op=ALU.mult),
                     reads=[apst, ctl], writes=[self.tmpA_tl])
                P.op("dve", lambda e: e.tensor_tensor(out=self.tmpB[:, :TT], in0=ps, in1=st_[:, :TT], op=ALU.mult),
                     reads=[pst, stl], writes=[self.tmpB_tl])
                i = sk[0] % 4
                sk[0] += 1
                P.op("pool", lambda e: e.tensor_tensor(out=stg[i][:, :TT], in0=self.tmpA[:, :TT],
                                                       in1=self.tmpB[:, :TT], op=ALU.add),
                     reads=[self.tmpA_tl, self.tmpB_tl], writes=[stg_tl[i]])
                if p < 16:
                    self.dma(D_["qT"][:, p, t0 - QOFF:t0 - QOFF + TT], stg[i][:, :TT], reads=[stg_tl[i]],
                             writes=[self.dT(("qT", p, t0))], eng="pool")
                elif p < 20:
                    self.dma(EXd["kT"][:, p - 16, kcol:kcol + TT], stg[i][:, :TT], reads=[stg_tl[i]],
                             writes=[self.dT(("kT", p - 16, t0))], eng="pool")
                elif p < 28:
                    self.dma(D_["iqT"][:, p - 20, t0 - QOFF:t0 - QOFF + TT], stg[i][:, :TT], reads=[stg_tl[i]],
                             writes=[self.dT(("iqT", p - 20, t0))], eng="pool")
                else:
                    self.dma(EXd["ikT"][:, kcol:kcol + TT], stg[i][:, :TT], reads=[stg_tl[i]],
                             writes=[self.dT(("ikT", t0))], eng="pool")

            if full:
                self.linear("win", DC, 58, 2, ins, Htl, TT, ep)
            else:
                self.linear("win", DC, 8, 2, ins, Htl, TT, lambda n, ps, pst, ep=ep: ep(n + 32, ps, pst), blk0=16)
                self.linear("win", DC, 2, 2, ins, Htl, TT, lambda n, ps, pst, ep=ep: ep(n + 56, ps, pst), blk0=28)
            for tb in range(TT // 128):
                blk = (kcol + tb * 128) // 128
                hb = lambda kc, tb=tb: H[:, kc, HALO + tb * 128:HALO + tb * 128 + 128]
                if True:
                    ps, pst = self.bank()
                    for kc in range(DC):
                        P.op("pe", lambda e, ps=ps, kc=kc, hb=hb: e.matmul(ps[:, :512], lhsT=hb(kc), rhs=wvv[:, kc, :],
                                                                          start=(kc == 0), stop=(kc == DC - 1)),
                             reads=[Htl[kc], Gtl[kc]], writes=[pst])
                    i = sk[0] % 4
                    sk[0] += 1
                    P.op("act", lambda e, ps=ps, i=i: e.activation(out=stg[i][:, :512], in_=ps[:, :512], func=AF.Copy),
                         reads=[pst], writes=[stg_tl[i]])
                    self.dma(EXd["v"][:, blk, :], stg[i][:, :512], reads=[stg_tl[i]],
                             writes=[self.dT(("v", t0, tb))], eng="pool")
                if not full:
                    continue
                qblk = (t0 - QOFF) // 128 + tb
                ps, pst = self.bank()
                for kc in range(DC):
                    P.op("pe", lambda e, ps=ps, kc=kc, hb=hb: e.matmul(ps[:, :16], lhsT=hb(kc), rhs=wiwv[:, kc, :],
                                                                      start=(kc == 0), stop=(kc == DC - 1)),
                         reads=[Htl[kc], wiw_tl], writes=[pst])
                P.op("act", lambda e, ps=ps: e.activation(out=iwst[:, :16], in_=ps[:, :16], func=AF.Copy,
                                                          scale=1.0 / 32.0),
                     reads=[pst], writes=[iwst_tl])
                self.dma(D_["iw"][:, qblk, :], iwst[:, :16], reads=[iwst_tl], writes=[self.dT(("iw", qblk))], eng="pool")

    def exchange(self):
        own = self.nt - PRE
        D_ = self.dsa
        EX = D_["EX"]
        gath = self.dscr("gath", [256, 9 * own], BF16)
        D_["low"] = gath[0:128, :]
        rd = [self.dT(("kT", g, t0)) for g in range(4) for (t0, TT) in self.tiles[1:]]
        rd += [self.dT(("v", b)) for b in range(1, own // 128 + 1)]
        rd += [self.dT(("ikT", t0)) for (t0, TT) in self.tiles[1:]]
        groups = [[2 * i, 2 * i + 1] for i in range(self.ncores // 2)]
        self.P.op("pool", lambda e: e.collective_compute("AllGather", op=ALU.bypass, replica_groups=groups,
                                                         ins=[EX[:, :]], outs=[gath[:, :]]),
                  reads=rd, writes=[self.dT(("low",))], dma=True)

    def dsa_attn(self, nsteps=22):
        P = self.P
        D_ = self.dsa
        nqb = self.nt // 128
        own = self.nt - PRE
        nob = own // 128
        LOWK = D_["low"][:, 0:4 * own].rearrange("p (g t) -> p g t", g=4)
        LOWV = D_["low"][:, 4 * own:8 * own].rearrange("p (b c) -> p b c", b=nob)
        OWNK = D_["kT"]
        OWNV = D_["v"]
        self.arena_reset(self.const_end)
        ikT = self.alloc(8192 * 2, BF16)
        ikT_tl = Tl("ikT")
        self.dma(ikT[:, 0:own], D_["low"][:, 8 * own:9 * own], writes=[ikT_tl])
        self.dma(ikT[:, own:2 * own], D_["ikT"][:, :], writes=[ikT_tl])
        kval = self.alloc(8192 * 2, BF16)
        kval_tl = Tl("kval")
        self.dma(kval[:, :2 * own], self.dram["kvalid"][:, :], writes=[kval_tl])
        tri = self.alloc(128 * 2, BF16)
        tri_tl = Tl("tri")
        self.dma(tri, self.dram["tri"][:, :], writes=[tri_tl])
        score = self.alloc(8192 * 4)
        score_tl = Tl("score")
        M = self.alloc(8192 * 2, BF16)
        M_tl = Tl("M")
        MT2 = [self.alloc(8192 * 2, BF16).rearrange("p (b q) -> p b q", b=64) for _ in range(2)]
        MT2_tl = [Tl("MT0"), Tl("MT1")]
        QT2 = [self.alloc(16 * 128 * 2, BF16).rearrange("p (h q) -> p h q", h=16) for _ in range(2)]
        QT2_tl = [Tl("QT0"), Tl("QT1")]
        NR = 6
        Rr = [self.alloc(512 * 2, BF16) for _ in range(NR)]
        Rr_tl = [Tl(f"R{i}") for i in range(NR)]
        NE = 6
        ETr = [self.alloc(512 * 2, BF16) for _ in range(NE)]
        ETr_tl = [Tl(f"ET{i}") for i in range(NE)]
        PTr = [self.alloc(512 * 2, BF16) for _ in range(NE)]
        PTr_tl = [Tl(f"PT{i}") for i in range(NE)]
        Kr = [self.alloc(512 * 2, BF16) for _ in range(4)]
        Kr_tl = [Tl(f"K{i}") for i in range(4)]
        Vr = [self.alloc(512 * 2, BF16).rearrange("p (b d) -> p b d", b=4) for _ in range(4)]
        Vr_tl = [Tl(f"V{i}") for i in range(4)]
        IQ = self.alloc(8 * 128 * 2, BF16).rearrange("p (h q) -> p h q", h=8)
        IQ_tl = Tl("IQ")
        IW = self.alloc(16 * 4)
        IW_tl = Tl("IW")
        diag = self.alloc(16 * 128 * 2, BF16).rearrange("p (h q) -> p h q", h=16)
        diag_tl = [Tl(f"diag{h}") for h in range(16)]
        osb = [self.alloc(512 * 4) for _ in range(2)]
        osb_tl = [Tl("osb0"), Tl("osb1")]
        lnd = [self.alloc(512 * 4) for _ in range(2)]
        lnd_tl = [Tl("lnd0"), Tl("lnd1")]
        OTs = [self.alloc(512 * 2, BF16) for _ in range(2)]
        OTs_tl = [Tl("OT0"), Tl("OT1")]
        sm = self.alloc(64 * 4)
        lo, hi, mid, cnt, u, rng = (sm[:, i:i + 1] for i in range(6))
        hsx = sm[:, 8:8 + nsteps + 2]
        sm_tl = {k: Tl("sm_" + k) for k in ("lo", "hi", "mid", "cnt", "u", "rng", "hsx")}
        pw = self.smalls[:, 132:132 + nsteps + 2]
        B_IDX = (0, 1, 2, 4, 5)
        B_SC = 3
        B_ST = (0, 1, 2, 4, 5)
        B_O = 6
        B_DEN = 7
        rk = [0, 0, 0, 0, 0, 0]

        def s1a(qb):
            c0 = qb * 128
            nkb = nob + qb
            nk = nkb * 128
            qc = 512 - PRE + c0
            QT, QT_tl = QT2[qb % 2], QT2_tl[qb % 2]
            self.dma(QT[:, :, :], D_["qT"][:, :, qc:qc + 128], writes=[QT_tl])
            self.dma(IQ[:, :, :], D_["iqT"][:, :, qc:qc + 128], writes=[IQ_tl])
            self.dma(IW[:, :], D_["iw"][:, qc // 128, :], writes=[IW_tl])
            for h in range(16):
                P.op("pool", lambda e, h=h: e.tensor_scalar(out=diag[:, h, :], in0=self.ident, scalar1=IW[:, h:h + 1],
                                                            scalar2=None, op0=ALU.mult),
                     reads=[IW_tl, self.cbf_tl], writes=[diag_tl[h]])
            items = [(k0, h) for k0 in range(0, nk, 512) for h in range(16)]
            st = {}

            def stage_a(t):
                k0, h = items[t]
                w = min(512, nk - k0)
                ps, pst = self.bank(B_IDX[rk[5] % 5])
                rk[5] += 1
                hp = 64 * (h % 2)
                P.op("pe", lambda e: e.matmul(
                    ps[:, :w], lhsT=IQ[hp:hp + 64, h // 2, :], rhs=ikT[hp:hp + 64, k0:k0 + w], start=True, stop=True),
                    reads=[IQ_tl, ikT_tl], writes=[pst])
                ri = rk[0] % NR
                rk[0] += 1
                P.op("dve", lambda e: e.tensor_scalar(out=Rr[ri][:, :w], in0=ps[:, :w], scalar1=0.0,
                                                      scalar2=None, op0=ALU.max),
                     reads=[pst], writes=[Rr_tl[ri]])
                st[t] = ri

            def stage_b(t):
                k0, h = items[t]
                w = min(512, nk - k0)
                ri = st.pop(t)
                scp, scpt = self.bank(B_SC)
                P.op("pe", lambda e: e.matmul(
                    scp[:, :w], lhsT=diag[:, h, :], rhs=Rr[ri][:, :w], start=(h == 0), stop=(h == 15)),
                    reads=[diag_tl[h], Rr_tl[ri]], writes=[scpt])
                if h == 15:
                    P.op("act", lambda e: e.activation(out=score[:, k0:k0 + w], in_=scp[:, :w], func=AF.Copy),
                         reads=[scpt], writes=[score_tl])

            LA = 4
            for t in range(len(items) + LA):
                if t < len(items):
                    stage_a(t)
                if t - LA >= 0:
                    stage_b(t - LA)

        def s1b(qb):
            nkb = nob + qb
            nk = nkb * 128
            P.op("dve", lambda e: e.tensor_reduce(out=lo, in_=score[:, :nk], axis=mybir.AxisListType.X, op=ALU.min),
                 reads=[score_tl], writes=[sm_tl["lo"]])
            P.op("dve", lambda e: e.tensor_reduce(out=hi, in_=score[:, :nk], axis=mybir.AxisListType.X, op=ALU.max),
                 reads=[score_tl], writes=[sm_tl["hi"]])
            P.op("dve", lambda e: e.tensor_tensor(out=score[:, :nk], in0=score[:, :nk], in1=kval[:, :nk], op=ALU.add),
                 reads=[score_tl, kval_tl], writes=[score_tl])
            P.op("dve", lambda e: e.tensor_tensor(out=score[:, nk - 128:nk], in0=score[:, nk - 128:nk], in1=tri[:, :],
                                                  op=ALU.add),
                 reads=[score_tl, tri_tl], writes=[score_tl])
            P.op("dve", lambda e: e.tensor_tensor(out=rng, in0=hi, in1=lo, op=ALU.subtract),
                 reads=[sm_tl["lo"], sm_tl["hi"]], writes=[sm_tl["rng"]])
            P.op("dve", lambda e: e.tensor_scalar(out=hsx, in0=pw, scalar1=rng, scalar2=None, op0=ALU.mult),
                 reads=[sm_tl["rng"], self.smalls_tl], writes=[sm_tl["hsx"]])
            P.op("dve", lambda e: e.tensor_tensor(out=mid, in0=lo, in1=hsx[:, 1:2], op=ALU.add),
                 reads=[sm_tl["lo"], sm_tl["hsx"]], writes=[sm_tl["mid"]])
            for s in range(nsteps):
                P.op("dve", lambda e: e.tensor_scalar(
                    out=M[:, :nk], in0=score[:, :nk], scalar1=mid, scalar2=0.0, op0=ALU.is_ge, op1=ALU.add,
                    accum_out=cnt),
                    reads=[score_tl, sm_tl["mid"]], writes=[M_tl, sm_tl["cnt"]])
                last = (s == nsteps - 1)
                ha = hsx[:, s + 1:s + 2] if not last else hsx[:, s + 1:s + 2]
                hb = hsx[:, s + 2:s + 3] if not last else hsx[:, s + 1:s + 2]
                P.op("dve", lambda e, ha=ha: e.scalar_tensor_tensor(out=u, in0=cnt, scalar=255.5, in1=ha,
                                                                    op0=ALU.is_ge, op1=ALU.mult),
                     reads=[sm_tl["cnt"], sm_tl["hsx"]], writes=[sm_tl["u"]])
                P.op("dve", lambda e, hb=hb: e.scalar_tensor_tensor(out=mid, in0=mid, scalar=hb, in1=u,
                                                                    op0=ALU.subtract, op1=ALU.add),
                     reads=[sm_tl["mid"], sm_tl["u"], sm_tl["hsx"]], writes=[sm_tl["mid"]])
            P.op("dve", lambda e: e.tensor_scalar(out=M[:, :nk], in0=score[:, :nk], scalar1=mid, scalar2=None, op0=ALU.is_ge),
                 reads=[score_tl, sm_tl["mid"]], writes=[M_tl])

        def s1c(qb):
            nkb = nob + qb
            MT, MT_tl = MT2[qb % 2], MT2_tl[qb % 2]
            for kb0 in range(0, nkb, 4):
                nb = min(4, nkb - kb0)
                trp, trpt = self.bank(B_SC)
                trb = trp.bitcast(BF16)
                for j in range(nb):
                    kb = kb0 + j
                    P.op("pe", lambda e, trb=trb, j=j, kb=kb: e.transpose(
                        out=trb[:, j * 128:(j + 1) * 128], in_=M[:, kb * 128:(kb + 1) * 128], identity=self.ident),
                        reads=[M_tl, self.cbf_tl], writes=[trpt])
                P.op("act", lambda e, trb=trb, kb0=kb0, nb=nb: e.activation(
                    out=MT[:, kb0:kb0 + nb, :], in_=trb[:, :nb * 128].rearrange("p (b q) -> p b q", b=nb),
                    func=AF.Identity, bias=self.negc, scale=30000.0),
                    reads=[trpt, self.smalls_tl], writes=[MT_tl])

        def s2(qb):
            c0 = qb * 128
            nkb = nob + qb
            nk = nkb * 128
            QT, QT_tl = QT2[qb % 2], QT2_tl[qb % 2]
            MT, MT_tl = MT2[qb % 2], MT2_tl[qb % 2]
            ops_, opst = self.bank(B_O)
            dps, dpst = self.bank(B_DEN)
            items = [(g, kb) for g in range(4) for kb in range(nkb)]
            st = {}
            kv = {}

            def stage_a(t):
                g, kb = items[t]
                if kb % 4 == 0:
                    k0 = kb * 128
                    w = min(512, nk - k0)
                    nb = w // 128
                    ki = rk[1] % 4
                    rk[1] += 1
                    if k0 < own:
                        ksrc = LOWK[:, g, k0:k0 + w]
                        vsrc = LOWV[:, k0 // 128:k0 // 128 + nb, g * 128:(g + 1) * 128]
                    else:
                        ksrc = OWNK[:, g, k0 - own:k0 - own + w]
                        vsrc = OWNV[:, (k0 - own) // 128:(k0 - own) // 128 + nb, g * 128:(g + 1) * 128]
                    self.dma(Kr[ki][:, :w], ksrc, writes=[Kr_tl[ki]])
                    self.dma(Vr[ki][:, :nb, :], vsrc, writes=[Vr_tl[ki]])
                    kv[(g, kb // 4)] = ki
                ki = kv[(g, kb // 4)]
                j = kb % 4
                stp, stpt = self.bank(B_ST[rk[2] % 5])
                rk[2] += 1
                P.op("pe", lambda e: e.matmul(
                    stp[:, :], lhsT=Kr[ki][:, j * 128:(j + 1) * 128],
                    rhs=QT[:, 4 * g:4 * g + 4, :], start=True, stop=False),
                    reads=[Kr_tl[ki], QT_tl], writes=[stpt])
                P.op("pe", lambda e: e.matmul(
                    stp[:, :].rearrange("p (r q) -> p r q", r=4), lhsT=self.ident,
                    rhs=MT[:, kb, :].unsqueeze(1).to_broadcast([128, 4, 128]), start=False, stop=True),
                    reads=[self.cbf_tl, MT_tl], writes=[stpt])
                ei = rk[3] % NE
                rk[3] += 1
                P.op("act", lambda e: e.activation(out=ETr[ei][:, :], in_=stp[:, :], func=AF.Exp,
                                                   scale=float(128 ** -0.5)),
                     reads=[stpt], writes=[ETr_tl[ei]])
                st[t] = (ei, ki, j)

            def stage_b(t):
                g, kb = items[t]
                ei, ki, j = st.pop(t)
                P.op("pe", lambda e: e.matmul(
                    ops_[:, :], lhsT=Vr[ki][:, j, :], rhs=ETr[ei][:, :], start=(kb == 0), stop=(kb == nkb - 1)),
                    reads=[Vr_tl[ki], ETr_tl[ei]], writes=[opst])
                P.op("pe", lambda e: e.matmul(
                    dps[:, :], lhsT=self.ones, rhs=ETr[ei][:, :], start=(kb == 0), stop=(kb == nkb - 1)),
                    reads=[self.cbf_tl, ETr_tl[ei]], writes=[dpst])
                if kb == nkb - 1:
                    oi = rk[4] % 2
                    rk[4] += 1
                    P.op("act", lambda e: e.activation(out=lnd[oi][:, :], in_=dps[:, :], func=AF.Ln,
                                                       bias=self.tinyc, scale=1.0),
                         reads=[dpst, self.smalls_tl], writes=[lnd_tl[oi]])
                    P.op("act", lambda e: e.activation(out=osb[oi][:, :], in_=ops_[:, :], func=AF.Copy),
                         reads=[opst], writes=[osb_tl[oi]])
                    P.op("act", lambda e: e.activation(out=lnd[oi][:, :], in_=lnd[oi][:, :], func=AF.Exp, scale=-1.0),
                         reads=[lnd_tl[oi]], writes=[lnd_tl[oi]])
                    P.op("pool", lambda e: e.tensor_tensor(out=OTs[oi][:, :], in0=osb[oi][:, :], in1=lnd[oi][:, :],
                                                           op=ALU.mult),
                         reads=[osb_tl[oi], lnd_tl[oi]], writes=[OTs_tl[oi]])
                    self.dma(D_["oT"][:, 4 * g:4 * g + 4, c0:c0 + 128], OTs[oi][:, :].rearrange("p (r q) -> p r q", r=4),
                             reads=[OTs_tl[oi]], writes=[self.dT(("oT", g, qb))], eng="pool")

            LA = 4
            for t in range(len(items) + LA):
                if t < len(items):
                    stage_a(t)
                if t - LA >= 0:
                    stage_b(t - LA)

        s1a(0)
        s1b(0)
        s1c(0)
        for qb in range(nqb):
            if qb + 1 < nqb:
                s1a(qb + 1)
                s1b(qb + 1)
            s2(qb)
            if qb + 1 < nqb:
                s1c(qb + 1)

    def dsa_out(self, t0, TT):
        P = self.P
        X, Xtl, H, Htl = self.X, self.X_tl, self.H, self.H_tl
        rd = [self.dT(("oT", g, qb)) for g in range(4) for qb in range(t0 // 128, (t0 + TT) // 128)]
        self.dma(H[:, :, HALO:HALO + TT], self.dsa["oT"][:, :, t0:t0 + TT], reads=rd, writes=Htl, eng="pool")
        ins = [H[:, c, HALO:HALO + TT] for c in range(DC)]

        def ep(n, ps, pst):
            P.op("dve", lambda e: e.tensor_tensor(out=X[:, n, :TT], in0=X[:, n, :TT], in1=ps, op=ALU.add),
                 reads=[pst, Xtl[n]], writes=[Xtl[n]])

        self.linear("wout", DC, DC, 4, ins, Htl, TT, ep)


def vec_layout():
    VJ = {}
    j = 0
    for name, n in (("norm_mix", 4), ("norm_mlp", 4), ("pool_scale", 2), ("b_pw1a", 1), ("b_pw1g", 1),
                    ("b_dw", 1), ("ln_g", 1), ("ln_b", 1), ("b_pw2", 1), ("norm_final", 1)):
        VJ[name] = j
        j += n
    return VJ, j


def fm(v):
    return np.ascontiguousarray(np.asarray(v, np.float32).reshape(16, 128).T)


def weight_specs(layers):
    specs = []
    for l in layers:
        k, j = l % 3, l // 3
        if k == 0:
            specs.append((f"pool{j}", 1, 8192))
        elif k == 2:
            specs.append(("pw1", 16, 4096))
            specs.append(("pw2", 4, 8192))
        specs.append((f"up{l}", 16, 8192))
        specs.append((f"dn{l}", 16, 8192))
    return specs


def dsa_tensors(B, mode):
    nt = B.nt
    own = nt - PRE
    nq = own + 512

    def exv(EX):
        return {"kT": EX[:, 0:4 * own].rearrange("p (g t) -> p g t", g=4),
                "v": EX[:, 4 * own:8 * own].rearrange("p (b c) -> p b c", b=own // 128),
                "ikT": EX[:, 8 * own:9 * own]}
    EX = B.dscr("EX", [128, 9 * own], BF16)
    EXL = B.dscr("EXL", [128, 9 * own], BF16)
    D_ = {"EX": EX, "low": EXL, "lowv": exv(EXL)}
    D_.update(exv(EX))
    D_["qT"] = B.dscr("qT", [128, 16, nq], BF16)
    D_["iqT"] = B.dscr("iqT", [128, 8, nq], BF16)
    D_["iw"] = B.dscr("iw", [128, nq // 128, 16], F32)
    D_["oT"] = B.dscr("oT", [128, 16, nt], BF16)
    B.din("kvalid", [128, 2 * own], BF16)
    B.din("tri", [128, 128], BF16)
    for nm in ("cos128", "sin128", "cos64", "sin64"):
        B.din(nm, [128, 2 * own])
    return D_


def mode_layers(mode):
    return {"A": (0, 1), "B": (1, 2, 3), "fused": (0, 1, 2, 3)}[mode]


def mode_specs(mode):
    specs = []
    if mode in ("A", "fused"):
        specs += [("pool0", 1, 8192), ("up0", 16, 8192), ("dn0", 16, 8192),
                  ("win", 29, 4096), ("wv", 1, 8192), ("wiw", 1, 256)]
    if mode in ("B", "fused"):
        specs += [("wout", 4, 8192), ("up1", 16, 8192), ("dn1", 16, 8192),
                  ("pw1", 16, 4096), ("pw2", 4, 8192), ("up2", 16, 8192), ("dn2", 16, 8192),
                  ("pool1", 1, 8192), ("up3", 16, 8192), ("dn3", 16, 8192)]
    return specs


def build_program(mode="fused", nt=NT, upto=3, final=True, ncores=NCORES):
    B = Builder(nt=nt, mode=mode, ncores=ncores)
    own = B.own
    B.arena_init()
    B.VJ, nv = vec_layout()
    B.NVEC = nv * 16
    B.wscr = {}
    B.load_consts()
    B.dsa = dsa_tensors(B, mode)
    xin = B.din("xin", [128, DC, 2 * own])
    x1 = B.dscr("x1", [128, DC, 2 * own], F32)
    xs = [B.dscr("xsA", [128, DC, nt], F32), B.dscr("xsB", [128, DC, nt], F32)]
    out = B.dout("out", [128, DC, own])
    B.cast_weights(mode_specs("fused"))
    B.barrier()
    B.tile_bufs()
    for ti, (t0, TT) in enumerate(B.tiles0):
        B.load_x(xin, t0, TT)
        icol = 66 if ti == 0 else (1 if t0 == own else None)
        B.pool_mixer(0, 0, ti, TT, icol=icol)
        B.mlp(0, TT)
        B.store_x(x1, t0, TT)
    B.dsa_proj(x1)
    B.barrier()
    B.dsa_attn()
    B.barrier()
    B.tile_bufs()

    def layer_loop(l, src, dst, last, off=0):
        kind = l % 3
        for ti, (t0, TT) in enumerate(B.tiles):
            B.load_x(src, t0, TT, off=off)
            if ti == 0:
                B.mask_pre(B.X[:, :, :TT], B.X_tl)
            if kind == 0:
                B.pool_mixer(l, l // 3, ti, TT, icol=(1 if ti == 1 else None))
            elif kind == 2:
                B.conv_mixer(l, ti, TT)
            else:
                B.dsa_out(t0, TT)
            B.mlp(l, TT)
            if last:
                if ti == 0:
                    continue
                if final:
                    B.final_norm(TT)
                B.store_x(dst, t0, TT, c0=PRE)
            else:
                B.store_x(dst, t0, TT)

    seq = [(1, x1, xs[0], own - PRE), (2, xs[0], xs[1], 0), (3, xs[1], out, 0)]
    seq = [s for s in seq if s[0] <= upto]
    for i, (l, s, d, off) in enumerate(seq):
        lastl = (i == len(seq) - 1)
        layer_loop(l, s, out if lastl else d, lastl, off=off)
    B.P.emit()
    return B


def prep_common(inp, layers):
    VJ, nv = vec_layout()
    vecs = np.zeros((128, nv * 16), np.float32)

    def put(name, idx, v):
        j = VJ[name] + idx
        vecs[:, j * 16:(j + 1) * 16] = fm(v)

    for i in range(4):
        put("norm_mix", i, inp["norm_mix"][i])
        put("norm_mlp", i, inp["norm_mlp"][i])
    for i in range(2):
        put("pool_scale", i, inp["pool_scale"][i])
    put("b_pw1a", 0, inp["conv_b_pw1"][0][:D])
    put("b_pw1g", 0, inp["conv_b_pw1"][0][D:])
    put("b_dw", 0, inp["conv_b_dw"][0])
    put("ln_g", 0, inp["conv_ln_g"][0])
    put("ln_b", 0, inp["conv_ln_b"][0])
    put("b_pw2", 0, inp["conv_b_pw2"][0])
    put("norm_final", 0, inp["norm_final"])
    cbf = np.zeros((128, 384), np.float32)
    cbf[:, 0:128] = np.eye(128, dtype=np.float32)
    cbf[:, 128:256] = 1.0 / 2048.0
    cbf[:, 256:384] = 1.0
    cbf = cbf.astype(ml_dtypes.bfloat16)
    wdw = np.asarray(inp["conv_w_dw"][0], np.float32)
    convw = np.ascontiguousarray(wdw.reshape(CONVW, 16, 128).transpose(2, 1, 0)).reshape(128, 16 * CONVW)
    com = {"vecs": vecs, "cbf": cbf, "convw": convw}
    for l in layers:
        k, j = l % 3, l // 3
        if k == 0:
            w = np.asarray(inp["pool_w"][j], np.float32)
            com[f"pool{j}_f"] = np.ascontiguousarray(
                w.reshape(4, 4, 128, 512).transpose(2, 0, 1, 3)).reshape(128, 8192)
        elif k == 2:
            w = np.asarray(inp["conv_w_pw1"][0], np.float32)
            w4 = w.reshape(16, 128, 2, 16, 128)
            com["pw1_f"] = np.ascontiguousarray(w4.transpose(1, 3, 0, 2, 4)).reshape(128, 16 * 4096)
            w = np.asarray(inp["conv_w_pw2"][0], np.float32)
            com["pw2_f"] = np.ascontiguousarray(
                w.reshape(16, 128, 4, 512).transpose(1, 2, 0, 3)).reshape(128, 4 * 8192)
        w = np.asarray(inp["mlp_up"][l], np.float32)
        com[f"up{l}_f"] = np.ascontiguousarray(
            w.reshape(16, 128, 16, 512).transpose(1, 2, 0, 3)).reshape(128, 16 * 8192)
        w = np.asarray(inp["mlp_down"][l], np.float32)
        com[f"dn{l}_f"] = np.ascontiguousarray(
            w.reshape(64, 128, 16, 128).transpose(1, 2, 0, 3)).reshape(128, 16 * 8192)
    return com


def core_tokens(x, k, nt=NT):
    b, half = k // 2, k % 2
    own = np.asarray(x[b, half * OWN: half * OWN + (nt - PRE)], np.float32)
    if half == 1:
        pre = np.asarray(x[b, half * OWN - PRE: half * OWN], np.float32)
    else:
        pre = np.zeros((PRE, D), np.float32)
    return np.concatenate([pre, own], axis=0)


def to_fm(a):
    n = a.shape[0]
    return np.ascontiguousarray(a.T.reshape(16, 128, n).transpose(1, 0, 2))


def from_fm(a):
    n = a.shape[2]
    return np.ascontiguousarray(a.transpose(1, 0, 2).reshape(2048, n).T)


def smalls_for(k):
    half = k % 2
    s = np.zeros((128, 192), np.float32)
    s[:, 0] = float(half)
    s[:, 65] = EPS
    s[:, 130] = 1e-18
    s[:, 131] = -30000.0
    for j in range(40):
        s[:, 132 + j] = 2.0 ** (-j)
    for g, w in enumerate(POOL_W):
        for t in range(16):
            start = 1.0 / min(t + 1, w)
            s[:, 1 + 16 * g + t] = start if half == 0 else 1.0 / w
            s[:, 66 + 16 * g + t] = start
    return s


def rope_tables(k, nt=NT):
    half = k % 2
    own = nt - PRE
    pos = (np.arange(2 * own) - (1 - half) * own).astype(np.float32)
    out = {}
    for nm, d in (("128", 128), ("64", 64)):
        inv = (np.float32(10000.0) ** (-np.arange(0, d, 2, dtype=np.float32) / np.float32(d))).astype(np.float32)
        ang = (pos[:, None] * inv[None, :]).astype(np.float32)
        cos = np.cos(ang).astype(np.float32)
        sin = np.sin(ang).astype(np.float32)
        dd = np.arange(128) % d
        idx = dd % (d // 2)
        sign = np.where(dd < d // 2, -1.0, 1.0).astype(np.float32)
        out["cos" + nm] = np.ascontiguousarray(cos[:, idx].T)
        out["sin" + nm] = np.ascontiguousarray((sin[:, idx] * sign[None, :]).T)
    return out


def dsa_weights(inp):
    w = np.asarray(inp["dsa_w_in"][0], np.float32)
    cols = []
    d = np.arange(128)
    for h in range(16):
        cols.append(h * 128 + d)
        cols.append(h * 128 + (d + 64) % 128)
    for g in range(4):
        cols.append(2048 + g * 128 + d)
        cols.append(2048 + g * 128 + (d + 64) % 128)
    for m in range(8):
        head = 2 * m + d // 64
        dd = d % 64
        cols.append(3072 + head * 64 + dd)
        cols.append(3072 + head * 64 + (dd + 32) % 64)
    dd = d % 64
    cols.append(4096 + dd)
    cols.append(4096 + (dd + 32) % 64)
    cols = np.concatenate(cols)
    wp = w[:, cols]
    win = np.ascontiguousarray(wp.reshape(16, 128, 29, 256).transpose(1, 2, 0, 3)).reshape(128, 29 * 4096)
    wv = np.ascontiguousarray(w[:, 2560:3072].reshape(16, 128, 512).transpose(1, 0, 2)).reshape(128, 8192)
    wiw = np.ascontiguousarray(w[:, 4160:4176].reshape(16, 128, 16).transpose(1, 0, 2)).reshape(128, 256)
    wo = np.asarray(inp["dsa_w_out"][0], np.float32)
    wout = np.ascontiguousarray(wo.reshape(16, 128, 4, 512).transpose(1, 2, 0, 3)).reshape(128, 4 * 8192)
    return {"win_f": win, "wv_f": wv, "wiw_f": wiw, "wout_f": wout}


def attn_masks(k, nt=NT):
    half = k % 2
    own = nt - PRE
    kv = np.zeros((128, 2 * own), np.float32)
    if half == 0:
        kv[:, :own] = NEG
    q = np.arange(128)[:, None]
    s = np.arange(128)[None, :]
    tri = np.where(s <= q, 0.0, NEG).astype(np.float32)
    return {"kvalid": kv.astype(ml_dtypes.bfloat16), "tri": tri.astype(ml_dtypes.bfloat16)}


def kernel(**inp):
    inp = {k: np.asarray(v) for k, v in inp.items()}
    com = prep_common(inp, (0, 1, 2, 3))
    com.update(dsa_weights(inp))
    x = inp["x"]
    maps = []
    for k in range(NCORES):
        b, half = k // 2, k % 2
        m = dict(com)
        seq = np.asarray(x[b], np.float32)
        if half == 0:
            seq = np.concatenate([np.zeros((OWN, D), np.float32), seq[:OWN]], axis=0)
        m["xin"] = to_fm(seq)
        m["smalls"] = smalls_for(k)
        m.update(rope_tables(k))
        m.update(attn_masks(k))
        maps.append(m)
    cores = list(range(NCORES))
    BF = build_program("fused")
    res = run_bass_kernel_spmd(BF.nc, [{n: m[n] for n in BF.in_names} for m in maps], core_ids=cores)
    out = np.empty((4, SEQ, D), np.float32)
    for k in cores:
        b, half = k // 2, k % 2
        out[b, half * OWN:(half + 1) * OWN] = from_fm(np.asarray(res.results[k]["out"]))
    return out
```

```python
import numpy as np
import ml_dtypes
import concourse.bass as bass
import concourse.mybir as mybir
from concourse.bass_utils import run_bass_kernel_spmd

F32 = mybir.dt.float32
BF16 = mybir.dt.bfloat16
AF = mybir.ActivationFunctionType
ALU = mybir.AluOpType

D = 2048
DC = 16
DFF = 8192
FC = 64
SEQ = 8192
OWN = 4096
PRE = 128
NT = OWN + PRE
HALO = 32
EPS = 1e-6
NEG = -1.0e30
POOL_W = (2, 4, 8, 16)
CONVW = 31
NCORES = 8


class Tl:
    __slots__ = ("name", "last_w", "readers")

    def __init__(self, name=""):
        self.name = name
        self.last_w = None
        self.readers = {}


class Op:
    __slots__ = ("eng", "fn", "deps", "dma", "idx")


ENG_BLOCK = {"pe": "tensor", "act": "scalar", "dve": "vector", "pool": "gpsimd", "sp": "sync"}


class Prog:
    NDS = 8

    def __init__(self, nc):
        self.nc = nc
        self.ops = []
        self.dma_since_barrier = []

    def op(self, eng, fn, reads=(), writes=(), dma=False, extra_deps=()):
        idx = len(self.ops)
        deps = set(extra_deps)
        for t in reads:
            if t.last_w is not None:
                deps.add(t.last_w)
        for t in writes:
            if t.last_w is not None:
                deps.add(t.last_w)
            deps.update(t.readers.values())
        key = ("d", idx) if dma else eng
        for t in reads:
            t.readers[key] = idx
        for t in writes:
            t.last_w = idx
            t.readers = {}
        deps.discard(idx)
        o = Op()
        o.eng = eng
        o.fn = fn
        o.dma = dma
        o.idx = idx
        ops = self.ops
        if eng == "pe" and not dma:
            o.deps = [d for d in deps if not (ops[d].eng == "pe" and not ops[d].dma)]
        else:
            o.deps = list(deps)
        ops.append(o)
        if dma:
            self.dma_since_barrier.append(idx)
        return idx

    def emit(self):
        nc = self.nc
        ops = self.ops
        need = [False] * len(ops)
        for o in ops:
            for d in o.deps:
                need[d] = True
        engs = []
        for o in ops:
            if o.eng not in engs:
                engs.append(o.eng)
        csem = {}
        ccnt = {}
        dsem = {}
        dcnt = {}
        drr = {}
        for e in engs:
            csem[e] = nc.alloc_semaphore(name=f"c_{e}")
            ccnt[e] = 0
            dsem[e] = [nc.alloc_semaphore(name=f"d_{e}_{i}") for i in range(self.NDS)]
            drr[e] = 0
            for i in range(self.NDS):
                dcnt[(e, i)] = 0
        sig = {}
        pre_wait = {}
        for o in ops:
            if o.dma:
                i = drr[o.eng]
                drr[o.eng] = (i + 1) % self.NDS
                prev = dcnt[(o.eng, i)]
                pre_wait[o.idx] = (dsem[o.eng][i], prev)
                dcnt[(o.eng, i)] = prev + 16
                sig[o.idx] = (dsem[o.eng][i], prev + 16)
            elif need[o.idx]:
                ccnt[o.eng] += 1
                sig[o.idx] = (csem[o.eng], ccnt[o.eng])
        self.max_counts = dict(ccnt)
        final_dma = [(dsem[e][i], dcnt[(e, i)]) for e in engs for i in range(self.NDS) if dcnt[(e, i)] > 0]
        with nc.Block() as block:
            for e in engs:
                my = [o for o in ops if o.eng == e]

                def body(eng, my=my, e=e):
                    waited = {}

                    def wait(sem, val):
                        if val <= 0:
                            return
                        k = sem.num
                        if waited.get(k, 0) >= val:
                            return
                        eng.wait_ge(sem, val)
                        waited[k] = val

                    for o in my:
                        for d in sorted(o.deps):
                            wait(*sig[d])
                        if o.dma:
                            wait(*pre_wait[o.idx])
                        ins = o.fn(eng)
                        if o.idx in sig:
                            ins.then_inc(sig[o.idx][0], 16 if o.dma else 1)
                    if e == "sp":
                        for s, v in final_dma:
                            wait(s, v)

                getattr(block, ENG_BLOCK[e])(body)


def make_tiles(nt):
    tiles = [(0, PRE)]
    t = PRE
    while t < nt:
        tiles.append((t, 512))
        t += 512
    return tiles


class Builder:
    def __init__(self, nt=NT, layers=(0, 1, 2, 3), mode="full", ncores=NCORES):
        self.ncores = ncores
        self.nt = nt
        self.tiles = make_tiles(nt)
        self.own = nt - PRE
        self.tiles0 = [(512 * i, 512) for i in range(2 * self.own // 512)]
        self.layers = layers
        self.mode = mode
        self.nc = bass.Bass("TRN2", target_bir_lowering=False)
        self.P = Prog(self.nc)
        self.dram = {}
        self.dtl = {}

    def din(self, name, shape, dt=F32):
        t = self.nc.dram_tensor(name, list(shape), dt, kind="ExternalInput")
        self.dram[name] = t
        if not hasattr(self, "in_names"):
            self.in_names = []
        self.in_names.append(name)
        return t

    def dout(self, name, shape, dt=F32):
        t = self.nc.dram_tensor(name, list(shape), dt, kind="ExternalOutput")
        self.dram[name] = t
        return t

    def dscr(self, name, shape, dt):
        t = self.nc.dram_tensor(name, list(shape), dt)
        self.dram[name] = t
        return t

    def dT(self, key):
        if key not in self.dtl:
            self.dtl[key] = Tl(str(key))
        return self.dtl[key]

    def arena_init(self):
        nc = self.nc
        self.ARENA_W = 52800
        self.arena = nc.alloc_sbuf_tensor("arena", [128, self.ARENA_W], F32)
        self.arena_off = 0
        self.psum = nc.alloc_psum_tensor("psum", [128, 4096], F32)
        self.pbank = [Tl(f"bank{i}") for i in range(8)]
        self.pb_rr = 0

    def arena_reset(self, keep=0):
        self.arena_off = keep

    def alloc(self, nbytes, dt=F32):
        words = (nbytes + 3) // 4
        words = (words + 7) // 8 * 8
        a = self.arena[:, self.arena_off:self.arena_off + words]
        self.arena_off += words
        assert self.arena_off <= self.ARENA_W, ("SBUF arena overflow", self.arena_off * 4)
        if dt == BF16:
            a = a.bitcast(BF16)
        return a

    def bank(self, i=None):
        if i is None:
            i = self.pb_rr
            self.pb_rr = (self.pb_rr + 1) % 8
        return self.psum[:, i * 512:(i + 1) * 512], self.pbank[i]

    def barrier(self):
        P = self.P
        nc = self.nc
        if not hasattr(self, "bar_sb"):
            self.bar_sb = nc.alloc_sbuf_tensor("bar_sb", [128, 64], F32)
            self.bar_tl = {e: Tl("bar_" + e) for e in ("act", "dve", "pool", "pe")}
        sb = self.bar_sb
        tl = self.bar_tl
        dmas = list(P.dma_since_barrier)
        P.dma_since_barrier = []
        i1 = P.op("act", lambda e: e.activation(out=sb[:, 0:8], in_=sb[:, 32:40], func=AF.Copy),
                  writes=[tl["act"]], extra_deps=dmas)
        i2 = P.op("dve", lambda e: e.memset(sb[:, 8:16], 0.0), writes=[tl["dve"]], extra_deps=dmas)
        i3 = P.op("pool", lambda e: e.memset(sb[:, 16:24], 0.0), writes=[tl["pool"]], extra_deps=dmas)
        pb, pbt = self.bank(7)
        i4 = P.op("pe", lambda e: e.matmul(pb[:, 0:8], lhsT=self.ident[:, 0:128], rhs=self.ident[:, 0:8],
                                           start=True, stop=True),
                  reads=[self.cbf_tl], writes=[pbt, tl["pe"]], extra_deps=dmas)
        allb = [i1, i2, i3, i4]
        P.op("act", lambda e: e.activation(out=sb[:, 0:8], in_=sb[:, 32:40], func=AF.Copy),
             writes=[tl["act"]], extra_deps=allb)
        P.op("dve", lambda e: e.memset(sb[:, 8:16], 0.0), writes=[tl["dve"]], extra_deps=allb)
        P.op("pool", lambda e: e.memset(sb[:, 16:24], 0.0), writes=[tl["pool"]], extra_deps=allb)
        P.op("pe", lambda e: e.matmul(pb[:, 0:8], lhsT=self.ident[:, 0:128], rhs=self.ident[:, 0:8],
                                      start=True, stop=True),
             writes=[pbt, tl["pe"]], extra_deps=allb)
        self.sp_fence = allb

    def dma(self, out, in_, reads=(), writes=(), eng="sp"):
        fence = getattr(self, "sp_fence", ())
        return self.P.op(eng, lambda e: e.dma_start(out=out, in_=in_), reads=reads, writes=writes,
                         dma=True, extra_deps=fence)

    def load_consts(self):
        P = self.P
        nc = self.nc
        vec_in = self.din("vecs", [128, self.NVEC])
        self.vecs = self.alloc(self.NVEC * 4)
        self.vecs_tl = Tl("vecs")
        self.dma(self.vecs, vec_in[:, :], writes=[self.vecs_tl])
        cb_in = self.din("cbf", [128, 384], BF16)
        self.cbf = self.alloc(384 * 2, BF16)
        self.cbf_tl = Tl("cbf")
        self.dma(self.cbf, cb_in[:, :], writes=[self.cbf_tl])
        self.ident = self.cbf[:, 0:128]
        self.onesm = self.cbf[:, 128:256]
        self.ones = self.cbf[:, 256:384]
        cw_in = self.din("convw", [128, 16 * CONVW])
        self.convw = self.alloc(16 * CONVW * 4)
        self.convw_tl = Tl("convw")
        self.dma(self.convw, cw_in[:, :], writes=[self.convw_tl])
        sm_in = self.din("smalls", [128, 192])
        self.smalls = self.alloc(192 * 4)
        self.smalls_tl = Tl("smalls")
        self.dma(self.smalls, sm_in[:, :], writes=[self.smalls_tl])
        self.pm = self.smalls[:, 0:1]
        self.epsc = self.smalls[:, 65:66]
        self.tinyc = self.smalls[:, 130:131]
        self.negc = self.smalls[:, 131:132]
        self.const_end = self.arena_off

    def vec(self, j, c):
        return self.vecs[:, j * 16 + c: j * 16 + c + 1]

    def cast_weights(self, specs):
        P = self.P
        CH = 4096
        NB = 3
        stage = [self.alloc(CH * 4) for _ in range(NB)]
        stage_tl = [Tl(f"stg{i}") for i in range(NB)]
        outb = [self.alloc(CH * 2, BF16) for _ in range(NB)]
        outb_tl = [Tl(f"cst{i}") for i in range(NB)]
        k = 0
        for name, nblk, E in specs:
            src = self.din(name + "_f", [128, nblk * E])
            dst = self.dscr(name, [128, nblk * E], BF16)
            self.wscr[name] = (dst, nblk, E)
            tot = nblk * E
            for o in range(0, tot, CH):
                w = min(CH, tot - o)
                i = k % NB
                self.dma(stage[i][:, :w], src[:, o:o + w], writes=[stage_tl[i]])
                ce = ("dve", "act", "pool")[k % 3]
                so, oo = stage[i][:, :w], outb[i][:, :w]
                if ce == "act":
                    P.op("act", lambda e, so=so, oo=oo: e.activation(out=oo, in_=so, func=AF.Copy),
                         reads=[stage_tl[i]], writes=[outb_tl[i]])
                elif ce == "dve":
                    P.op("dve", lambda e, so=so, oo=oo: e.tensor_copy(out=oo, in_=so),
                         reads=[stage_tl[i]], writes=[outb_tl[i]])
                else:
                    P.op("pool", lambda e, so=so, oo=oo: e.tensor_copy(out=oo, in_=so),
                         reads=[stage_tl[i]], writes=[outb_tl[i]])
                self.dma(dst[:, o:o + w], outb[i][:, :w], reads=[outb_tl[i]],
                         writes=[self.dT((name, "c", o // CH))])
                k += 1

    def wring_init(self, nbuf=3, E=8192):
        self.wr = [self.alloc(E * 2, BF16) for _ in range(nbuf)]
        self.wr_tl = [Tl(f"wr{i}") for i in range(nbuf)]
        self.wr_i = 0

    def wload(self, name, blk):
        dst, nblk, E = self.wscr[name]
        i = self.wr_i
        self.wr_i = (self.wr_i + 1) % len(self.wr)
        buf = self.wr[i][:, :E]
        rd = [self.dT((name, "c", q // 4096)) for q in range(blk * E, (blk + 1) * E, 4096)]
        self.dma(buf, dst[:, blk * E:(blk + 1) * E], reads=rd, writes=[self.wr_tl[i]])
        return buf, self.wr_tl[i]

    def linear(self, wname, KC, nchunks, NBC, ins, ins_tl, TT, epilogue, blk0=0):
        P = self.P
        nb = NBC * 128
        for b in range(nchunks // NBC):
            wt, wtl = self.wload(wname, blk0 + b)
            wv = wt.rearrange("p (k n) -> p k n", k=KC)
            for jj in range(NBC):
                n = b * NBC + jj
                ps, pst = self.bank()
                for kc in range(KC):
                    P.op("pe", lambda e, ps=ps, wv=wv, kc=kc, jj=jj, r=ins[kc]: e.matmul(
                        ps[:, :TT], lhsT=wv[:, kc, jj * 128:(jj + 1) * 128], rhs=r,
                        start=(kc == 0), stop=(kc == KC - 1)),
                        reads=[wtl, ins_tl[kc]], writes=[pst])
                epilogue(n, ps[:, :TT], pst)

    def rmsnorm(self, TT, gj, out_ap_fn, out_tl_fn, sq_view, sq_tl):
        P = self.P
        X, Xtl = self.X, self.X_tl
        sqs = [sq_view(c)[:, :TT] for c in range(DC)]
        sqt = [sq_tl(c) for c in range(DC)]
        oaps = [out_ap_fn(c) for c in range(DC)]
        otls = [out_tl_fn(c) for c in range(DC)]
        for c in range(DC):
            P.op("act", lambda e, c=c: e.activation(out=sqs[c], in_=X[:, c, :TT], func=AF.Square),
                 reads=[Xtl[c]], writes=[sqt[c]])
        ps, pst = self.bank()
        for c in range(DC):
            P.op("pe", lambda e, c=c, ps=ps: e.matmul(ps[:, :TT], lhsT=self.onesm, rhs=sqs[c],
                                                      start=(c == 0), stop=(c == DC - 1)),
                 reads=[sqt[c], self.cbf_tl], writes=[pst])
        rs = self.rstd
        P.op("act", lambda e, ps=ps: e.activation(out=rs[:, :TT], in_=ps[:, :TT], func=AF.Sqrt,
                                                  bias=self.epsc, scale=1.0),
             reads=[pst, self.smalls_tl], writes=[self.rstd_tl])
        P.op("dve", lambda e: e.reciprocal(out=rs[:, :TT], in_=rs[:, :TT]),
             reads=[self.rstd_tl], writes=[self.rstd_tl])
        for c in range(DC):
            P.op("dve", lambda e, c=c: e.scalar_tensor_tensor(
                out=oaps[c], in0=X[:, c, :TT], scalar=self.vec(gj, c), in1=rs[:, :TT],
                op0=ALU.mult, op1=ALU.mult),
                reads=[Xtl[c], self.rstd_tl, self.vecs_tl], writes=[otls[c]])

    def mlp(self, l, TT):
        P = self.P
        X, Xtl = self.X, self.X_tl
        H, Htl = self.H, self.H_tl
        G = self.G
        Gtl = self.G_tl

        def gv(j):
            return G[j // 4][:, (j % 4) * 512:(j % 4) * 512 + 512]

        self.rmsnorm(TT, self.VJ["norm_mlp"] + l,
                     lambda c: H[:, c, HALO:HALO + TT], lambda c: Htl[c],
                     lambda c: gv(c), lambda c: Gtl[c // 4])
        ins = [H[:, c, HALO:HALO + TT] for c in range(DC)]

        def ep_up(n, ps, pst):
            r, rtl = self.R[n % 2], self.R_tl[n % 2]
            P.op("act", lambda e: e.activation(out=r[:, :TT], in_=ps, func=AF.Relu),
                 reads=[pst], writes=[rtl])
            P.op("pool" if (n % 2) else "dve",
                 lambda e: e.tensor_tensor(out=gv(n)[:, :TT], in0=r[:, :TT], in1=r[:, :TT], op=ALU.mult),
                 reads=[rtl], writes=[Gtl[n // 4]])

        self.linear(f"up{l}", DC, FC, 4, ins, Htl, TT, ep_up)
        gins = [gv(j)[:, :TT] for j in range(FC)]
        gtl = [Gtl[j // 4] for j in range(FC)]

        def ep_dn(n, ps, pst):
            P.op("dve", lambda e: e.tensor_tensor(out=X[:, n, :TT], in0=X[:, n, :TT], in1=ps, op=ALU.add),
                 reads=[pst, Xtl[n]], writes=[Xtl[n]])

        self.linear(f"dn{l}", FC, DC, 1, gins, gtl, TT, ep_dn)

    def tile_bufs(self):
        self.arena_reset(self.const_end)
        Xf = self.alloc(DC * 512 * 4)
        self.X = Xf.rearrange("p (c t) -> p c t", c=DC)
        self.X_tl = [Tl(f"X{c}") for c in range(DC)]
        Hf = self.alloc(DC * (HALO + 512) * 2, BF16)
        self.H = Hf.rearrange("p (c t) -> p c t", c=DC)
        self.H_tl = [Tl(f"H{c}") for c in range(DC)]
        RB = 4352
        self.G_off = self.arena_off
        self.G = [self.alloc(RB, BF16) for _ in range(16)]
        self.G_tl = [Tl(f"G{i}") for i in range(16)]
        self.rstd = self.alloc(512 * 4)
        self.rstd_tl = Tl("rstd")
        self.tmpA = self.alloc(512 * 4)
        self.tmpA_tl = Tl("tmpA")
        self.tmpB = self.alloc(512 * 4)
        self.tmpB_tl = Tl("tmpB")
        self.UH = self.alloc(DC * HALO * 4).rearrange("p (c t) -> p c t", c=DC)
        self.UH_tl = Tl("UH")
        self.HH = self.alloc(DC * HALO * 2, BF16).rearrange("p (c t) -> p c t", c=DC)
        self.HH_tl = Tl("HH")
        self.R = [self.alloc(512 * 2, BF16) for _ in range(2)]
        self.R_tl = [Tl("R0"), Tl("R1")]
        self.wring_init(3, 8192)

    def load_x(self, src, t0, TT, off=0):
        self.dma(self.X[:, :, :TT], src[:, :, off + t0:off + t0 + TT], reads=[self.dT((src.name, off + t0))],
                 writes=self.X_tl, eng="pool")

    def store_x(self, dst, t0, TT, c0=0):
        self.dma(dst[:, :, t0 - c0:t0 - c0 + TT], self.X[:, :, :TT], reads=self.X_tl,
                 writes=[self.dT((dst.name, t0))], eng="pool")

    def mask_pre(self, ap3, tls):
        P = self.P
        P.op("dve", lambda e: e.tensor_scalar(out=ap3, in0=ap3, scalar1=self.pm, scalar2=None, op0=ALU.mult),
             reads=list(tls) + [self.smalls_tl], writes=list(tls))

    def pool_mixer(self, l, j, ti, TT, icol=None):
        P = self.P
        X, Xtl, H, Htl = self.X, self.X_tl, self.H, self.H_tl
        G, Gtl = self.G, self.G_tl
        if ti == 0:
            P.op("pool", lambda e: e.memset(H[:, :, 0:HALO], 0.0), writes=Htl)
        else:
            P.op("pool", lambda e: e.tensor_copy(out=H[:, :, 0:HALO], in_=self.HH[:, :, :]),
                 reads=[self.HH_tl], writes=Htl)
        self.rmsnorm(TT, self.VJ["norm_mix"] + l,
                     lambda c: H[:, c, HALO:HALO + TT], lambda c: Htl[c],
                     lambda c: G[c][:, 0:512], lambda c: Gtl[c])
        P.op("pool", lambda e: e.tensor_copy(out=self.HH[:, :, :], in_=H[:, :, TT:TT + HALO]),
             reads=Htl, writes=[self.HH_tl])
        W = HALO + TT
        for g in range(4):
            nsteps = g + 1
            w = POOL_W[g]
            for k in range(4):
                r = 4 * g + k
                reg32 = G[r].bitcast(F32)
                rtl = Gtl[r]
                eng = "pool" if (k % 2) else "dve"
                sh = 1
                for s in range(nsteps):
                    lo = 2 * sh - 1
                    dst = reg32[:, (s % 2) * 544:(s % 2) * 544 + 544]
                    if s == 0:
                        a = H[:, r, lo:W]
                        b = H[:, r, lo - sh:W - sh]
                        rd = [Htl[r]]
                    else:
                        srcb = reg32[:, ((s - 1) % 2) * 544:((s - 1) % 2) * 544 + 544]
                        a = srcb[:, lo:W]
                        b = srcb[:, lo - sh:W - sh]
                        rd = [rtl]
                    P.op(eng, lambda e, dst=dst, a=a, b=b, lo=lo: e.tensor_tensor(out=dst[:, lo:W], in0=a, in1=b, op=ALU.add),
                         reads=rd, writes=[rtl])
                    sh *= 2
                fin = reg32[:, ((nsteps - 1) % 2) * 544:((nsteps - 1) % 2) * 544 + 544]
                y = G[r][:, (nsteps % 2) * 1088:(nsteps % 2) * 1088 + 512]
                P.op("dve", lambda e, y=y, fin=fin, r=r, w=w: e.scalar_tensor_tensor(
                    out=y[:, :TT], in0=fin[:, HALO:HALO + TT], scalar=1.0 / w, in1=H[:, r, HALO:HALO + TT],
                    op0=ALU.mult, op1=ALU.subtract),
                    reads=[rtl, Htl[r]], writes=[rtl])
                if icol is not None:
                    ic = self.smalls[:, icol + 16 * g: icol + 16 * g + 16]
                    tmp = self.tmpA[:, 0:16]
                    P.op("dve", lambda e, fin=fin, ic=ic, tmp=tmp: e.tensor_tensor(
                        out=tmp, in0=fin[:, HALO:HALO + 16], in1=ic, op=ALU.mult),
                        reads=[rtl, self.smalls_tl], writes=[self.tmpA_tl])
                    P.op("dve", lambda e, y=y, tmp=tmp, r=r: e.tensor_tensor(
                        out=y[:, 0:16], in0=tmp, in1=H[:, r, HALO:HALO + 16], op=ALU.subtract),
                        reads=[self.tmpA_tl, Htl[r]], writes=[rtl])
        wt, wtl = self.wload(f"pool{j}", 0)
        wv = wt.rearrange("p (g k n) -> p g k n", g=4, k=4)
        for g in range(4):
            nsteps = g + 1
            for jj in range(4):
                n = 4 * g + jj
                ps, pst = self.bank()
                for kc in range(4):
                    y = G[4 * g + kc][:, (nsteps % 2) * 1088:(nsteps % 2) * 1088 + 512]
                    P.op("pe", lambda e, ps=ps, g=g, kc=kc, jj=jj, y=y: e.matmul(
                        ps[:, :TT], lhsT=wv[:, g, kc, jj * 128:(jj + 1) * 128], rhs=y[:, :TT],
                        start=(kc == 0), stop=(kc == 3)),
                        reads=[wtl, Gtl[4 * g + kc]], writes=[pst])
                P.op("dve", lambda e, ps=ps, n=n: e.scalar_tensor_tensor(
                    out=X[:, n, :TT], in0=ps[:, :TT], scalar=self.vec(self.VJ["pool_scale"] + j, n), in1=X[:, n, :TT],
                    op0=ALU.mult, op1=ALU.add),
                    reads=[pst, Xtl[n], self.vecs_tl], writes=[Xtl[n]])

    def conv_mixer(self, l, ti, TT):
        P = self.P
        X, Xtl, H, Htl = self.X, self.X_tl, self.H, self.H_tl
        G, Gtl = self.G, self.G_tl
        VJ = self.VJ
        UW = HALO + 512

        def U(c):
            return G[c].bitcast(F32)[:, 0:UW]

        def ACC(c):
            return G[c].bitcast(F32)[:, UW:UW + 512]

        self.rmsnorm(TT, VJ["norm_mix"] + l,
                     lambda c: H[:, c, HALO:HALO + TT], lambda c: Htl[c],
                     lambda c: G[c][:, 0:512], lambda c: Gtl[c])
        for c in range(DC):
            if ti == 0:
                P.op("pool", lambda e, c=c: e.memset(U(c)[:, 0:HALO], 0.0), writes=[Gtl[c]])
            else:
                P.op("pool", lambda e, c=c: e.tensor_copy(out=U(c)[:, 0:HALO], in_=self.UH[:, c, :]),
                     reads=[self.UH_tl], writes=[Gtl[c]])
        ins = [H[:, c, HALO:HALO + TT] for c in range(DC)]
        state = {}

        def ep_pw1(n, ps, pst):
            c = n // 2
            if n % 2 == 0:
                state["a"] = (ps, pst)
                return
            aps, apst = state["a"]
            sig = self.tmpA
            P.op("act", lambda e: e.activation(out=sig[:, :TT], in_=ps, func=AF.Sigmoid,
                                               bias=self.vec(VJ["b_pw1g"], c), scale=1.0),
                 reads=[pst, self.vecs_tl], writes=[self.tmpA_tl])
            P.op("dve", lambda e: e.scalar_tensor_tensor(
                out=U(c)[:, HALO:HALO + TT], in0=aps, scalar=self.vec(VJ["b_pw1a"], c), in1=sig[:, :TT],
                op0=ALU.add, op1=ALU.mult),
                reads=[apst, self.tmpA_tl, self.vecs_tl], writes=[Gtl[c]])

        self.linear("pw1", DC, 32, 2, ins, Htl, TT, ep_pw1)
        if ti == 0:
            for c in range(DC):
                self.mask_pre(U(c)[:, HALO:HALO + TT], [Gtl[c]])
        P.op("pool", lambda e: e.tensor_copy(
            out=self.UH[:, 0, :], in_=U(0)[:, TT:TT + HALO]), reads=[Gtl[0]], writes=[self.UH_tl])
        for c in range(1, DC):
            P.op("pool", lambda e, c=c: e.tensor_copy(out=self.UH[:, c, :], in_=U(c)[:, TT:TT + HALO]),
                 reads=[Gtl[c], self.UH_tl], writes=[self.UH_tl])
        for c in range(DC):
            u = U(c)
            acc = ACC(c)
            for k in range(CONVW):
                off = HALO - (CONVW - 1) + k
                wcol = self.convw[:, c * CONVW + k: c * CONVW + k + 1]
                if k == 0:
                    P.op("dve", lambda e, u=u, acc=acc, off=off, wcol=wcol, c=c: e.tensor_scalar(
                        out=acc[:, :TT], in0=u[:, off:off + TT], scalar1=wcol, scalar2=self.vec(VJ["b_dw"], c),
                        op0=ALU.mult, op1=ALU.add),
                        reads=[Gtl[c], self.convw_tl, self.vecs_tl], writes=[Gtl[c]])
                else:
                    P.op("dve", lambda e, u=u, acc=acc, off=off, wcol=wcol: e.scalar_tensor_tensor(
                        out=acc[:, :TT], in0=u[:, off:off + TT], scalar=wcol, in1=acc[:, :TT],
                        op0=ALU.mult, op1=ALU.add),
                        reads=[Gtl[c], self.convw_tl], writes=[Gtl[c]])
        for c in range(DC):
            P.op("act", lambda e, c=c: e.activation(out=H[:, c, HALO:HALO + TT], in_=ACC(c)[:, :TT], func=AF.Copy),
                 reads=[Gtl[c]], writes=[Htl[c]])
            P.op("act", lambda e, c=c: e.activation(out=G[c][:, 0:TT], in_=ACC(c)[:, :TT], func=AF.Square),
                 reads=[Gtl[c]], writes=[Gtl[c]])
        psm, psmt = self.bank()
        pss, psst = self.bank()
        for c in range(DC):
            P.op("pe", lambda e, c=c: e.matmul(psm[:, :TT], lhsT=self.onesm, rhs=H[:, c, HALO:HALO + TT],
                                               start=(c == 0), stop=(c == DC - 1)),
                 reads=[Htl[c], self.cbf_tl], writes=[psmt])
        for c in range(DC):
            P.op("pe", lambda e, c=c: e.matmul(pss[:, :TT], lhsT=self.onesm, rhs=G[c][:, 0:TT],
                                               start=(c == 0), stop=(c == DC - 1)),
                 reads=[Gtl[c], self.cbf_tl], writes=[psst])
        mean = self.tmpA
        var = self.tmpB
        rs = self.rstd
        P.op("act", lambda e: e.activation(out=mean[:, :TT], in_=psm[:, :TT], func=AF.Copy),
             reads=[psmt], writes=[self.tmpA_tl])
        P.op("dve", lambda e: e.tensor_tensor(out=var[:, :TT], in0=mean[:, :TT], in1=mean[:, :TT], op=ALU.mult),
             reads=[self.tmpA_tl], writes=[self.tmpB_tl])
        P.op("dve", lambda e: e.tensor_tensor(out=var[:, :TT], in0=pss[:, :TT], in1=var[:, :TT], op=ALU.subtract),
             reads=[psst, self.tmpB_tl], writes=[self.tmpB_tl])
        P.op("act", lambda e: e.activation(out=rs[:, :TT], in_=var[:, :TT], func=AF.Sqrt,
                                           bias=self.epsc, scale=1.0),
             reads=[self.tmpB_tl, self.smalls_tl], writes=[self.rstd_tl])
        P.op("dve", lambda e: e.reciprocal(out=rs[:, :TT], in_=rs[:, :TT]),
             reads=[self.rstd_tl], writes=[self.rstd_tl])
        for c in range(DC):
            acc = ACC(c)
            P.op("dve", lambda e, acc=acc: e.tensor_tensor(out=acc[:, :TT], in0=acc[:, :TT], in1=mean[:, :TT],
                                                           op=ALU.subtract),
                 reads=[Gtl[c], self.tmpA_tl], writes=[Gtl[c]])
            P.op("dve", lambda e, acc=acc, c=c: e.scalar_tensor_tensor(
                out=acc[:, :TT], in0=acc[:, :TT], scalar=self.vec(VJ["ln_g"], c), in1=rs[:, :TT],
                op0=ALU.mult, op1=ALU.mult),
                reads=[Gtl[c], self.rstd_tl, self.vecs_tl], writes=[Gtl[c]])
            P.op("act", lambda e, acc=acc, c=c: e.activation(
                out=H[:, c, HALO:HALO + TT], in_=acc[:, :TT], func=AF.Silu, bias=self.vec(VJ["ln_b"], c), scale=1.0),
                reads=[Gtl[c], self.vecs_tl], writes=[Htl[c]])

        def ep_pw2(n, ps, pst):
            P.op("dve", lambda e: e.scalar_tensor_tensor(
                out=X[:, n, :TT], in0=ps, scalar=self.vec(VJ["b_pw2"], n), in1=X[:, n, :TT],
                op0=ALU.add, op1=ALU.add),
                reads=[pst, Xtl[n], self.vecs_tl], writes=[Xtl[n]])

        self.linear("pw2", DC, DC, 4, ins, Htl, TT, ep_pw2)

    def final_norm(self, TT):
        X, Xtl = self.X, self.X_tl
        G, Gtl = self.G, self.G_tl
        self.rmsnorm(TT, self.VJ["norm_final"],
                     lambda c: X[:, c, :TT], lambda c: Xtl[c],
                     lambda c: G[c][:, 0:512], lambda c: Gtl[c])


    def dsa_proj(self, src, l=1):
        P = self.P
        X, Xtl, H, Htl = self.X, self.X_tl, self.H, self.H_tl
        G, Gtl = self.G, self.G_tl
        D_ = self.dsa
        tabs = [self.alloc(512 * 4) for _ in range(4)]
        tabs_tl = [Tl(f"tab{i}") for i in range(4)]
        wvv = self.arena[:, self.G_off:self.G_off + 16 * 1088].bitcast(BF16).rearrange(
            "p (k n) -> p k n", k=16)[:, :, 1024:1536]
        wiw = self.alloc(256 * 2, BF16)
        wiw_tl = Tl("wiw")
        dst, _, _ = self.wscr["wv"]
        self.dma(wvv, dst[:, :].rearrange("p (k n) -> p k n", k=16),
                 reads=[self.dT(("wv", "c", 0)), self.dT(("wv", "c", 1))], writes=Gtl)
        dst, _, _ = self.wscr["wiw"]
        self.dma(wiw, dst[:, :], reads=[self.dT(("wiw", "c", 0))], writes=[wiw_tl])
        wiwv = wiw.rearrange("p (k n) -> p k n", k=16)
        stg = [self.alloc(512 * 2, BF16) for _ in range(4)]
        stg_tl = [Tl(f"stg{i}") for i in range(4)]
        iwst = self.alloc(16 * 4)
        iwst_tl = Tl("iwst")
        sk = [0]
        own = self.own
        QOFF = own - 512
        for ti, (t0, TT) in enumerate(self.tiles0):
            self.load_x(src, t0, TT)
            full = (t0 >= QOFF)
            EXd = D_ if t0 >= own else D_["lowv"]
            kcol = t0 - own if t0 >= own else t0
            for i, nm in enumerate(("cos128", "sin128", "cos64", "sin64")):
                self.dma(tabs[i][:, :TT], self.dram[nm][:, t0:t0 + TT], writes=[tabs_tl[i]], eng="pool")
            self.rmsnorm(TT, self.VJ["norm_mix"] + l,
                         lambda c: H[:, c, HALO:HALO + TT], lambda c: Htl[c],
                         lambda c: G[c][:, 0:512], lambda c: Gtl[c])
            ins = [H[:, c, HALO:HALO + TT] for c in range(DC)]
            state = {}

            def ep(n, ps, pst, ti=ti, t0=t0, TT=TT, EXd=EXd, kcol=kcol):
                p = n // 2
                if n % 2 == 0:
                    state["a"] = (ps, pst)
                    return
                aps, apst = state["a"]
                big = p < 20
                ct, st_ = (tabs[0], tabs[1]) if big else (tabs[2], tabs[3])
                ctl, stl = (tabs_tl[0], tabs_tl[1]) if big else (tabs_tl[2], tabs_tl[3])
                P.op("dve", lambda e: e.tensor_tensor(out=self.tmpA[:, :TT], in0=aps, in1=ct[:, :TT], op=ALU.mult),
                     reads=[apst, ctl], writes=[self.tmpA_tl])
                P.op("dve", lambda e: e.tensor_tensor(out=self.tmpB[:, :TT], in0=ps, in1=st_[:, :TT], op=ALU.mult),
                     reads=[pst, stl], writes=[self.tmpB_tl])
                i = sk[0] % 4
                sk[0] += 1
                P.op("pool", lambda e: e.tensor_tensor(out=stg[i][:, :TT], in0=self.tmpA[:, :TT],
                                                       in1=self.tmpB[:, :TT], op=ALU.add),
                     reads=[self.tmpA_tl, self.tmpB_tl], writes=[stg_tl[i]])
                if p < 16:
                    self.dma(D_["qT"][:, p, t0 - QOFF:t0 - QOFF + TT], stg[i][:, :TT], reads=[stg_tl[i]],
                             writes=[self.dT(("qT", p, t0))], eng="pool")
                elif p < 20:
                    self.dma(EXd["kT"][:, p - 16, kcol:kcol + TT], stg[i][:, :TT], reads=[stg_tl[i]],
                             writes=[self.dT(("kT", p - 16, t0))], eng="pool")
                elif p < 28:
                    self.dma(D_["iqT"][:, p - 20, t0 - QOFF:t0 - QOFF + TT], stg[i][:, :TT], reads=[stg_tl[i]],
                             writes=[self.dT(("iqT", p - 20, t0))], eng="pool")
                else:
                    self.dma(EXd["ikT"][:, kcol:kcol + TT], stg[i][:, :TT], reads=[stg_tl[i]],
                             writes=[self.dT(("ikT", t0))], eng="pool")

            if full:
                self.linear("win", DC, 58, 2, ins, Htl, TT, ep)
            else:
                self.linear("win", DC, 8, 2, ins, Htl, TT, lambda n, ps, pst, ep=ep: ep(n + 32, ps, pst), blk0=16)
                self.linear("win", DC, 2, 2, ins, Htl, TT, lambda n, ps, pst, ep=ep: ep(n + 56, ps, pst), blk0=28)
            for tb in range(TT // 128):
                blk = (kcol + tb * 128) // 128
                hb = lambda kc, tb=tb: H[:, kc, HALO + tb * 128:HALO + tb * 128 + 128]
                if True:
                    ps, pst = self.bank()
                    for kc in range(DC):
                        P.op("pe", lambda e, ps=ps, kc=kc, hb=hb: e.matmul(ps[:, :512], lhsT=hb(kc), rhs=wvv[:, kc, :],
                                                                          start=(kc == 0), stop=(kc == DC - 1)),
                             reads=[Htl[kc], Gtl[kc]], writes=[pst])
                    i = sk[0] % 4
                    sk[0] += 1
                    P.op("act", lambda e, ps=ps, i=i: e.activation(out=stg[i][:, :512], in_=ps[:, :512], func=AF.Copy),
                         reads=[pst], writes=[stg_tl[i]])
                    self.dma(EXd["v"][:, blk, :], stg[i][:, :512], reads=[stg_tl[i]],
                             writes=[self.dT(("v", t0, tb))], eng="pool")
                if not full:
                    continue
                qblk = (t0 - QOFF) // 128 + tb
                ps, pst = self.bank()
                for kc in range(DC):
                    P.op("pe", lambda e, ps=ps, kc=kc, hb=hb: e.matmul(ps[:, :16], lhsT=hb(kc), rhs=wiwv[:, kc, :],
                                                                      start=(kc == 0), stop=(kc == DC - 1)),
                         reads=[Htl[kc], wiw_tl], writes=[pst])
                P.op("act", lambda e, ps=ps: e.activation(out=iwst[:, :16], in_=ps[:, :16], func=AF.Copy,
                                                          scale=1.0 / 32.0),
                     reads=[pst], writes=[iwst_tl])
                self.dma(D_["iw"][:, qblk, :], iwst[:, :16], reads=[iwst_tl], writes=[self.dT(("iw", qblk))], eng="pool")

    def exchange(self):
        own = self.nt - PRE
        D_ = self.dsa
        EX = D_["EX"]
        gath = self.dscr("gath", [256, 9 * own], BF16)
        D_["low"] = gath[0:128, :]
        rd = [self.dT(("kT", g, t0)) for g in range(4) for (t0, TT) in self.tiles[1:]]
        rd += [self.dT(("v", b)) for b in range(1, own // 128 + 1)]
        rd += [self.dT(("ikT", t0)) for (t0, TT) in self.tiles[1:]]
        groups = [[2 * i, 2 * i + 1] for i in range(self.ncores // 2)]
        self.P.op("pool", lambda e: e.collective_compute("AllGather", op=ALU.bypass, replica_groups=groups,
                                                         ins=[EX[:, :]], outs=[gath[:, :]]),
                  reads=rd, writes=[self.dT(("low",))], dma=True)

    def dsa_attn(self, nsteps=22):
        P = self.P
        D_ = self.dsa
        nqb = self.nt // 128
        own = self.nt - PRE
        nob = own // 128
        LOWK = D_["low"][:, 0:4 * own].rearrange("p (g t) -> p g t", g=4)
        LOWV = D_["low"][:, 4 * own:8 * own].rearrange("p (b c) -> p b c", b=nob)
        OWNK = D_["kT"]
        OWNV = D_["v"]
        self.arena_reset(self.const_end)
        ikT = self.alloc(8192 * 2, BF16)
        ikT_tl = Tl("ikT")
        self.dma(ikT[:, 0:own], D_["low"][:, 8 * own:9 * own], writes=[ikT_tl])
        self.dma(ikT[:, own:2 * own], D_["ikT"][:, :], writes=[ikT_tl])
        kval = self.alloc(8192 * 2, BF16)
        kval_tl = Tl("kval")
        self.dma(kval[:, :2 * own], self.dram["kvalid"][:, :], writes=[kval_tl])
        tri = self.alloc(128 * 2, BF16)
        tri_tl = Tl("tri")
        self.dma(tri, self.dram["tri"][:, :], writes=[tri_tl])
        score = self.alloc(8192 * 4)
        score_tl = Tl("score")
        M = self.alloc(8192 * 2, BF16)
        M_tl = Tl("M")
        MT2 = [self.alloc(8192 * 2, BF16).rearrange("p (b q) -> p b q", b=64) for _ in range(2)]
        MT2_tl = [Tl("MT0"), Tl("MT1")]
        QT2 = [self.alloc(16 * 128 * 2, BF16).rearrange("p (h q) -> p h q", h=16) for _ in range(2)]
        QT2_tl = [Tl("QT0"), Tl("QT1")]
        NR = 6
        Rr = [self.alloc(512 * 2, BF16) for _ in range(NR)]
        Rr_tl = [Tl(f"R{i}") for i in range(NR)]
        NE = 6
        ETr = [self.alloc(512 * 2, BF16) for _ in range(NE)]
        ETr_tl = [Tl(f"ET{i}") for i in range(NE)]
        PTr = [self.alloc(512 * 2, BF16) for _ in range(NE)]
        PTr_tl = [Tl(f"PT{i}") for i in range(NE)]
        Kr = [self.alloc(512 * 2, BF16) for _ in range(4)]
        Kr_tl = [Tl(f"K{i}") for i in range(4)]
        Vr = [self.alloc(512 * 2, BF16).rearrange("p (b d) -> p b d", b=4) for _ in range(4)]
        Vr_tl = [Tl(f"V{i}") for i in range(4)]
        IQ = self.alloc(8 * 128 * 2, BF16).rearrange("p (h q) -> p h q", h=8)
        IQ_tl = Tl("IQ")
        IW = self.alloc(16 * 4)
        IW_tl = Tl("IW")
        diag = self.alloc(16 * 128 * 2, BF16).rearrange("p (h q) -> p h q", h=16)
        diag_tl = [Tl(f"diag{h}") for h in range(16)]
        osb = [self.alloc(512 * 4) for _ in range(2)]
        osb_tl = [Tl("osb0"), Tl("osb1")]
        lnd = [self.alloc(512 * 4) for _ in range(2)]
        lnd_tl = [Tl("lnd0"), Tl("lnd1")]
        OTs = [self.alloc(512 * 2, BF16) for _ in range(2)]
        OTs_tl = [Tl("OT0"), Tl("OT1")]
        sm = self.alloc(64 * 4)
        lo, hi, mid, cnt, u, rng = (sm[:, i:i + 1] for i in range(6))
        hsx = sm[:, 8:8 + nsteps + 2]
        sm_tl = {k: Tl("sm_" + k) for k in ("lo", "hi", "mid", "cnt", "u", "rng", "hsx")}
        pw = self.smalls[:, 132:132 + nsteps + 2]
        B_IDX = (0, 1, 2, 4, 5)
        B_SC = 3
        B_ST = (0, 1, 2, 4, 5)
        B_O = 6
        B_DEN = 7
        rk = [0, 0, 0, 0, 0, 0]

        def s1a(qb):
            c0 = qb * 128
            nkb = nob + qb
            nk = nkb * 128
            qc = 512 - PRE + c0
            QT, QT_tl = QT2[qb % 2], QT2_tl[qb % 2]
            self.dma(QT[:, :, :], D_["qT"][:, :, qc:qc + 128], writes=[QT_tl])
            self.dma(IQ[:, :, :], D_["iqT"][:, :, qc:qc + 128], writes=[IQ_tl])
            self.dma(IW[:, :], D_["iw"][:, qc // 128, :], writes=[IW_tl])
            for h in range(16):
                P.op("pool", lambda e, h=h: e.tensor_scalar(out=diag[:, h, :], in0=self.ident, scalar1=IW[:, h:h + 1],
                                                            scalar2=None, op0=ALU.mult),
                     reads=[IW_tl, self.cbf_tl], writes=[diag_tl[h]])
            items = [(k0, h) for k0 in range(0, nk, 512) for h in range(16)]
            st = {}

            def stage_a(t):
                k0, h = items[t]
                w = min(512, nk - k0)
                ps, pst = self.bank(B_IDX[rk[5] % 5])
                rk[5] += 1
                hp = 64 * (h % 2)
                P.op("pe", lambda e: e.matmul(
                    ps[:, :w], lhsT=IQ[hp:hp + 64, h // 2, :], rhs=ikT[hp:hp + 64, k0:k0 + w], start=True, stop=True),
                    reads=[IQ_tl, ikT_tl], writes=[pst])
                ri = rk[0] % NR
                rk[0] += 1
                if h % 2 == 0:
                    P.op("act", lambda e: e.activation(out=Rr[ri][:, :w], in_=ps[:, :w], func=AF.Relu),
                         reads=[pst], writes=[Rr_tl[ri]])
                else:
                    P.op("dve", lambda e: e.tensor_scalar(out=Rr[ri][:, :w], in0=ps[:, :w], scalar1=0.0,
                                                          scalar2=None, op0=ALU.max),
                         reads=[pst], writes=[Rr_tl[ri]])
                st[t] = ri

            def stage_b(t):
                k0, h = items[t]
                w = min(512, nk - k0)
                ri = st.pop(t)
                scp, scpt = self.bank(B_SC)
                P.op("pe", lambda e: e.matmul(
                    scp[:, :w], lhsT=diag[:, h, :], rhs=Rr[ri][:, :w], start=(h == 0), stop=(h == 15)),
                    reads=[diag_tl[h], Rr_tl[ri]], writes=[scpt])
                if h == 15:
                    P.op("act", lambda e: e.activation(out=score[:, k0:k0 + w], in_=scp[:, :w], func=AF.Copy),
                         reads=[scpt], writes=[score_tl])

            LA = 4
            for t in range(len(items) + LA):
                if t < len(items):
                    stage_a(t)
                if t - LA >= 0:
                    stage_b(t - LA)

        def s1b(qb):
            nkb = nob + qb
            nk = nkb * 128
            P.op("dve", lambda e: e.tensor_reduce(out=lo, in_=score[:, :nk], axis=mybir.AxisListType.X, op=ALU.min),
                 reads=[score_tl], writes=[sm_tl["lo"]])
            P.op("dve", lambda e: e.tensor_reduce(out=hi, in_=score[:, :nk], axis=mybir.AxisListType.X, op=ALU.max),
                 reads=[score_tl], writes=[sm_tl["hi"]])
            P.op("dve", lambda e: e.tensor_tensor(out=score[:, :nk], in0=score[:, :nk], in1=kval[:, :nk], op=ALU.add),
                 reads=[score_tl, kval_tl], writes=[score_tl])
            P.op("dve", lambda e: e.tensor_tensor(out=score[:, nk - 128:nk], in0=score[:, nk - 128:nk], in1=tri[:, :],
                                                  op=ALU.add),
                 reads=[score_tl, tri_tl], writes=[score_tl])
            P.op("dve", lambda e: e.tensor_tensor(out=rng, in0=hi, in1=lo, op=ALU.subtract),
                 reads=[sm_tl["lo"], sm_tl["hi"]], writes=[sm_tl["rng"]])
            P.op("dve", lambda e: e.tensor_scalar(out=hsx, in0=pw, scalar1=rng, scalar2=None, op0=ALU.mult),
                 reads=[sm_tl["rng"], self.smalls_tl], writes=[sm_tl["hsx"]])
            P.op("dve", lambda e: e.tensor_tensor(out=mid, in0=lo, in1=hsx[:, 1:2], op=ALU.add),
                 reads=[sm_tl["lo"], sm_tl["hsx"]], writes=[sm_tl["mid"]])
            for s in range(nsteps):
                P.op("dve", lambda e: e.tensor_scalar(
                    out=M[:, :nk], in0=score[:, :nk], scalar1=mid, scalar2=0.0, op0=ALU.is_ge, op1=ALU.add,
                    accum_out=cnt),
                    reads=[score_tl, sm_tl["mid"]], writes=[M_tl, sm_tl["cnt"]])
                last = (s == nsteps - 1)
                ha = hsx[:, s + 1:s + 2] if not last else hsx[:, s + 1:s + 2]
                hb = hsx[:, s + 2:s + 3] if not last else hsx[:, s + 1:s + 2]
                P.op("dve", lambda e, ha=ha: e.scalar_tensor_tensor(out=u, in0=cnt, scalar=255.5, in1=ha,
                                                                    op0=ALU.is_ge, op1=ALU.mult),
                     reads=[sm_tl["cnt"], sm_tl["hsx"]], writes=[sm_tl["u"]])
                P.op("dve", lambda e, hb=hb: e.scalar_tensor_tensor(out=mid, in0=mid, scalar=hb, in1=u,
                                                                    op0=ALU.subtract, op1=ALU.add),
                     reads=[sm_tl["mid"], sm_tl["u"], sm_tl["hsx"]], writes=[sm_tl["mid"]])
            P.op("dve", lambda e: e.tensor_scalar(out=M[:, :nk], in0=score[:, :nk], scalar1=mid, scalar2=None, op0=ALU.is_ge),
                 reads=[score_tl, sm_tl["mid"]], writes=[M_tl])

        def s1c(qb):
            nkb = nob + qb
            MT, MT_tl = MT2[qb % 2], MT2_tl[qb % 2]
            for kb0 in range(0, nkb, 4):
                nb = min(4, nkb - kb0)
                trp, trpt = self.bank(B_SC)
                trb = trp.bitcast(BF16)
                for j in range(nb):
                    kb = kb0 + j
                    P.op("pe", lambda e, trb=trb, j=j, kb=kb: e.transpose(
                        out=trb[:, j * 128:(j + 1) * 128], in_=M[:, kb * 128:(kb + 1) * 128], identity=self.ident),
                        reads=[M_tl, self.cbf_tl], writes=[trpt])
                P.op("act", lambda e, trb=trb, kb0=kb0, nb=nb: e.activation(
                    out=MT[:, kb0:kb0 + nb, :], in_=trb[:, :nb * 128].rearrange("p (b q) -> p b q", b=nb),
                    func=AF.Identity, bias=self.negc, scale=30000.0),
                    reads=[trpt, self.smalls_tl], writes=[MT_tl])

        def s2(qb):
            c0 = qb * 128
            nkb = nob + qb
            nk = nkb * 128
            QT, QT_tl = QT2[qb % 2], QT2_tl[qb % 2]
            MT, MT_tl = MT2[qb % 2], MT2_tl[qb % 2]
            ops_, opst = self.bank(B_O)
            dps, dpst = self.bank(B_DEN)
            items = [(g, kb) for g in range(4) for kb in range(nkb)]
            st = {}
            kv = {}

            def stage_a(t):
                g, kb = items[t]
                if kb % 4 == 0:
                    k0 = kb * 128
                    w = min(512, nk - k0)
                    nb = w // 128
                    ki = rk[1] % 4
                    rk[1] += 1
                    if k0 < own:
                        ksrc = LOWK[:, g, k0:k0 + w]
                        vsrc = LOWV[:, k0 // 128:k0 // 128 + nb, g * 128:(g + 1) * 128]
                    else:
                        ksrc = OWNK[:, g, k0 - own:k0 - own + w]
                        vsrc = OWNV[:, (k0 - own) // 128:(k0 - own) // 128 + nb, g * 128:(g + 1) * 128]
                    self.dma(Kr[ki][:, :w], ksrc, writes=[Kr_tl[ki]])
                    self.dma(Vr[ki][:, :nb, :], vsrc, writes=[Vr_tl[ki]])
                    kv[(g, kb // 4)] = ki
                ki = kv[(g, kb // 4)]
                j = kb % 4
                stp, stpt = self.bank(B_ST[rk[2] % 5])
                rk[2] += 1
                P.op("pe", lambda e: e.matmul(
                    stp[:, :], lhsT=Kr[ki][:, j * 128:(j + 1) * 128],
                    rhs=QT[:, 4 * g:4 * g + 4, :], start=True, stop=False),
                    reads=[Kr_tl[ki], QT_tl], writes=[stpt])
                P.op("pe", lambda e: e.matmul(
                    stp[:, :].rearrange("p (r q) -> p r q", r=4), lhsT=self.ident,
                    rhs=MT[:, kb, :].unsqueeze(1).to_broadcast([128, 4, 128]), start=False, stop=True),
                    reads=[self.cbf_tl, MT_tl], writes=[stpt])
                ei = rk[3] % NE
                rk[3] += 1
                P.op("act", lambda e: e.activation(out=ETr[ei][:, :], in_=stp[:, :], func=AF.Exp,
                                                   scale=float(128 ** -0.5)),
                     reads=[stpt], writes=[ETr_tl[ei]])
                st[t] = (ei, ki, j)

            def stage_b(t):
                g, kb = items[t]
                ei, ki, j = st.pop(t)
                P.op("pe", lambda e: e.matmul(
                    ops_[:, :], lhsT=Vr[ki][:, j, :], rhs=ETr[ei][:, :], start=(kb == 0), stop=(kb == nkb - 1)),
                    reads=[Vr_tl[ki], ETr_tl[ei]], writes=[opst])
                P.op("pe", lambda e: e.matmul(
                    dps[:, :], lhsT=self.ones, rhs=ETr[ei][:, :], start=(kb == 0), stop=(kb == nkb - 1)),
                    reads=[self.cbf_tl, ETr_tl[ei]], writes=[dpst])
                if kb == nkb - 1:
                    oi = rk[4] % 2
                    rk[4] += 1
                    P.op("act", lambda e: e.activation(out=lnd[oi][:, :], in_=dps[:, :], func=AF.Ln,
                                                       bias=self.tinyc, scale=1.0),
                         reads=[dpst, self.smalls_tl], writes=[lnd_tl[oi]])
                    P.op("act", lambda e: e.activation(out=osb[oi][:, :], in_=ops_[:, :], func=AF.Copy),
                         reads=[opst], writes=[osb_tl[oi]])
                    P.op("act", lambda e: e.activation(out=lnd[oi][:, :], in_=lnd[oi][:, :], func=AF.Exp, scale=-1.0),
                         reads=[lnd_tl[oi]], writes=[lnd_tl[oi]])
                    P.op("pool", lambda e: e.tensor_tensor(out=OTs[oi][:, :], in0=osb[oi][:, :], in1=lnd[oi][:, :],
                                                           op=ALU.mult),
                         reads=[osb_tl[oi], lnd_tl[oi]], writes=[OTs_tl[oi]])
                    self.dma(D_["oT"][:, 4 * g:4 * g + 4, c0:c0 + 128], OTs[oi][:, :].rearrange("p (r q) -> p r q", r=4),
                             reads=[OTs_tl[oi]], writes=[self.dT(("oT", g, qb))], eng="pool")

            LA = 4
            for t in range(len(items) + LA):
                if t < len(items):
                    stage_a(t)
                if t - LA >= 0:
                    stage_b(t - LA)

        s1a(0)
        s1b(0)
        s1c(0)
        for qb in range(nqb):
            if qb + 1 < nqb:
                s1a(qb + 1)
                s1b(qb + 1)
            s2(qb)
            if qb + 1 < nqb:
                s1c(qb + 1)

    def dsa_out(self, t0, TT):
        P = self.P
        X, Xtl, H, Htl = self.X, self.X_tl, self.H, self.H_tl
        rd = [self.dT(("oT", g, qb)) for g in range(4) for qb in range(t0 // 128, (t0 + TT) // 128)]
        self.dma(H[:, :, HALO:HALO + TT], self.dsa["oT"][:, :, t0:t0 + TT], reads=rd, writes=Htl, eng="pool")
        ins = [H[:, c, HALO:HALO + TT] for c in range(DC)]

        def ep(n, ps, pst):
            P.op("dve", lambda e: e.tensor_tensor(out=X[:, n, :TT], in0=X[:, n, :TT], in1=ps, op=ALU.add),
                 reads=[pst, Xtl[n]], writes=[Xtl[n]])

        self.linear("wout", DC, DC, 4, ins, Htl, TT, ep)


def vec_layout():
    VJ = {}
    j = 0
    for name, n in (("norm_mix", 4), ("norm_mlp", 4), ("pool_scale", 2), ("b_pw1a", 1), ("b_pw1g", 1),
                    ("b_dw", 1), ("ln_g", 1), ("ln_b", 1), ("b_pw2", 1), ("norm_final", 1)):
        VJ[name] = j
        j += n
    return VJ, j


def fm(v):
    return np.ascontiguousarray(np.asarray(v, np.float32).reshape(16, 128).T)


def weight_specs(layers):
    specs = []
    for l in layers:
        k, j = l % 3, l // 3
        if k == 0:
            specs.append((f"pool{j}", 1, 8192))
        elif k == 2:
            specs.append(("pw1", 16, 4096))
            specs.append(("pw2", 4, 8192))
        specs.append((f"up{l}", 16, 8192))
        specs.append((f"dn{l}", 16, 8192))
    return specs


def dsa_tensors(B, mode):
    nt = B.nt
    own = nt - PRE
    nq = own + 512

    def exv(EX):
        return {"kT": EX[:, 0:4 * own].rearrange("p (g t) -> p g t", g=4),
                "v": EX[:, 4 * own:8 * own].rearrange("p (b c) -> p b c", b=own // 128),
                "ikT": EX[:, 8 * own:9 * own]}
    EX = B.dscr("EX", [128, 9 * own], BF16)
    EXL = B.dscr("EXL", [128, 9 * own], BF16)
    D_ = {"EX": EX, "low": EXL, "lowv": exv(EXL)}
    D_.update(exv(EX))
    D_["qT"] = B.dscr("qT", [128, 16, nq], BF16)
    D_["iqT"] = B.dscr("iqT", [128, 8, nq], BF16)
    D_["iw"] = B.dscr("iw", [128, nq // 128, 16], F32)
    D_["oT"] = B.dscr("oT", [128, 16, nt], BF16)
    B.din("kvalid", [128, 2 * own], BF16)
    B.din("tri", [128, 128], BF16)
    for nm in ("cos128", "sin128", "cos64", "sin64"):
        B.din(nm, [128, 2 * own])
    return D_


def mode_layers(mode):
    return {"A": (0, 1), "B": (1, 2, 3), "fused": (0, 1, 2, 3)}[mode]


def mode_specs(mode):
    specs = []
    if mode in ("A", "fused"):
        specs += [("pool0", 1, 8192), ("up0", 16, 8192), ("dn0", 16, 8192),
                  ("win", 29, 4096), ("wv", 1, 8192), ("wiw", 1, 256)]
    if mode in ("B", "fused"):
        specs += [("wout", 4, 8192), ("up1", 16, 8192), ("dn1", 16, 8192),
                  ("pw1", 16, 4096), ("pw2", 4, 8192), ("up2", 16, 8192), ("dn2", 16, 8192),
                  ("pool1", 1, 8192), ("up3", 16, 8192), ("dn3", 16, 8192)]
    return specs


def build_program(mode="fused", nt=NT, upto=3, final=True, ncores=NCORES):
    B = Builder(nt=nt, mode=mode, ncores=ncores)
    own = B.own
    B.arena_init()
    B.VJ, nv = vec_layout()
    B.NVEC = nv * 16
    B.wscr = {}
    B.load_consts()
    B.dsa = dsa_tensors(B, mode)
    xin = B.din("xin", [128, DC, 2 * own])
    x1 = B.dscr("x1", [128, DC, 2 * own], F32)
    xs = [B.dscr("xsA", [128, DC, nt], F32), B.dscr("xsB", [128, DC, nt], F32)]
    out = B.dout("out", [128, DC, own])
    B.cast_weights(mode_specs("fused"))
    B.barrier()
    B.tile_bufs()
    for ti, (t0, TT) in enumerate(B.tiles0):
        B.load_x(xin, t0, TT)
        icol = 66 if ti == 0 else (1 if t0 == own else None)
        B.pool_mixer(0, 0, ti, TT, icol=icol)
        B.mlp(0, TT)
        B.store_x(x1, t0, TT)
    B.dsa_proj(x1)
    B.barrier()
    B.dsa_attn()
    B.barrier()
    B.tile_bufs()

    def layer_loop(l, src, dst, last, off=0):
        kind = l % 3
        for ti, (t0, TT) in enumerate(B.tiles):
            B.load_x(src, t0, TT, off=off)
            if ti == 0:
                B.mask_pre(B.X[:, :, :TT], B.X_tl)
            if kind == 0:
                B.pool_mixer(l, l // 3, ti, TT, icol=(1 if ti == 1 else None))
            elif kind == 2:
                B.conv_mixer(l, ti, TT)
            else:
                B.dsa_out(t0, TT)
            B.mlp(l, TT)
            if last:
                if ti == 0:
                    continue
                if final:
                    B.final_norm(TT)
                B.store_x(dst, t0, TT, c0=PRE)
            else:
                B.store_x(dst, t0, TT)

    seq = [(1, x1, xs[0], own - PRE), (2, xs[0], xs[1], 0), (3, xs[1], out, 0)]
    seq = [s for s in seq if s[0] <= upto]
    for i, (l, s, d, off) in enumerate(seq):
        lastl = (i == len(seq) - 1)
        layer_loop(l, s, out if lastl else d, lastl, off=off)
    B.P.emit()
    return B


def prep_common(inp, layers):
    VJ, nv = vec_layout()
    vecs = np.zeros((128, nv * 16), np.float32)

    def put(name, idx, v):
        j = VJ[name] + idx
        vecs[:, j * 16:(j + 1) * 16] = fm(v)

    for i in range(4):
        put("norm_mix", i, inp["norm_mix"][i])
        put("norm_mlp", i, inp["norm_mlp"][i])
    for i in range(2):
        put("pool_scale", i, inp["pool_scale"][i])
    put("b_pw1a", 0, inp["conv_b_pw1"][0][:D])
    put("b_pw1g", 0, inp["conv_b_pw1"][0][D:])
    put("b_dw", 0, inp["conv_b_dw"][0])
    put("ln_g", 0, inp["conv_ln_g"][0])
    put("ln_b", 0, inp["conv_ln_b"][0])
    put("b_pw2", 0, inp["conv_b_pw2"][0])
    put("norm_final", 0, inp["norm_final"])
    cbf = np.zeros((128, 384), np.float32)
    cbf[:, 0:128] = np.eye(128, dtype=np.float32)
    cbf[:, 128:256] = 1.0 / 2048.0
    cbf[:, 256:384] = 1.0
    cbf = cbf.astype(ml_dtypes.bfloat16)
    wdw = np.asarray(inp["conv_w_dw"][0], np.float32)
    convw = np.ascontiguousarray(wdw.reshape(CONVW, 16, 128).transpose(2, 1, 0)).reshape(128, 16 * CONVW)
    com = {"vecs": vecs, "cbf": cbf, "convw": convw}
    for l in layers:
        k, j = l % 3, l // 3
        if k == 0:
            w = np.asarray(inp["pool_w"][j], np.float32)
            com[f"pool{j}_f"] = np.ascontiguousarray(
                w.reshape(4, 4, 128, 512).transpose(2, 0, 1, 3)).reshape(128, 8192)
        elif k == 2:
            w = np.asarray(inp["conv_w_pw1"][0], np.float32)
            w4 = w.reshape(16, 128, 2, 16, 128)
            com["pw1_f"] = np.ascontiguousarray(w4.transpose(1, 3, 0, 2, 4)).reshape(128, 16 * 4096)
            w = np.asarray(inp["conv_w_pw2"][0], np.float32)
            com["pw2_f"] = np.ascontiguousarray(
                w.reshape(16, 128, 4, 512).transpose(1, 2, 0, 3)).reshape(128, 4 * 8192)
        w = np.asarray(inp["mlp_up"][l], np.float32)
        com[f"up{l}_f"] = np.ascontiguousarray(
            w.reshape(16, 128, 16, 512).transpose(1, 2, 0, 3)).reshape(128, 16 * 8192)
        w = np.asarray(inp["mlp_down"][l], np.float32)
        com[f"dn{l}_f"] = np.ascontiguousarray(
            w.reshape(64, 128, 16, 128).transpose(1, 2, 0, 3)).reshape(128, 16 * 8192)
    return com


def core_tokens(x, k, nt=NT):
    b, half = k // 2, k % 2
    own = np.asarray(x[b, half * OWN: half * OWN + (nt - PRE)], np.float32)
    if half == 1:
        pre = np.asarray(x[b, half * OWN - PRE: half * OWN], np.float32)
    else:
        pre = np.zeros((PRE, D), np.float32)
    return np.concatenate([pre, own], axis=0)


def to_fm(a):
    n = a.shape[0]
    return np.ascontiguousarray(a.T.reshape(16, 128, n).transpose(1, 0, 2))


def from_fm(a):
    n = a.shape[2]
    return np.ascontiguousarray(a.transpose(1, 0, 2).reshape(2048, n).T)


def smalls_for(k):
    half = k % 2
    s = np.zeros((128, 192), np.float32)
    s[:, 0] = float(half)
    s[:, 65] = EPS
    s[:, 130] = 1e-18
    s[:, 131] = -30000.0
    for j in range(40):
        s[:, 132 + j] = 2.0 ** (-j)
    for g, w in enumerate(POOL_W):
        for t in range(16):
            start = 1.0 / min(t + 1, w)
            s[:, 1 + 16 * g + t] = start if half == 0 else 1.0 / w
            s[:, 66 + 16 * g + t] = start
    return s


def rope_tables(k, nt=NT):
    half = k % 2
    own = nt - PRE
    pos = (np.arange(2 * own) - (1 - half) * own).astype(np.float32)
    out = {}
    for nm, d in (("128", 128), ("64", 64)):
        inv = (np.float32(10000.0) ** (-np.arange(0, d, 2, dtype=np.float32) / np.float32(d))).astype(np.float32)
        ang = (pos[:, None] * inv[None, :]).astype(np.float32)
        cos = np.cos(ang).astype(np.float32)
        sin = np.sin(ang).astype(np.float32)
        dd = np.arange(128) % d
        idx = dd % (d // 2)
        sign = np.where(dd < d // 2, -1.0, 1.0).astype(np.float32)
        out["cos" + nm] = np.ascontiguousarray(cos[:, idx].T)
        out["sin" + nm] = np.ascontiguousarray((sin[:, idx] * sign[None, :]).T)
    return out


def dsa_weights(inp):
    w = np.asarray(inp["dsa_w_in"][0], np.float32)
    cols = []
    d = np.arange(128)
    for h in range(16):
        cols.append(h * 128 + d)
        cols.append(h * 128 + (d + 64) % 128)
    for g in range(4):
        cols.append(2048 + g * 128 + d)
        cols.append(2048 + g * 128 + (d + 64) % 128)
    for m in range(8):
        head = 2 * m + d // 64
        dd = d % 64
        cols.append(3072 + head * 64 + dd)
        cols.append(3072 + head * 64 + (dd + 32) % 64)
    dd = d % 64
    cols.append(4096 + dd)
    cols.append(4096 + (dd + 32) % 64)
    cols = np.concatenate(cols)
    wp = w[:, cols]
    win = np.ascontiguousarray(wp.reshape(16, 128, 29, 256).transpose(1, 2, 0, 3)).reshape(128, 29 * 4096)
    wv = np.ascontiguousarray(w[:, 2560:3072].reshape(16, 128, 512).transpose(1, 0, 2)).reshape(128, 8192)
    wiw = np.ascontiguousarray(w[:, 4160:4176].reshape(16, 128, 16).transpose(1, 0, 2)).reshape(128, 256)
    wo = np.asarray(inp["dsa_w_out"][0], np.float32)
    wout = np.ascontiguousarray(wo.reshape(16, 128, 4, 512).transpose(1, 2, 0, 3)).reshape(128, 4 * 8192)
    return {"win_f": win, "wv_f": wv, "wiw_f": wiw, "wout_f": wout}


def attn_masks(k, nt=NT):
    half = k % 2
    own = nt - PRE
    kv = np.zeros((128, 2 * own), np.float32)
    if half == 0:
        kv[:, :own] = NEG
    q = np.arange(128)[:, None]
    s = np.arange(128)[None, :]
    tri = np.where(s <= q, 0.0, NEG).astype(np.float32)
    return {"kvalid": kv.astype(ml_dtypes.bfloat16), "tri": tri.astype(ml_dtypes.bfloat16)}


def kernel(**inp):
    inp = {k: np.asarray(v) for k, v in inp.items()}
    com = prep_common(inp, (0, 1, 2, 3))
    com.update(dsa_weights(inp))
    x = inp["x"]
    maps = []
    for k in range(NCORES):
        b, half = k // 2, k % 2
        m = dict(com)
        seq = np.asarray(x[b], np.float32)
        if half == 0:
            seq = np.concatenate([np.zeros((OWN, D), np.float32), seq[:OWN]], axis=0)
        m["xin"] = to_fm(seq)
        m["smalls"] = smalls_for(k)
        m.update(rope_tables(k))
        m.update(attn_masks(k))
        maps.append(m)
    cores = list(range(NCORES))
    BF = build_program("fused")
    res = run_bass_kernel_spmd(BF.nc, [{n: m[n] for n in BF.in_names} for m in maps], core_ids=cores)
    out = np.empty((4, SEQ, D), np.float32)
    for k in cores:
        b, half = k // 2, k % 2
        out[b, half * OWN:(half + 1) * OWN] = from_fm(np.asarray(res.results[k]["out"]))
    return out
```
